# Optimizing a Trainium2 kernel written in Bass

```python
import jax, jax.numpy as jnp
from jax import lax
import numpy as np

D_MODEL = 1024
BATCH = 8
SEQ = 2048
DEPTH = 1

PLE_DIM = 256
HEAD_DIM = 64
RWKV_HEADS = 8
RWKV_DIM = RWKV_HEADS * HEAD_DIM
DECAY_LORA = 64
ICLR_LORA = 64
GATE_LORA = 128
GN_EPS = 64e-5
ATTN_GROUPS = ((128, 1), (512, 4), (2048, 16))
HEADS_PER_GROUP = 4
ATTN_HEADS = HEADS_PER_GROUP * len(ATTN_GROUPS)
ATTN_DIM = ATTN_HEADS * HEAD_DIM
ATTN_OUT_DIM = HEADS_PER_GROUP * HEAD_DIM
BAND_BLOCK = 128
ROPE_THETA = 10000.0
NEG_INF = -1e30
D_FF = 2816
RMS_EPS = 1e-6
N_BRANCHES = 2
RWKV_COLS = 3 * RWKV_DIM + DECAY_LORA + ICLR_LORA + GATE_LORA
ATTN_COLS = 3 * ATTN_DIM
GATE_COLS = N_BRANCHES * D_MODEL
IN_COLS = RWKV_COLS + ATTN_COLS + GATE_COLS

kernel_name = 'hybrid_rwkv7_dilated_attn_macaron_block'


def rms_norm(x, gain):
    xf = x.astype(jnp.float32)
    y = xf * lax.rsqrt(jnp.mean(xf * xf, axis=-1, keepdims=True) + RMS_EPS)
    return (y * gain.astype(jnp.float32)).astype(x.dtype)


def swiglu(h, w_gate, w_up, w_down):
    return (jax.nn.silu(h @ w_gate) * (h @ w_up)) @ w_down


def token_shift(z):
    return jnp.pad(z, ((0, 0), (1, 0), (0, 0)))[:, :-1]


def apply_rope(x, cos, sin):
    x1, x2 = jnp.split(x.astype(jnp.float32), 2, axis=-1)
    return jnp.concatenate([x1 * cos - x2 * sin, x2 * cos + x1 * sin], axis=-1).astype(x.dtype)


def wkv7_scan(r, decay, k, v, kk, a):
    B, S, H, N = r.shape

    def step(state, inp):
        r_t, w_t, k_t, v_t, kk_t, a_t = inp
        sa = jnp.einsum('bhvk,bhk->bhv', state, -kk_t)
        state = (state * w_t[:, :, None, :]
                 + sa[..., None] * (kk_t * a_t)[:, :, None, :]
                 + v_t[..., None] * k_t[:, :, None, :])
        y_t = jnp.einsum('bhvk,bhk->bhv', state, r_t)
        return state, y_t

    xs = tuple(jnp.moveaxis(t, 1, 0) for t in (r, decay, k, v, kk, a))
    s0 = jnp.zeros((B, H, N, N), jnp.float32)
    _, y = lax.scan(step, s0, xs)
    return jnp.moveaxis(y, 0, 1)


def rwkv7_time_mix(z, mu, w0, w2, a0, a2, g2, k_k, k_a, r_k, gn_w, gn_b):
    B, S, _ = z.shape
    z = z + (token_shift(z) - z) * mu
    r, k, v, wd, ad, gd = jnp.split(
        z, [RWKV_DIM, 2 * RWKV_DIM, 3 * RWKV_DIM, 3 * RWKV_DIM + DECAY_LORA,
            3 * RWKV_DIM + DECAY_LORA + ICLR_LORA], axis=-1)
    w = -jax.nn.softplus(-(w0 + jnp.tanh(wd) @ w2)) - 0.5
    a = jax.nn.sigmoid(a0 + ad @ a2)
    g = jax.nn.sigmoid(gd) @ g2
    kk = k * k_k
    k = k * (1.0 + (a - 1.0) * k_a)

    def heads(t):
        return t.reshape(B, S, RWKV_HEADS, HEAD_DIM).astype(jnp.float32)

    kk = heads(kk)
    kk = kk * lax.rsqrt(jnp.maximum(jnp.sum(kk * kk, axis=-1, keepdims=True), 1e-24))
    decay = jnp.exp(-jnp.exp(heads(w)))
    r_h, k_h, v_h, a_h = heads(r), heads(k), heads(v), heads(a)
    y = wkv7_scan(r_h, decay, k_h, v_h, kk, a_h)
    mean = jnp.mean(y, axis=-1, keepdims=True)
    var = jnp.mean(jnp.square(y - mean), axis=-1, keepdims=True)
    y = ((y - mean) * lax.rsqrt(var + GN_EPS)).reshape(B, S, RWKV_DIM)
    y = y * gn_w.astype(jnp.float32) + gn_b.astype(jnp.float32)
    bonus = jnp.sum(r_h * k_h * r_k.astype(jnp.float32), axis=-1, keepdims=True) * v_h
    y = y + bonus.reshape(B, S, RWKV_DIM)
    return (y * g.astype(jnp.float32)).astype(z.dtype)


def banded_causal_attention(q, k, v, band):
    N, L, H, Dh = q.shape
    nb = -(-L // BAND_BLOCK)
    Lp = nb * BAND_BLOCK
    pad = Lp - L
    qb = jnp.pad(q, ((0, 0), (0, pad), (0, 0), (0, 0))).reshape(N, nb, BAND_BLOCK, H, Dh)

    def key_blocks(t):
        tp = jnp.pad(t, ((0, 0), (BAND_BLOCK, pad), (0, 0), (0, 0))).reshape(N, nb + 1, BAND_BLOCK, H, Dh)
        return jnp.concatenate([tp[:, :-1], tp[:, 1:]], axis=2)

    kb, vb = key_blocks(k), key_blocks(v)
    s = jnp.einsum('nbqhd,nbkhd->nbhqk', qb.astype(jnp.float32), kb.astype(jnp.float32)) * (Dh ** -0.5)
    blk = jnp.arange(nb)[:, None]
    qpos = blk * BAND_BLOCK + jnp.arange(BAND_BLOCK)[None, :]
    kpos = blk * BAND_BLOCK - BAND_BLOCK + jnp.arange(2 * BAND_BLOCK)[None, :]
    dist = qpos[:, :, None] - kpos[:, None, :]
    valid = (dist >= 0) & (dist <= band) & (kpos[:, None, :] >= 0)
    s = jnp.where(valid[None, :, None], s, NEG_INF)
    m = jnp.max(s, axis=-1, keepdims=True)
    e = jnp.exp(s - m)
    l = jnp.sum(e, axis=-1, keepdims=True)
    o = jnp.einsum('nbhqk,nbkhd->nbqhd', e, vb.astype(jnp.float32)) / jnp.swapaxes(l, 2, 3)
    lse = jnp.swapaxes((m + jnp.log(l))[..., 0], 2, 3)
    return o.reshape(N, Lp, H, Dh)[:, :L], lse.reshape(N, Lp, H)[:, :L]


def dilated_causal_attention(q, k, v, window, dilation):
    B, S, H, Dh = q.shape
    L = S // dilation

    def fold(t):
        return t.reshape(B, L, dilation, H, Dh).transpose(0, 2, 1, 3, 4).reshape(B * dilation, L, H, Dh)

    o, lse = banded_causal_attention(fold(q), fold(k), fold(v), window // dilation)
    o = o.reshape(B, dilation, L, H, Dh).transpose(0, 2, 1, 3, 4).reshape(B, S, H, Dh)
    lse = lse.reshape(B, dilation, L, H).transpose(0, 2, 1, 3).reshape(B, S, H)
    return o, lse


def dilated_attention_mix(z, cos, sin, q_gain, k_gain):
    B, S, _ = z.shape
    q, k, v = jnp.split(z, 3, axis=-1)
    q = q.reshape(B, S, ATTN_HEADS, HEAD_DIM)
    k = k.reshape(B, S, ATTN_HEADS, HEAD_DIM)
    v = v.reshape(B, S, ATTN_HEADS, HEAD_DIM)
    q = apply_rope(rms_norm(q, q_gain), cos, sin)
    k = apply_rope(rms_norm(k, k_gain), cos, sin)
    outs, lses = [], []
    for gi, (window, dilation) in enumerate(ATTN_GROUPS):
        hs = slice(gi * HEADS_PER_GROUP, (gi + 1) * HEADS_PER_GROUP)
        o, lse = dilated_causal_attention(q[:, :, hs], k[:, :, hs], v[:, :, hs], window, dilation)
        outs.append(o)
        lses.append(lse)
    wts = jax.nn.softmax(jnp.stack(lses, axis=0), axis=0)
    o = jnp.sum(wts[..., None] * jnp.stack(outs, axis=0), axis=0)
    return o.reshape(B, S, ATTN_OUT_DIM).astype(z.dtype)


def hybrid_layer(x, p_i, cos, sin, ffn1_norm, ffn1_w_gate, ffn1_w_up, ffn1_w_down, mix_norm, w_in,
                 rwkv_mu, rwkv_w0, rwkv_w2, rwkv_a0, rwkv_a2, rwkv_g2, rwkv_k_k, rwkv_k_a, rwkv_r_k,
                 rwkv_gn_w, rwkv_gn_b, q_norm, k_norm, w_br_rwkv, w_br_attn, w_out,
                 ffn2_norm, ffn2_w_gate, ffn2_w_up, ffn2_w_down, ple_norm, ple_w_gate, ple_w_proj):
    x = x + 0.5 * swiglu(rms_norm(x, ffn1_norm), ffn1_w_gate, ffn1_w_up, ffn1_w_down)
    h = rms_norm(x, mix_norm)
    z = h @ w_in
    z_rwkv, z_attn, z_gate = jnp.split(z, [RWKV_COLS, RWKV_COLS + ATTN_COLS], axis=-1)
    y_rwkv = rwkv7_time_mix(z_rwkv, rwkv_mu, rwkv_w0, rwkv_w2, rwkv_a0, rwkv_a2, rwkv_g2,
                            rwkv_k_k, rwkv_k_a, rwkv_r_k, rwkv_gn_w, rwkv_gn_b)
    y_attn = dilated_attention_mix(z_attn, cos, sin, q_norm, k_norm)
    g_rwkv, g_attn = jnp.split(jax.nn.sigmoid(z_gate), N_BRANCHES, axis=-1)
    merged = g_rwkv * (y_rwkv @ w_br_rwkv) + g_attn * (y_attn @ w_br_attn)
    x = x + merged @ w_out
    x = x + 0.5 * swiglu(rms_norm(x, ffn2_norm), ffn2_w_gate, ffn2_w_up, ffn2_w_down)
    x = x + jax.nn.sigmoid(rms_norm(x, ple_norm) @ ple_w_gate) * (p_i @ ple_w_proj)
    return x


def setup_inputs(seed: int = 0) -> dict:
    key = jax.random.key(seed)
    ks = jax.random.split(key, 40)
    f32 = jnp.float32

    def nrm(i, shape, scale):
        return jax.random.normal(ks[i], shape, f32) * scale

    def gain(i, shape):
        return 1.0 + 0.02 * jax.random.normal(ks[i], shape, f32)

    L = DEPTH
    return {
        'x': nrm(0, (BATCH, SEQ, D_MODEL), 1.0),
        'p': nrm(1, (DEPTH, BATCH, SEQ, PLE_DIM), 1.0),
        'positions': (jnp.arange(SEQ, dtype=jnp.int32)[None, :]
                      + jax.random.randint(ks[2], (BATCH, 1), 0, 4096, dtype=jnp.int32)),
        'ffn1_norm': gain(3, (L, D_MODEL)),
        'ffn1_w_gate': nrm(4, (L, D_MODEL, D_FF), D_MODEL ** -0.5),
        'ffn1_w_up': nrm(5, (L, D_MODEL, D_FF), D_MODEL ** -0.5),
        'ffn1_w_down': nrm(6, (L, D_FF, D_MODEL), D_FF ** -0.5),
        'mix_norm': gain(7, (L, D_MODEL)),
        'w_in': nrm(8, (L, D_MODEL, IN_COLS), D_MODEL ** -0.5),
        'rwkv_mu': jax.random.uniform(ks[9], (L, RWKV_COLS), f32),
        'rwkv_w0': -0.5 - 4.5 * jax.random.uniform(ks[10], (L, RWKV_DIM), f32),
        'rwkv_w2': nrm(11, (L, DECAY_LORA, RWKV_DIM), 0.1),
        'rwkv_a0': nrm(12, (L, RWKV_DIM), 0.1),
        'rwkv_a2': nrm(13, (L, ICLR_LORA, RWKV_DIM), 0.1),
        'rwkv_g2': nrm(14, (L, GATE_LORA, RWKV_DIM), GATE_LORA ** -0.5),
        'rwkv_k_k': 0.85 + nrm(15, (L, RWKV_DIM), 0.02),
        'rwkv_k_a': 1.0 + nrm(16, (L, RWKV_DIM), 0.02),
        'rwkv_r_k': nrm(17, (L, RWKV_HEADS, HEAD_DIM), 0.1),
        'rwkv_gn_w': gain(18, (L, RWKV_DIM)),
        'rwkv_gn_b': nrm(19, (L, RWKV_DIM), 0.01),
        'q_norm': gain(20, (L, HEAD_DIM)),
        'k_norm': gain(21, (L, HEAD_DIM)),
        'w_br_rwkv': nrm(22, (L, RWKV_DIM, D_MODEL), RWKV_DIM ** -0.5),
        'w_br_attn': nrm(23, (L, ATTN_OUT_DIM, D_MODEL), ATTN_OUT_DIM ** -0.5),
        'w_out': nrm(24, (L, D_MODEL, D_MODEL), D_MODEL ** -0.5),
        'ffn2_norm': gain(25, (L, D_MODEL)),
        'ffn2_w_gate': nrm(26, (L, D_MODEL, D_FF), D_MODEL ** -0.5),
        'ffn2_w_up': nrm(27, (L, D_MODEL, D_FF), D_MODEL ** -0.5),
        'ffn2_w_down': nrm(28, (L, D_FF, D_MODEL), D_FF ** -0.5),
        'ple_norm': gain(29, (L, D_MODEL)),
        'ple_w_gate': nrm(30, (L, D_MODEL, D_MODEL), D_MODEL ** -0.5),
        'ple_w_proj': nrm(31, (L, PLE_DIM, D_MODEL), PLE_DIM ** -0.5),
    }


def reference(x, p, positions, ffn1_norm, ffn1_w_gate, ffn1_w_up, ffn1_w_down, mix_norm, w_in,
              rwkv_mu, rwkv_w0, rwkv_w2, rwkv_a0, rwkv_a2, rwkv_g2, rwkv_k_k, rwkv_k_a, rwkv_r_k,
              rwkv_gn_w, rwkv_gn_b, q_norm, k_norm, w_br_rwkv, w_br_attn, w_out,
              ffn2_norm, ffn2_w_gate, ffn2_w_up, ffn2_w_down, ple_norm, ple_w_gate, ple_w_proj):
    inv_freq = 1.0 / (ROPE_THETA ** (jnp.arange(0, HEAD_DIM, 2, dtype=jnp.float32) / HEAD_DIM))
    ang = positions.astype(jnp.float32)[..., None] * inv_freq
    cos = jnp.cos(ang)[:, :, None, :]
    sin = jnp.sin(ang)[:, :, None, :]
    for i in range(DEPTH):
        x = hybrid_layer(x, p[i], cos, sin, ffn1_norm[i], ffn1_w_gate[i], ffn1_w_up[i], ffn1_w_down[i],
                         mix_norm[i], w_in[i], rwkv_mu[i], rwkv_w0[i], rwkv_w2[i], rwkv_a0[i], rwkv_a2[i],
                         rwkv_g2[i], rwkv_k_k[i], rwkv_k_a[i], rwkv_r_k[i], rwkv_gn_w[i], rwkv_gn_b[i],
                         q_norm[i], k_norm[i], w_br_rwkv[i], w_br_attn[i], w_out[i],
                         ffn2_norm[i], ffn2_w_gate[i], ffn2_w_up[i], ffn2_w_down[i],
                         ple_norm[i], ple_w_gate[i], ple_w_proj[i])
    return x
```

```python
import numpy as np
import concourse.bass as bass
import concourse.mybir as mybir
from concourse.bass_utils import run_bass_kernel_spmd

F32 = mybir.dt.float32
BF16 = mybir.dt.bfloat16
I32 = mybir.dt.int32
AF = mybir.ActivationFunctionType
ALU = mybir.AluOpType
AX = mybir.AxisListType

S = 2048
D = 1024
NT = 16
DFF = 2816
PLE = 256
RMS_EPS = 1e-6


class _Op:
    __slots__ = ("eng", "fn", "dma", "deps", "ddeps", "mark", "cnt", "slot", "val", "idx", "throttle")


class Prog:
    ENGS = ("pe", "act", "dve", "pool", "sync")
    EP = 2000
    NSLOT = 8

    def __init__(self, nc, stack):
        self.nc = nc
        self.stack = stack
        self.ops = []
        self.trk = {}
        self.last = {e: None for e in self.ENGS}
        self.ndma = {e: 0 for e in self.ENGS}
        self.dma_hist = {e: [] for e in self.ENGS}
        self.uid = 0

    def sb(self, shape, dt, name=None, ctx=None):
        self.uid += 1
        nm = f"{name or 't'}_{self.uid}"
        return (ctx or self.stack).enter_context(self.nc.sbuf_tensor(nm, list(shape), dt))

    def ps(self, shape, dt, name=None, ctx=None):
        self.uid += 1
        nm = f"{name or 'p'}_{self.uid}"
        return (ctx or self.stack).enter_context(self.nc.psum_tensor(nm, list(shape), dt))

    @staticmethod
    def _rect(ap):
        a = ap.ap
        off = int(ap.offset)
        pstep = int(a[0][0])
        npart = int(a[0][1])
        if pstep == 0:
            pstep = 1 << 40
        p0 = off // pstep
        f0 = off % pstep
        ext = 1
        for s, c in a[1:]:
            ext += (int(c) - 1) * abs(int(s))
        if "PSUM" in str(ap.space).upper():
            return ap.tensor.name, ((p0 // 32) * 32, ((p0 + npart + 31) // 32) * 32, 0, 1 << 30)
        return ap.tensor.name, (p0, p0 + npart, f0, f0 + ext)

    @staticmethod
    def _ov(a, b):
        return a[0] < b[1] and b[0] < a[1] and a[2] < b[3] and b[2] < a[3]

    @staticmethod
    def _inside(a, b):
        return a[0] >= b[0] and a[1] <= b[1] and a[2] >= b[2] and a[3] <= b[3]

    def _is_tracked(self, ap):
        sp = str(ap.space)
        return ("SB" in sp.upper()) or ("PSUM" in sp.upper())

    def op(self, eng, fn, reads=(), writes=(), dma=False):
        o = _Op()
        o.eng = eng
        o.fn = fn
        o.dma = dma
        o.deps = {}
        o.ddeps = []
        o.mark = False
        o.cnt = 0
        o.throttle = None
        prods = []
        for ap in reads:
            if ap is None or isinstance(ap, (int, float)) or not self._is_tracked(ap):
                continue
            nm, r = self._rect(ap)
            ents = self.trk.setdefault(nm, [])
            is_ps = "PSUM" in str(ap.space).upper()
            for e in ents:
                if self._ov(e[0], r):
                    if e[1] is not None:
                        prods.append(e[1])
                    if is_ps:
                        for rd in e[2]:
                            if rd.eng != eng:
                                prods.append(rd)
                    e[2].append(o)
        for ap in writes:
            if ap is None or not self._is_tracked(ap):
                continue
            nm, r = self._rect(ap)
            ents = self.trk.setdefault(nm, [])
            keep = []
            for e in ents:
                if self._ov(e[0], r):
                    if e[1] is not None:
                        prods.append(e[1])
                    prods.extend(e[2])
                    if self._inside(e[0], r):
                        continue
                keep.append(e)
            keep.append([r, o, []])
            self.trk[nm] = keep
        for p in prods:
            if p is o:
                continue
            if p.dma:
                if p not in o.ddeps:
                    o.ddeps.append(p)
            else:
                if p.eng == "pe" and eng == "pe" and not dma:
                    continue
                cur = o.deps.get(p.eng)
                if cur is None or p.idx > cur.idx:
                    o.deps[p.eng] = p
        if dma:
            i = self.ndma[eng]
            self.ndma[eng] = i + 1
            o.slot = i % self.NSLOT
            o.val = 16 * (i // self.NSLOT + 1)
            if i >= self.NSLOT:
                o.throttle = self.dma_hist[eng][i - self.NSLOT]
            self.dma_hist[eng].append(o)
        else:
            self.last[eng] = o
        o.idx = len(self.ops)
        self.ops.append(o)
        return o

    def barrier(self):
        lasts = [self.last[e] for e in self.ENGS if self.last[e] is not None]
        dm = []
        for e in self.ENGS:
            dm.extend(self.dma_hist[e][-self.NSLOT:])
        for e in self.ENGS:
            o = _Op()
            o.eng = e
            o.fn = None
            o.dma = False
            o.deps = {}
            for p in lasts:
                if p.eng == e and e == "pe":
                    continue
                o.deps[p.eng] = p
            o.ddeps = list(dm)
            o.mark = False
            o.cnt = 0
            o.throttle = None
            o.idx = len(self.ops)
            self.ops.append(o)
        self.trk = {}
        self.last = {e: None for e in self.ENGS}
        self.flush()

    def _csem(self, e, b):
        lst = self.csem.setdefault(e, [])
        while len(lst) <= b:
            lst.append(self.stack.enter_context(self.nc.semaphore(f"c_{e}_{len(lst)}")))
        return lst[b]

    def _dsem(self, e, s):
        if e not in self.dsem:
            self.dsem[e] = [self.stack.enter_context(self.nc.semaphore(f"d_{e}_{i}")) for i in range(self.NSLOT)]
        return self.dsem[e][s]

    def _wait_dma(self, E, e, d):
        key = (d.eng, d.slot)
        if self.dwaited[e].get(key, 0) < d.val:
            E.wait_ge(self._dsem(d.eng, d.slot), d.val)
            self.dwaited[e][key] = d.val

    def flush(self):
        nc = self.nc
        if not hasattr(self, "counts"):
            self.counts = {e: 0 for e in self.ENGS}
            self.csem = {}
            self.dsem = {}
            self.waited = {e: {f: 0 for f in self.ENGS} for e in self.ENGS}
            self.dwaited = {e: {} for e in self.ENGS}
            self.pos = 0
        engobj = {"pe": nc.tensor, "act": nc.scalar, "dve": nc.vector, "pool": nc.gpsimd, "sync": nc.sync}
        batch = self.ops[self.pos:]
        for o in batch:
            for p in o.deps.values():
                assert p.idx >= self.pos, "dependency on an already-flushed op"
                p.mark = True
        for o in batch:
            if o.fn is not None and not o.dma and o.mark:
                self.counts[o.eng] += 1
                o.cnt = self.counts[o.eng]
        for o in batch:
            e = o.eng
            E = engobj[e]
            for f, p in o.deps.items():
                if p.cnt > self.waited[e][f]:
                    b = (p.cnt - 1) // self.EP
                    E.wait_ge(self._csem(f, b), (p.cnt - 1) % self.EP + 1)
                    self.waited[e][f] = p.cnt
            for d in o.ddeps:
                self._wait_dma(E, e, d)
            if o.throttle is not None:
                self._wait_dma(E, e, o.throttle)
            if o.fn is None:
                continue
            ins = o.fn(E)
            if o.dma:
                ins.then_inc(self._dsem(e, o.slot), 16)
            elif o.mark:
                b = (o.cnt - 1) // self.EP
                ins.then_inc(self._csem(e, b), 1)
            o.fn = None
        self.pos = len(self.ops)

    def finish(self):
        self.flush()
        E = self.nc.sync
        for q in self.ENGS:
            for d in self.dma_hist[q][-self.NSLOT:]:
                self._wait_dma(E, "sync", d)

    def mm(self, out, lhsT, rhs, start=True, stop=True):
        return self.op("pe", lambda E: E.matmul(out, lhsT, rhs, start=start, stop=stop),
                       reads=[lhsT, rhs], writes=[out])

    def tr(self, out, in_, ident):
        return self.op("pe", lambda E: E.transpose(out, in_, ident), reads=[in_, ident], writes=[out])

    def act(self, out, in_, func, bias=None, scale=None, accum_out=None, eng="act"):
        kw = {}
        rd = [in_]
        if bias is not None:
            kw["bias"] = bias
            rd.append(bias)
        if scale is not None:
            kw["scale"] = scale
            rd.append(scale)
        wr = [out]
        if accum_out is not None:
            kw["accum_out"] = accum_out
            wr.append(accum_out)
        return self.op(eng, lambda E: E.activation(out, in_, func, **kw), reads=rd, writes=wr)

    def tt(self, out, in0, in1, op, eng="dve"):
        return self.op(eng, lambda E: E.tensor_tensor(out, in0, in1, op), reads=[in0, in1], writes=[out])

    def ts(self, out, in0, s1, s2, op0, op1=None, eng="dve", accum_out=None):
        rd = [in0, s1, s2]
        wr = [out]
        if accum_out is not None:
            wr.append(accum_out)

        def f(E):
            kw = {}
            if accum_out is not None:
                kw["accum_out"] = accum_out
            if op1 is None:
                return E.tensor_scalar(out, in0, s1, None, op0, **kw)
            return E.tensor_scalar(out, in0, s1, s2, op0, op1, **kw)
        return self.op(eng, f, reads=rd, writes=wr)

    def stt(self, out, in0, scalar, in1, op0, op1, eng="dve"):
        return self.op(eng, lambda E: E.scalar_tensor_tensor(out, in0, scalar, in1, op0, op1),
                       reads=[in0, scalar, in1], writes=[out])

    def copy(self, out, in_, eng="dve"):
        if eng == "act":
            return self.op("act", lambda E: E.copy(out, in_), reads=[in_], writes=[out])
        return self.op(eng, lambda E: E.tensor_copy(out, in_), reads=[in_], writes=[out])

    def red(self, out, in_, op=None, eng="dve"):
        op = op or ALU.add
        return self.op(eng, lambda E: E.tensor_reduce(out, in_, AX.X, op), reads=[in_], writes=[out])

    def recip(self, out, in_):
        return self.op("dve", lambda E: E.reciprocal(out, in_), reads=[in_], writes=[out])

    def memset(self, ap, val, eng="dve"):
        return self.op(eng, lambda E: E.memset(ap, val), reads=[], writes=[ap])

    def dma(self, out, in_, q="sync", slow=False):
        kw = {}
        if slow:
            kw["allow_slow_non_contiguous"] = True
        return self.op(q, lambda E: E.dma_start(out=out, in_=in_, **kw), reads=[in_], writes=[out], dma=True)


FF_SPLITS = (4, 4, 4, 4, 3, 3)


def build_program(dbg=None):
    from contextlib import ExitStack
    nc = bass.Bass("TRN2", target_bir_lowering=False)
    stack = ExitStack()
    P = Prog(nc, stack)

    def din(name, shape, dt=F32):
        return nc.dram_tensor(name, list(shape), dt, kind="ExternalInput").ap()

    x_d = din("x", [S, D])
    p_d = din("p", [S, PLE])
    pos_d = din("positions", [S], I32)
    wn = {}
    for nm, shp in (("ffn1_norm", [D]), ("ffn1_w_gate", [D, DFF]), ("ffn1_w_up", [D, DFF]), ("ffn1_w_down", [DFF, D]),
                    ("mix_norm", [D]), ("w_in", [D, 6144]), ("rwkv_mu", [1792]), ("rwkv_w0", [512]),
                    ("rwkv_w2", [64, 512]), ("rwkv_a0", [512]), ("rwkv_a2", [64, 512]), ("rwkv_g2", [128, 512]),
                    ("rwkv_k_k", [512]), ("rwkv_k_a", [512]), ("rwkv_r_k", [512]), ("rwkv_gn_w", [512]),
                    ("rwkv_gn_b", [512]), ("q_norm", [64]), ("k_norm", [64]), ("w_br_rwkv", [512, D]),
                    ("w_br_attn", [256, D]), ("w_out", [D, D]), ("ffn2_norm", [D]), ("ffn2_w_gate", [D, DFF]),
                    ("ffn2_w_up", [D, DFF]), ("ffn2_w_down", [DFF, D]), ("ple_norm", [D]), ("ple_w_gate", [D, D]),
                    ("ple_w_proj", [PLE, D])):
        wn[nm] = din(nm, shp)
    ident_d = din("c_ident", [128, 128])
    cst = {}
    for nm, shp in (("c_maskc", [128, 128]), ("c_maskp", [128, 128]), ("c_onespad", [128, 2, 128]),
                    ("c_invfreq", [128, 32]), ("c_maskuu", [128, 256]), ("c_masksl", [128, 128]),
                    ("c_blockones", [128, 128])):
        cst[nm] = din(nm, shp)
    if dbg is not None:
        dbg = {"ya": nc.dram_tensor("dbg_ya", [128, 2, S], F32, kind="ExternalOutput").ap(),
               "yr": nc.dram_tensor("dbg_yr", [128, 4, S], F32, kind="ExternalOutput").ap()}
    out_d = nc.dram_tensor("out", [S, D], F32, kind="ExternalOutput").ap()

    X = P.sb([128, NT, D], F32, "X")
    ident = P.sb([128, 128], BF16, "ident")
    P.dma(ident[:], ident_d[:, :], q="pool")
    xv = x_d.rearrange("(t p) d -> p t d", p=128)
    for i in range(4):
        P.dma(X[:, 4 * i:4 * i + 4, :], xv[:, 4 * i:4 * i + 4, :], q="sync")

    from contextlib import ExitStack as _ES

    def norm_to_hT(ctx, hT, gain_d, tp_banks):
        gain_bc = P.sb([128, D], F32, "gain_bc", ctx)
        P.dma(gain_bc[:], gain_d.partition_broadcast(128), q="sync")
        ss = P.sb([128, NT], F32, "ss", ctx)
        rstd = P.sb([128, NT], F32, "rstd", ctx)
        junk = P.sb([128, D], BF16, "junk", ctx)
        xn = [P.sb([128, D], BF16, "xn", ctx) for _ in range(2)]
        P.memset(ss[:], 0.0)
        for t in range(NT):
            P.act(junk[:], X[:, t, :], AF.Square, accum_out=ss[:, t:t + 1])
        P.ts(rstd[:], ss[:], 1.0 / D, RMS_EPS, ALU.mult, ALU.add)
        P.act(rstd[:], rstd[:], AF.Sqrt)
        P.recip(rstd[:], rstd[:])
        for t in range(NT):
            xt = xn[t % 2]
            P.stt(xt[:], X[:, t, :], rstd[:, t:t + 1], gain_bc[:], ALU.mult, ALU.mult)
            pt = tp_banks[t % 2]
            for kc in range(8):
                P.tr(pt[:, kc * 128:(kc + 1) * 128], xt[:, kc * 128:(kc + 1) * 128], ident[:])
            P.copy(hT[:, :, t * 128:(t + 1) * 128], pt[:].rearrange("p (k c) -> p k c", k=8),
                   eng=("act" if t % 2 else "dve"))

    def ffn(ctx, hT, wg_d, wu_d, wd_d, psum):
        wgv = wg_d.rearrange("(k p) c -> p k c", p=128)
        wuv = wu_d.rearrange("(k p) c -> p k c", p=128)
        wdv = wd_d.rearrange("(f p) d -> p f d", p=128)
        Wg = [P.sb([128, 8, 512], BF16, "Wg", ctx) for _ in range(2)]
        Wu = [P.sb([128, 8, 512], BF16, "Wu", ctx) for _ in range(2)]
        Wd = [P.sb([128, 4, D], BF16, "Wd", ctx) for _ in range(2)]
        aT = [P.sb([128, 4, S], BF16, "aT", ctx) for _ in range(2)]
        sg = [P.sb([128, 512], BF16, "sg", ctx) for _ in range(2)]
        pg, pu, pd = psum["g"], psum["u"], psum["d"]
        offs = []
        f0 = 0
        for nf in FF_SPLITS:
            offs.append((f0, nf))
            f0 += nf
        cnt = {"gu": 0, "d": 0}

        def load(si):
            f0, nf = offs[si]
            sl = si % 2
            c0, c1 = f0 * 128, (f0 + nf) * 128
            P.dma(Wg[sl][:, :, 0:nf * 128], wgv[:, :, c0:c1], q="pool")
            P.dma(Wu[sl][:, :, 0:nf * 128], wuv[:, :, c0:c1], q="pool")
            P.dma(Wd[sl][:, 0:nf, :], wdv[:, f0:f0 + nf, :], q="pool")

        def gu(si):
            f0, nf = offs[si]
            sl = si % 2
            for fl in range(nf):
                for tg in range(4):
                    i = cnt["gu"] % 2
                    cnt["gu"] += 1
                    for kc in range(8):
                        P.mm(pg[i][:], Wg[sl][:, kc, fl * 128:(fl + 1) * 128], hT[:, kc, tg * 512:(tg + 1) * 512],
                             start=(kc == 0), stop=(kc == 7))
                    for kc in range(8):
                        P.mm(pu[i][:], Wu[sl][:, kc, fl * 128:(fl + 1) * 128], hT[:, kc, tg * 512:(tg + 1) * 512],
                             start=(kc == 0), stop=(kc == 7))
                    P.act(sg[i][:], pg[i][:], AF.Silu)
                    P.tt(aT[sl][:, fl, tg * 512:(tg + 1) * 512], sg[i][:], pu[i][:], ALU.mult)

        def down(si):
            f0, nf = offs[si]
            sl = si % 2
            for t in range(NT):
                for dh in range(2):
                    i = cnt["d"] % 2
                    cnt["d"] += 1
                    for fl in range(nf):
                        P.mm(pd[i][:], aT[sl][:, fl, t * 128:(t + 1) * 128], Wd[sl][:, fl, dh * 512:(dh + 1) * 512],
                             start=(fl == 0), stop=(fl == nf - 1))
                    xs = X[:, t, dh * 512:(dh + 1) * 512]
                    P.stt(xs, pd[i][:], 0.5, xs, ALU.mult, ALU.add)

        ns = len(offs)
        load(0)
        load(1)
        gu(0)
        for si in range(1, ns):
            gu(si)
            down(si - 1)
            if si + 1 < ns:
                load(si + 1)
        down(ns - 1)

    psA = [P.ps([128, 512], F32, "psA") for _ in range(6)]
    tpb = [P.ps([128, 1024], BF16, "tpb") for _ in range(2)]
    psum_ffn = {"g": psA[0:2], "u": psA[2:4], "d": psA[4:6]}

    def ffn_phase(norm_d, wg_d, wu_d, wd_d):
        with _ES() as ctx:
            hT = P.sb([128, 8, S], BF16, "hT", ctx)
            norm_to_hT(ctx, hT, norm_d, tpb)
            ffn(ctx, hT, wg_d, wu_d, wd_d, psum_ffn)
            P.barrier()

    def ple_phase():
        with _ES() as ctx:
            hT = P.sb([128, 8, S], BF16, "hT", ctx)
            Wpg = P.sb([128, 8, D], BF16, "Wpg", ctx)
            Wpp = P.sb([128, 2, D], BF16, "Wpp", ctx)
            P.dma(Wpg[:], wn["ple_w_gate"].rearrange("(k p) c -> p k c", p=128), q="pool")
            P.dma(Wpp[:], wn["ple_w_proj"].rearrange("(k p) c -> p k c", p=128), q="pool")
            pin = P.sb([128, NT, PLE], BF16, "pin", ctx)
            P.dma(pin[:], p_d.rearrange("(t p) c -> p t c", p=128), q="pool")
            pT = P.sb([128, 2, S], BF16, "pT", ctx)
            norm_to_hT(ctx, hT, wn["ple_norm"], tpb)
            for t in range(NT):
                pt = tpb[t % 2]
                for k in range(2):
                    P.tr(pt[:, k * 128:(k + 1) * 128], pin[:, t, k * 128:(k + 1) * 128], ident[:])
                P.copy(pT[:, :, t * 128:(t + 1) * 128], pt[:, 0:256].rearrange("p (k c) -> p k c", k=2),
                       eng=("act" if t % 2 else "dve"))
            sgs = [P.sb([128, 512], F32, "sgs", ctx) for _ in range(2)]
            n = 0
            ov = out_d.rearrange("(t p) d -> p t d", p=128)
            for t in range(NT):
                for dh in range(2):
                    i = n % 2
                    n += 1
                    pG, pP = psA[i], psA[2 + i]
                    for kc in range(8):
                        P.mm(pG[:], hT[:, kc, t * 128:(t + 1) * 128], Wpg[:, kc, dh * 512:(dh + 1) * 512],
                             start=(kc == 0), stop=(kc == 7))
                    for k in range(2):
                        P.mm(pP[:], pT[:, k, t * 128:(t + 1) * 128], Wpp[:, k, dh * 512:(dh + 1) * 512],
                             start=(k == 0), stop=(k == 1))
                    P.act(sgs[i][:], pG[:], AF.Sigmoid)
                    P.tt(sgs[i][:], sgs[i][:], pP[:], ALU.mult)
                    xs = X[:, t, dh * 512:(dh + 1) * 512]
                    P.tt(xs, xs, sgs[i][:], ALU.add)
                P.dma(ov[:, t, :], X[:, t, :], q="sync")
            P.barrier()


    TWO_PI = 6.283185307179586
    C1 = 6.28125
    C2 = TWO_PI - C1
    PI = 3.141592653589793

    def tokslice(hT, kc, t0, d):
        if d == 1:
            return hT[:, kc, t0:t0 + 128]
        return hT[:, kc, t0:t0 + 127 * d + 1:d]

    def merge_branch(ctx0, hT, yT, nyc, wbr_d, gate_col0):
        if SKIPMERGE:
            return
        with _ES() as ctx:
            Wbr = P.sb([128, nyc, D], BF16, "Wbr", ctx)
            P.dma(Wbr[:], wbr_d.rearrange("(c p) d -> p c d", p=128), q="pool")
            Wg = P.sb([128, 8, D], BF16, "Wgt", ctx)
            wiv = wn["w_in"].rearrange("(k p) c -> p k c", p=128)
            P.dma(Wg[:], wiv[:, :, gate_col0:gate_col0 + D], q="pool")
            Wo = P.sb([128, 8, D], BF16, "Wo", ctx)
            P.dma(Wo[:], wn["w_out"].rearrange("(k p) c -> p k c", p=128), q="pool")
            mT = P.sb([128, 8, S], BF16, "mT", ctx)
            sgm = [P.sb([128, 512], F32, "sgm", ctx) for _ in range(2)]
            n = 0
            for dc in range(8):
                for tg in range(4):
                    i = n % 2
                    n += 1
                    pG, pB = psA[i], psA[2 + i]
                    ts_ = slice(tg * 512, (tg + 1) * 512)
                    for kc in range(8):
                        P.mm(pG[:], Wg[:, kc, dc * 128:(dc + 1) * 128], hT[:, kc, ts_], start=(kc == 0), stop=(kc == 7))
                    for c in range(nyc):
                        P.mm(pB[:], Wbr[:, c, dc * 128:(dc + 1) * 128], yT[:, c, ts_], start=(c == 0), stop=(c == nyc - 1))
                    P.act(sgm[i][:], pG[:], AF.Sigmoid)
                    P.tt(mT[:, dc, ts_], sgm[i][:], pB[:], ALU.mult)
            n = 0
            for t in range(NT):
                for dh in range(2):
                    pd_ = psA[4 + n % 2]
                    n += 1
                    for dc in range(8):
                        P.mm(pd_[:], mT[:, dc, t * 128:(t + 1) * 128], Wo[:, dc, dh * 512:(dh + 1) * 512],
                             start=(dc == 0), stop=(dc == 7))
                    xs = X[:, t, dh * 512:(dh + 1) * 512]
                    P.tt(xs, xs, pd_[:], ALU.add)
            P.barrier()

    def attn_branch(hT):
        with _ES() as ctx:
            yaT = P.sb([128, 2, S], BF16, "yaT", ctx)
            with _ES() as c2:
                maskC = P.sb([128, 128], BF16, "maskC", c2)
                maskP = P.sb([128, 128], BF16, "maskP", c2)
                onesp = P.sb([128, 2, 128], BF16, "onesp", c2)
                P.dma(maskC[:], cst["c_maskc"][:, :], q="pool")
                P.dma(maskP[:], cst["c_maskp"][:, :], q="pool")
                P.dma(onesp[:], cst["c_onespad"][:, :, :], q="pool")
                invf = P.sb([128, 32], F32, "invf", c2)
                P.dma(invf[:], cst["c_invfreq"][:, :], q="sync")
                gain4 = P.sb([128, 4, 64], F32, "gain4", c2)
                for hh in range(4):
                    P.dma(gain4[:, hh, :], (wn["q_norm"] if hh < 2 else wn["k_norm"]).partition_broadcast(128), q="sync")
                cosT = [P.sb([128, 16 * 32], F32, "cosT", c2) for _ in range(3)]
                sinT = [P.sb([128, 16 * 32], F32, "sinT", c2) for _ in range(3)]
                with _ES() as c3:
                    posi = P.sb([128, 16], I32, "posi", c3)
                    posf = P.sb([128, 16], F32, "posf", c3)
                    ang = P.sb([128, 512], F32, "ang", c3)
                    aa = P.sb([128, 512], F32, "aa", c3)
                    kf = P.sb([128, 512], F32, "kf", c3)
                    ki = P.sb([128, 512], I32, "ki", c3)
                    rr = P.sb([128, 512], F32, "rr", c3)
                    mk = P.sb([128, 512], F32, "mk", c3)
                    for g, d in enumerate((1, 4, 16)):
                        pv = pos_d.rearrange("(ib p j) -> p j ib", p=128, j=d)
                        nib = 16 // d
                        for m in range(16):
                            P.dma(posi[:, m:m + 1], pv[:, m // nib, (m % nib):(m % nib) + 1], q="sync", slow=True)
                        P.copy(posf[:], posi[:])
                        for m in range(16):
                            P.ts(ang[:, m * 32:(m + 1) * 32], invf[:], posf[:, m:m + 1], None, ALU.mult)
                        for dst, shift in ((sinT[g], 0.0), (cosT[g], PI / 2)):
                            P.ts(aa[:], ang[:], shift, None, ALU.add)
                            P.ts(kf[:], aa[:], 1.0 / TWO_PI, None, ALU.mult)
                            P.copy(ki[:], kf[:])
                            P.copy(kf[:], ki[:])
                            P.stt(rr[:], kf[:], -C1, aa[:], ALU.mult, ALU.add)
                            P.stt(rr[:], kf[:], -C2, rr[:], ALU.mult, ALU.add)
                            P.ts(mk[:], rr[:], PI, None, ALU.is_gt)
                            P.stt(rr[:], mk[:], -TWO_PI, rr[:], ALU.mult, ALU.add)
                            P.ts(mk[:], rr[:], -PI, None, ALU.is_lt)
                            P.stt(rr[:], mk[:], TWO_PI, rr[:], ALU.mult, ALU.add)
                            P.act(dst[:], rr[:], AF.Sin)
                    P.barrier()
                Wq = [[P.sb([128, 8, 128], BF16, "Wq", c2) for _ in range(3)] for _ in range(2)]
                qkT = P.sb([128, 2, S], BF16, "qkT", c2)
                Vpad = P.sb([128, 16, 2, 128], BF16, "Vpad", c2)
                acc = P.sb([128, 2, S], F32, "acc", c2)
                qkb = P.sb([128, 4, 256], F32, "qkb", c2)
                sqb = P.sb([128, 4, 256], F32, "sqb", c2)
                qnb = P.sb([128, 4, 256], F32, "qnb", c2)
                ss16 = P.sb([128, 16], F32, "ss16", c2)
                t1 = P.sb([128, 512], F32, "t1", c2)
                t2 = P.sb([128, 512], F32, "t2", c2)
                t3 = P.sb([128, 512], F32, "t3", c2)
                t4 = P.sb([128, 512], F32, "t4", c2)
                qrb = P.sb([128, 4, 256], BF16, "qrb", c2)
                PT = [P.sb([128, 4, 128], BF16, "PT", c2) for _ in range(2)]
                if _os.environ.get('KPRINTMEM'):
                    print('ATTN sbuf remaining', nc.sbuf_bytes_remaining)
                P.memset(Vpad[:], 0.0)
                wiv = wn["w_in"].rearrange("(k p) c -> p k c", p=128)
                it = 0
                for pr in range(ATT_PR):
                    for g, d in enumerate((1, 4, 16)):
                        if g not in ATT_G:
                            continue
                        L = S // d
                        W = Wq[it % 2]
                        it += 1
                        qc0 = 1792 + (g * 4 + pr * 2) * 64
                        for j3 in range(3):
                            P.dma(W[j3][:], wiv[:, :, qc0 + 768 * j3:qc0 + 768 * j3 + 128], q="pool")
                        TB = 4
                        for mb in range(0, 16, TB):
                            for mi in range(TB):
                                m = mb + mi
                                j, i0 = divmod(m * 128, L)
                                t0 = i0 * d + j
                                bank = psA[3 + m % 2]
                                for j3 in range(3):
                                    for kc in range(8):
                                        P.mm(bank[:, j3 * 128:(j3 + 1) * 128], tokslice(hT, kc, t0, d), W[j3][:, kc, :],
                                             start=(kc == 0), stop=(kc == 7))
                                P.act(qkb[:, mi, :], bank[:, 0:256], AF.Copy)
                                P.copy(Vpad[:, m, 0, 0:64], bank[:, 256:320])
                                P.copy(Vpad[:, m, 1, 64:128], bank[:, 320:384])
                            P.tt(sqb[:], qkb[:], qkb[:], ALU.mult)
                            P.red(ss16[:], sqb[:].rearrange("p a (h d) -> p (a h) d", d=64))
                            P.ts(ss16[:], ss16[:], 1.0 / 64, RMS_EPS, ALU.mult, ALU.add)
                            P.act(ss16[:], ss16[:], AF.Sqrt)
                            P.recip(ss16[:], ss16[:])
                            P.tt(qnb[:].rearrange("p a (h d) -> p (a h) d", d=64), qkb[:].rearrange("p a (h d) -> p (a h) d", d=64),
                                 ss16[:].unsqueeze(2).to_broadcast([128, 16, 64]), ALU.mult)
                            P.tt(qnb[:], qnb[:], gain4[:].rearrange("p h d -> p (h d)").unsqueeze(1).to_broadcast([128, TB, 256]), ALU.mult)
                            q5 = qnb[:].rearrange("p a (h t i) -> p a h t i", t=2, i=32)
                            x1, x2 = q5[:, :, :, 0, :], q5[:, :, :, 1, :]
                            cb = cosT[g][:, mb * 32:(mb + TB) * 32].rearrange("p (a i) -> p a i", i=32).unsqueeze(2).to_broadcast([128, TB, 4, 32])
                            sb_ = sinT[g][:, mb * 32:(mb + TB) * 32].rearrange("p (a i) -> p a i", i=32).unsqueeze(2).to_broadcast([128, TB, 4, 32])
                            v4 = lambda tl: tl[:].rearrange("p (a h i) -> p a h i", h=4, i=32)
                            P.tt(v4(t1), x1, cb, ALU.mult)
                            P.tt(v4(t2), x2, sb_, ALU.mult)
                            P.tt(v4(t3), x2, cb, ALU.mult)
                            P.tt(v4(t4), x1, sb_, ALU.mult)
                            r5 = qrb[:].rearrange("p a (h t i) -> p a h t i", t=2, i=32)
                            P.tt(r5[:, :, :, 0, :], v4(t1), v4(t2), ALU.subtract)
                            P.tt(r5[:, :, :, 1, :], v4(t3), v4(t4), ALU.add)
                            tp = tpb[(mb // TB) % 2]
                            for mi in range(TB):
                                for a in range(2):
                                    P.tr(tp[:, a * 512 + mi * 128:a * 512 + (mi + 1) * 128], qrb[:, mi, a * 128:(a + 1) * 128], ident[:])
                            P.copy(qkT[:, :, mb * 128:(mb + TB) * 128], tp[:].rearrange("p (a c) -> p a c", a=2), eng="act")
                        for m in range(ATT_M if ATT_STAGE >= 2 else 0):
                            j, i0 = divmod(m * 128, L)
                            t0 = i0 * d + j
                            has_prev = (i0 // 128) > 0
                            sc = psA[m % 2][:].rearrange("p (a c) -> p a c", a=4)
                            qs = slice(m * 128, (m + 1) * 128)
                            for e in range(2):
                                pse = slice(e * 64, (e + 1) * 64)
                                P.mm(sc[:, e, :], qkT[pse, 1, qs], qkT[pse, 0, qs], start=True, stop=False)
                                P.mm(sc[:, e, :], ident[:], maskC[:], start=False, stop=True)
                                if has_prev:
                                    P.mm(sc[:, 2 + e, :], qkT[pse, 1, (m - 1) * 128:m * 128], qkT[pse, 0, qs], start=True, stop=False)
                                    P.mm(sc[:, 2 + e, :], ident[:], maskP[:], start=False, stop=True)
                            nk = 4 if has_prev else 2
                            pt_ = PT[m % 2]
                            P.act(pt_[:, 0:nk, :], sc[:, 0:nk, :], AF.Exp, scale=0.125)
                            OL = psA[2][:, (m % 2) * 256:(m % 2) * 256 + 256].rearrange("p (a c) -> p a c", a=2)
                            seq = [(kb, e) for kb in range(nk // 2) for e in range(2)]
                            for ii, (kb, e) in enumerate(seq):
                                P.mm(OL[:, 0, :], Vpad[:, m - kb, e, :], pt_[:, kb * 2 + e, :], start=(ii == 0), stop=(ii == len(seq) - 1))
                            for ii, (kb, e) in enumerate(seq):
                                P.mm(OL[:, 1, :], onesp[:, e, :], pt_[:, kb * 2 + e, :], start=(ii == 0), stop=(ii == len(seq) - 1))
                            if d == 1:
                                av = acc[:, :, t0:t0 + 128]
                            else:
                                av = acc[:, :, t0:t0 + 127 * d + 1:d]
                            if g == 0:
                                P.copy(av, OL)
                            else:
                                P.tt(av, av, OL, ALU.add)
                    P.recip(acc[:, 1, :], acc[:, 1, :])
                    P.tt(yaT[:, pr, :], acc[:, 0, :], acc[:, 1, :], ALU.mult)
                if dbg is not None:
                    P.dma(dbg["ya"][:, :, :], yaT[:], q="pool")
                P.barrier()
            merge_branch(ctx, hT, yaT, 2, wn["w_br_attn"], 4096 + 1024)

    def rwkv_branch(hT):
        with _ES() as ctx:
            yrT = P.sb([128, 4, S], BF16, "yrT", ctx)
            with _ES() as c2:
                sbt = lambda shp, dt, nm: P.sb(shp, dt, nm, c2)
                maskuu = sbt([128, 256], BF16, "maskuu")
                masksl = sbt([128, 128], BF16, "masksl")
                bones = sbt([128, 128], BF16, "bones")
                P.dma(maskuu[:], cst["c_maskuu"][:, :], q="pool")
                P.dma(masksl[:], cst["c_masksl"][:, :], q="pool")
                P.dma(bones[:], cst["c_blockones"][:, :], q="pool")
                muT = sbt([128, 14], F32, "muT")
                for c in range(14):
                    P.dma(muT[:, c:c + 1], wn["rwkv_mu"][c * 128:(c + 1) * 128].rearrange("(p o) -> p o", o=1), q="sync", slow=True)
                cols = {}
                for nm in ("rwkv_w0", "rwkv_a0", "rwkv_k_k", "rwkv_k_a", "rwkv_r_k", "rwkv_gn_w", "rwkv_gn_b"):
                    cols[nm] = sbt([128, 4], F32, nm)
                    for c in range(4):
                        P.dma(cols[nm][:, c:c + 1], wn[nm][c * 128:(c + 1) * 128].rearrange("(p o) -> p o", o=1), q="sync", slow=True)
                omk = sbt([128, 4], F32, "omk")
                P.ts(omk[:], cols["rwkv_k_a"][:], -1.0, 1.0, ALU.mult, ALU.add)
                w2b = sbt([128, 512], BF16, "w2b")
                a2b = sbt([128, 512], BF16, "a2b")
                g2b = sbt([128, 512], BF16, "g2b")
                P.dma(w2b[0:64, :], wn["rwkv_w2"][:, :], q="pool")
                P.dma(a2b[64:128, :], wn["rwkv_a2"][:, :], q="pool")
                P.dma(g2b[:], wn["rwkv_g2"][:, :], q="pool")
                Wst = [sbt([128, 8, 256], BF16, "Wst") for _ in range(2)]
                zt = sbt([128, 14, 128], F32, "zt")
                zcb = [sbt([128, 129], F32, "zc") for _ in range(2)]
                carry = sbt([128, 14], F32, "carry")
                P.memset(carry[:], 0.0)
                ones_b = sbt([128, 128], BF16, "ones_b")
                P.memset(ones_b[:], 1.0)
                dz = sbt([128, 128], F32, "dz")
                th = sbt([128, 128], BF16, "th")
                adb = sbt([128, 128], BF16, "adb")
                sgd = sbt([128, 128], BF16, "sgd")
                f32t = lambda nm: sbt([128, 128], F32, nm)
                NSET = 4
                tset = []
                for _i in range(NSET):
                    tset.append({nm: f32t(nm) for nm in ("s_", "a_", "L0", "E1", "E2", "E3", "E4", "kk", "bv", "tA", "kp", "rn")})
                    tset[-1]["kk2"] = sbt([128, 128], BF16, "kk2")
                    tset[-1]["t2b"] = sbt([128, 128], BF16, "t2b")
                gCt = sbt([128, 4], F32, "gCt")
                AR = sbt([128, 4, 256], BF16, "AR")
                Bt = sbt([128, 4, 128], BF16, "Bt")
                Kt = sbt([128, 4, 128], BF16, "Kt")
                gb = sbt([128, 4, 128], BF16, "gb")
                bonus = sbt([128, 4, 128], F32, "bonus")
                Bh = sbt([128, 4, 128], BF16, "Bh")
                Kh = sbt([128, 4, 128], BF16, "Kh")
                vb = sbt([128, 4, 128], BF16, "vb")
                VT = sbt([128, 512], BF16, "VT")
                KhT = sbt([128, 512], BF16, "KhT")
                BhT = sbt([128, 512], BF16, "BhT")
                Ytok = sbt([128, 512], F32, "Ytok")
                yc = sbt([128, 512], F32, "yc")
                ynb = sbt([128, 512], BF16, "ynb")
                sm = sbt([128, 8], F32, "sm"); sm2 = sbt([128, 8], F32, "sm2"); mean = sbt([128, 8], F32, "mean")
                msq = sbt([128, 8], F32, "msq"); var = sbt([128, 8], F32, "var")
                yo = [f32t("yo") for _ in range(2)]
                MB = [sbt([128, 4, 256], BF16, "MB") for _ in range(2)]
                MKt = [sbt([128, 4, 256], BF16, "MKt") for _ in range(2)]
                Tfw = [sbt([128, 4, 128], BF16, "Tfw") for _ in range(2)]
                TA = [[sbt([128, 4, 256], BF16, "TA") for _ in range(2)] for _ in range(2)]
                ATT = [[sbt([128, 4, 128], BF16, "ATT") for _ in range(2)] for _ in range(2)]
                Sf = sbt([128, 4, 64], F32, "Sf")
                Sb = sbt([128, 4, 64], BF16, "Sb")
                W1 = sbt([128, 512], BF16, "W1")
                UT = sbt([128, 512], BF16, "UT")
                if _os.environ.get('KPRINTMEM'):
                    print('RWKV sbuf remaining', nc.sbuf_bytes_remaining)
                P.memset(Sf[:], 0.0)
                P.memset(Sb[:], 0.0)
                wiv = wn["w_in"].rearrange("(k p) c -> p k c", p=128)
                GN_EPS = 64e-5
                for n in range(RWKV_N):
                    tl = slice(n * 128, (n + 1) * 128)
                    for pc in range(7):
                        Wp = Wst[pc % 2]
                        P.dma(Wp[:], wiv[:, :, pc * 256:(pc + 1) * 256], q="pool")
                        for cc in range(2):
                            c = 2 * pc + cc
                            for kc in range(8):
                                P.mm(psA[c // 4][:, (c % 4) * 128:(c % 4 + 1) * 128], Wp[:, kc, cc * 128:(cc + 1) * 128], hT[:, kc, tl],
                                     start=(kc == 0), stop=(kc == 7))
                    for c in range(14):
                        zc = zcb[c % 2]
                        pz = psA[c // 4][:, (c % 4) * 128:(c % 4 + 1) * 128]
                        P.copy(zc[:, 0:1], carry[:, c:c + 1])
                        P.act(zc[:, 1:129], pz, AF.Copy)
                        P.copy(carry[:, c:c + 1], zc[:, 128:129])
                        P.tt(dz[:], zc[:, 0:128], zc[:, 1:129], ALU.subtract)
                        P.stt(zt[:, c, :], dz[:], muT[:, c:c + 1], zc[:, 1:129], ALU.mult, ALU.add)
                    if RW_STAGE < 2:
                        continue
                    P.act(th[0:64, :], zt[0:64, 12, :], AF.Tanh)
                    P.act(adb[64:128, :], zt[64:128, 12, :], AF.Copy)
                    P.act(sgd[:], zt[:, 13, :], AF.Sigmoid)
                    for c4 in range(4):
                        T_ = tset[c4 % NSET]
                        s_, a_, L0, E1, E2, E3, E4 = T_["s_"], T_["a_"], T_["L0"], T_["E1"], T_["E2"], T_["E3"], T_["E4"]
                        kk, bv, tA, kp, rn, kk2, t2b = T_["kk"], T_["bv"], T_["tA"], T_["kp"], T_["rn"], T_["kk2"], T_["t2b"]
                        ld = s_
                        Lx = E3
                        kkn = kk
                        r_ = zt[:, c4, :]
                        k_ = zt[:, 4 + c4, :]
                        v_ = zt[:, 8 + c4, :]
                        cs = slice(c4 * 128, (c4 + 1) * 128)
                        pU = psA[4][:, 0:128]; pAa = psA[5][:, 0:128]; pG = psA[4][:, 256:384]; pN = psA[4][:, 384:512]
                        pBn = psA[5][:, 384:512]
                        P.mm(pU, w2b[0:64, cs], th[0:64, :])
                        P.mm(pAa, a2b[64:128, cs], adb[64:128, :])
                        P.mm(pG, g2b[:, cs], sgd[:])
                        P.act(s_[:], pU, AF.Sigmoid, bias=cols["rwkv_w0"][:, c4:c4 + 1])
                        P.act(a_[:], pAa, AF.Sigmoid, bias=cols["rwkv_a0"][:, c4:c4 + 1])
                        P.act(gb[:, c4, :], pG, AF.Copy)
                        if RW_SUB < 2:
                            continue
                        P.ts(ld[:], s_[:], -0.6065306597126334, None, ALU.mult)
                        P.op("dve", lambda E, o_=L0[:], d_=ld[:]: E.tensor_tensor_scan(o_, ones_b[:], d_, 0.0, ALU.mult, ALU.add),
                             reads=[ones_b[:], ld[:]], writes=[L0[:]])
                        Lc = L0
                        P.act(E1[:], Lc[:], AF.Exp)
                        P.act(E2[:], Lc[:], AF.Exp, scale=-1.0)
                        P.tt(Lx[:], Lc[:], ld[:], ALU.subtract)
                        P.act(E3[:], Lx[:], AF.Exp)
                        P.act(E4[:], Lc[:], AF.Exp, scale=-1.0, bias=Lc[:, 127:128])
                        P.copy(gCt[:, c4:c4 + 1], E1[:, 127:128])
                        if RW_SUB < 3:
                            continue
                        P.ts(kk[:], k_, cols["rwkv_k_k"][:, c4:c4 + 1], None, ALU.mult)
                        P.tt(kk2[:], kk[:], kk[:], ALU.mult)
                        P.mm(pN, bones[:], kk2[:])
                        P.ts(rn[:], pN, 1e-24, None, ALU.max)
                        P.act(rn[:], rn[:], AF.Sqrt)
                        P.recip(rn[:], rn[:])
                        P.tt(kkn[:], kk[:], rn[:], ALU.mult)
                        P.tt(bv[:], kkn[:], a_[:], ALU.mult)
                        if RW_SUB < 4:
                            continue
                        P.ts(tA[:], a_[:], cols["rwkv_k_a"][:, c4:c4 + 1], omk[:, c4:c4 + 1], ALU.mult, ALU.add)
                        P.tt(kp[:], k_, tA[:], ALU.mult)
                        P.tt(tA[:], r_, kp[:], ALU.mult)
                        P.ts(t2b[:], tA[:], cols["rwkv_r_k"][:, c4:c4 + 1], None, ALU.mult)
                        P.mm(pBn, bones[:], t2b[:])
                        P.tt(bonus[:, c4, :], pBn, v_, ALU.mult)
                        if RW_SUB < 5:
                            continue
                        P.stt(AR[:, c4, 0:128], kkn[:], -1.0, E3[:], ALU.mult, ALU.mult)
                        P.tt(AR[:, c4, 128:256], r_, E1[:], ALU.mult)
                        P.tt(Bt[:, c4, :], bv[:], E2[:], ALU.mult)
                        P.tt(Kt[:, c4, :], kp[:], E2[:], ALU.mult)
                        P.tt(Bh[:, c4, :], bv[:], E4[:], ALU.mult)
                        P.tt(Kh[:, c4, :], kp[:], E4[:], ALU.mult)
                        P.copy(vb[:, c4, :], v_, eng="act")
                    if RW_SUB < 6:
                        continue
                    for i3, (srcf, dstt) in enumerate(((vb, VT), (Kh, KhT), (Bh, BhT))):
                        tp = tpb[i3 % 2]
                        for c4 in range(4):
                            P.tr(tp[:, c4 * 128:(c4 + 1) * 128], srcf[:, c4, :], ident[:])
                        P.copy(dstt[:], tp[:, 0:512], eng=("act" if i3 % 2 else "dve"))
                    if RW_STAGE < 3:
                        continue
                    def head_info(w, hi):
                        h = 4 * w + hi
                        c4, e = divmod(h, 2)
                        return h, c4, e, slice(e * 64, (e + 1) * 64)

                    ubc = maskuu[:, :].unsqueeze(1).to_broadcast([128, 2, 256])
                    slbc = masksl[:, :].unsqueeze(1).to_broadcast([128, 4, 128])
                    idbc = ident[:, :].unsqueeze(1).to_broadcast([128, 4, 128])
                    for w in range(2):
                        Pb = [psA[3 * w + 0][:].rearrange("p (a c) -> p a c", a=2), psA[3 * w + 1][:].rearrange("p (a c) -> p a c", a=2)]
                        P2 = psA[3 * w + 2][:].rearrange("p (a c) -> p a c", a=4)
                        for hi in range(4):
                            h, c4, e, pse = head_info(w, hi)
                            P.mm(Pb[e][:, hi // 2, :], Bt[pse, c4, :], AR[pse, c4, :])
                        for e in range(2):
                            P.tt(MB[w][:, e::2, :], Pb[e], ubc, ALU.mult)
                        for hi in (0, 2):
                            h, c4, e, pse = head_info(w, hi)
                            P.mm(P2[:, hi, :], AR[pse, c4, 0:128], Bt[pse, c4, :])
                        for hi in (1, 3, 0, 2):
                            h, c4, e, pse = head_info(w, hi)
                            P.mm(Pb[e][:, hi // 2, :], Kt[pse, c4, :], AR[pse, c4, :])
                        for hi in (1, 3):
                            h, c4, e, pse = head_info(w, hi)
                            P.mm(P2[:, hi, :], AR[pse, c4, 0:128], Bt[pse, c4, :])
                        for e in range(2):
                            P.tt(MKt[w][:, e::2, :], Pb[e], ubc, ALU.mult)
                        P.tt(ATT[w][0][:], P2, slbc, ALU.mult)
                    for j in range(1, 7):
                        for w in range(2):
                            Pb = [psA[3 * w + 0][:].rearrange("p (a c) -> p a c", a=2), psA[3 * w + 1][:].rearrange("p (a c) -> p a c", a=2)]
                            P2 = psA[3 * w + 2][:].rearrange("p (a c) -> p a c", a=4)
                            src_ta = TA[w][(j - 1) % 2]
                            dst_ta = TA[w][j % 2]
                            At = ATT[w][(j - 1) % 2]
                            for hi in range(4):
                                e = hi % 2
                                if j == 1:
                                    P.mm(Pb[e][:, hi // 2, 128:256], At[:, hi, :], MB[w][:, hi, 0:128])
                                    P.mm(P2[:, hi, :], MB[w][:, hi, 0:128], At[:, hi, :])
                                else:
                                    P.mm(Pb[e][:, hi // 2, :], At[:, hi, :], src_ta[:, hi, :])
                                    P.mm(P2[:, hi, :], src_ta[:, hi, 128:256], At[:, hi, :])
                            if j == 1:
                                P.tt(dst_ta[:, :, 0:128], MB[w][:, :, 0:128], idbc, ALU.add)
                            else:
                                for e in range(2):
                                    P.tt(dst_ta[:, e::2, 0:128], Pb[e][:, :, 0:128], src_ta[:, e::2, 0:128], ALU.add)
                            if j < 6:
                                for e in range(2):
                                    P.act(dst_ta[:, e::2, 128:256], Pb[e][:, :, 128:256], AF.Copy)
                            P.act(ATT[w][j % 2][:], P2, AF.Copy)
                    for w in range(2):
                        P2 = psA[3 * w + 2][:].rearrange("p (a c) -> p a c", a=4)
                        for hi in range(4):
                            P.mm(P2[:, hi, :], ATT[w][0][:, hi, :], TA[w][0][:, hi, 0:128])
                        P.tt(Tfw[w][:], P2, TA[w][0][:, :, 0:128], ALU.add)
                    if RW_STAGE < 4:
                        continue
                    W1ps = psA[4][:].rearrange("p (h v) -> p h v", v=64)
                    Ups = psA[5][:].rearrange("p (h v) -> p h v", v=64)
                    Yps = psA[3][:].rearrange("p (h v) -> p h v", v=64)
                    Sps = psA[2][:].rearrange("p (c v) -> p c v", v=128)
                    hinfo = []
                    for h in range(8):
                        c4, e = divmod(h, 2)
                        hinfo.append((h, c4, slice(e * 64, (e + 1) * 64), slice(h * 64, (h + 1) * 64)))
                    for h, c4, pse, hsl in hinfo:
                        P.mm(W1ps[:, h, :], AR[pse, c4, 0:128], Sb[pse, c4, :], start=True, stop=False)
                        P.mm(W1ps[:, h, :], MKt[h // 4][:, h % 4, 0:128], VT[:, hsl], start=False, stop=True)
                    P.act(W1[:], psA[4][:], AF.Copy)
                    for h, c4, pse, hsl in hinfo:
                        P.mm(Ups[:, h, :], Tfw[h // 4][:, h % 4, :], W1[:, hsl])
                    P.copy(UT[:], psA[5][:])
                    for h, c4, pse, hsl in hinfo:
                        P.mm(Yps[:, h, :], AR[pse, c4, 128:256], Sb[pse, c4, :], start=True, stop=False)
                        P.mm(Yps[:, h, :], MB[h // 4][:, h % 4, 128:256], UT[:, hsl], start=False, stop=False)
                        P.mm(Yps[:, h, :], MKt[h // 4][:, h % 4, 128:256], VT[:, hsl], start=False, stop=True)
                    P.act(Ytok[:], psA[3][:], AF.Copy)
                    for c4 in range(4):
                        cs = slice(c4 * 128, (c4 + 1) * 128)
                        P.mm(Sps[:, c4, :], BhT[:, cs], UT[:, cs], start=True, stop=False)
                        P.mm(Sps[:, c4, :], KhT[:, cs], VT[:, cs], start=False, stop=True)
                    for e in range(2):
                        pse = slice(e * 64, (e + 1) * 64)
                        P.tt(Sf[pse, :, :], Sf[pse, :, :], gCt[pse, :].unsqueeze(2).to_broadcast([64, 4, 64]), ALU.mult)
                        P.tt(Sf[pse, :, :], Sf[pse, :, :], Sps[pse, :, e * 64:(e + 1) * 64], ALU.add)
                    P.act(Sb[:], Sf[:], AF.Copy)
                    if RW_STAGE < 5:
                        continue
                    y3 = Ytok[:].rearrange("p (h d) -> p h d", d=64)
                    P.red(sm[:], y3)
                    P.act(yc[:], Ytok[:], AF.Square)
                    P.red(sm2[:], yc[:].rearrange("p (h d) -> p h d", d=64))
                    P.ts(mean[:], sm[:], 1.0 / 64, None, ALU.mult)
                    P.tt(msq[:], mean[:], mean[:], ALU.mult)
                    P.stt(var[:], sm2[:], 1.0 / 64, msq[:], ALU.mult, ALU.subtract)
                    P.ts(var[:], var[:], GN_EPS, None, ALU.add)
                    P.act(var[:], var[:], AF.Sqrt)
                    P.recip(var[:], var[:])
                    P.tt(yc[:].rearrange("p (h d) -> p h d", d=64), y3, mean[:].unsqueeze(2).to_broadcast([128, 8, 64]), ALU.subtract)
                    P.tt(ynb[:].rearrange("p (h d) -> p h d", d=64), yc[:].rearrange("p (h d) -> p h d", d=64),
                         var[:].unsqueeze(2).to_broadcast([128, 8, 64]), ALU.mult)
                    tp = tpb[n % 2]
                    for c4 in range(4):
                        P.tr(tp[:, c4 * 128:(c4 + 1) * 128], ynb[:, c4 * 128:(c4 + 1) * 128], ident[:])
                    for c4 in range(4):
                        yo_ = yo[c4 % 2]
                        P.ts(yo_[:], tp[:, c4 * 128:(c4 + 1) * 128], cols["rwkv_gn_w"][:, c4:c4 + 1], cols["rwkv_gn_b"][:, c4:c4 + 1],
                             ALU.mult, ALU.add)
                        P.tt(yo_[:], yo_[:], bonus[:, c4, :], ALU.add)
                        P.tt(yrT[:, c4, tl], yo_[:], gb[:, c4, :], ALU.mult)
                if dbg is not None:
                    P.dma(dbg["yr"][:, :, :], yrT[:], q="pool")
                P.barrier()
            merge_branch(ctx, hT, yrT, 4, wn["w_br_rwkv"], 4096)

    def mix_phase():
        with _ES() as ctx:
            hT = P.sb([128, 8, S], BF16, "hT", ctx)
            with _ES() as c1:
                norm_to_hT(c1, hT, wn["mix_norm"], tpb)
                P.barrier()
            if MIX_ATTN:
                attn_branch(hT)
            if MIX_RWKV:
                rwkv_branch(hT)
            P.barrier()

    P.barrier()
    if not SKIPFFN:
        ffn_phase(wn["ffn1_norm"], wn["ffn1_w_gate"], wn["ffn1_w_up"], wn["ffn1_w_down"])
    mix_phase()
    if not SKIPFFN:
        ffn_phase(wn["ffn2_norm"], wn["ffn2_w_gate"], wn["ffn2_w_up"], wn["ffn2_w_down"])
    ple_phase()
    P.finish()
    stack.close()
    return nc


_CACHE = {}


import os as _os
DEBUG = bool(_os.environ.get("KDEBUG"))
MIX_ATTN = not _os.environ.get("KNOATTN")
MIX_RWKV = not _os.environ.get("KNORWKV")
SKIPFFN = bool(_os.environ.get("KSKIPFFN"))
ATT_M = int(_os.environ.get("KATT_M", "16"))
ATT_PR = int(_os.environ.get("KATT_PR", "2"))
RWKV_N = int(_os.environ.get("KRWKV_N", "16"))
SKIPMERGE = bool(_os.environ.get("KSKIPMERGE"))
ATT_G = [int(c) for c in _os.environ.get("KATT_G", "012")]
ATT_STAGE = int(_os.environ.get("KATT_STAGE", "9"))
ATT_SUB = int(_os.environ.get("KATT_SUB", "9"))
RW_STAGE = int(_os.environ.get("KRW_STAGE", "9"))
RW_SUB = int(_os.environ.get("KRW_SUB", "9"))


def _consts():
    p = np.arange(128)[:, None]
    c = np.arange(128)[None, :]
    NEG = -30000.0
    onespad = np.zeros((128, 2, 128), np.float32)
    onespad[:, 0, 0:64] = 1.0
    onespad[:, 1, 64:128] = 1.0
    invf = (1.0 / (np.float32(10000.0) ** (np.arange(0, 64, 2, dtype=np.float32) / np.float32(64)))).astype(np.float32)
    bo = np.zeros((128, 128), np.float32)
    bo[0:64, 0:64] = 1.0
    bo[64:128, 64:128] = 1.0
    su = (p < c).astype(np.float32)
    iu = (p <= c).astype(np.float32)
    return {"c_ident": np.eye(128, dtype=np.float32),
            "c_maskc": np.where(p <= c, 0.0, NEG).astype(np.float32),
            "c_maskp": np.where(p >= c, 0.0, NEG).astype(np.float32),
            "c_onespad": onespad,
            "c_invfreq": np.ascontiguousarray(np.broadcast_to(invf[None, :], (128, 32))),
            "c_maskuu": np.concatenate([su, iu], axis=1),
            "c_masksl": (c < p).astype(np.float32),
            "c_blockones": bo}


def kernel(**inputs):
    if "nc" not in _CACHE:
        _CACHE["nc"] = build_program(dbg=({} if DEBUG else None))
    nc = _CACHE["nc"]
    consts = _consts()
    in_maps = []
    for b in range(8):
        m = dict(consts)
        m["x"] = np.ascontiguousarray(inputs["x"][b])
        m["p"] = np.ascontiguousarray(inputs["p"][0, b])
        m["positions"] = np.ascontiguousarray(inputs["positions"][b])
        for k, v in inputs.items():
            if k in ("x", "p", "positions"):
                continue
            a = np.asarray(v)[0]
            m[k] = np.ascontiguousarray(a.reshape(-1) if k == "rwkv_r_k" else a)
        in_maps.append(m)
    res = run_bass_kernel_spmd(nc, in_maps, core_ids=list(range(8)))
    if DEBUG:
        _CACHE["dbg"] = [{k: np.asarray(v) for k, v in r.items() if k.startswith("dbg_")} for r in res.results]
    return np.stack([np.asarray(r["out"]) for r in res.results], axis=0).astype(np.float32)
```

```python
import numpy as np
import concourse.bass as bass
import concourse.mybir as mybir
from concourse.bass_utils import run_bass_kernel_spmd

F32 = mybir.dt.float32
BF16 = mybir.dt.bfloat16
I32 = mybir.dt.int32
AF = mybir.ActivationFunctionType
ALU = mybir.AluOpType
AX = mybir.AxisListType

S = 2048
D = 1024
NT = 16
DFF = 2816
PLE = 256
RMS_EPS = 1e-6


class _Op:
    __slots__ = ("eng", "fn", "dma", "deps", "ddeps", "mark", "cnt", "slot", "val", "idx", "throttle")


class Prog:
    ENGS = ("pe", "act", "dve", "pool", "sync")
    EP = 2000
    NSLOT = 8

    def __init__(self, nc, stack):
        self.nc = nc
        self.stack = stack
        self.ops = []
        self.trk = {}
        self.last = {e: None for e in self.ENGS}
        self.ndma = {e: 0 for e in self.ENGS}
        self.dma_hist = {e: [] for e in self.ENGS}
        self.uid = 0

    def sb(self, shape, dt, name=None, ctx=None):
        self.uid += 1
        nm = f"{name or 't'}_{self.uid}"
        return (ctx or self.stack).enter_context(self.nc.sbuf_tensor(nm, list(shape), dt))

    def ps(self, shape, dt, name=None, ctx=None):
        self.uid += 1
        nm = f"{name or 'p'}_{self.uid}"
        return (ctx or self.stack).enter_context(self.nc.psum_tensor(nm, list(shape), dt))

    @staticmethod
    def _rect(ap):
        a = ap.ap
        off = int(ap.offset)
        pstep = int(a[0][0])
        npart = int(a[0][1])
        if pstep == 0:
            pstep = 1 << 40
        p0 = off // pstep
        f0 = off % pstep
        ext = 1
        for s, c in a[1:]:
            ext += (int(c) - 1) * abs(int(s))
        if "PSUM" in str(ap.space).upper():
            return ap.tensor.name, ((p0 // 32) * 32, ((p0 + npart + 31) // 32) * 32, 0, 1 << 30)
        return ap.tensor.name, (p0, p0 + npart, f0, f0 + ext)

    @staticmethod
    def _ov(a, b):
        return a[0] < b[1] and b[0] < a[1] and a[2] < b[3] and b[2] < a[3]

    @staticmethod
    def _inside(a, b):
        return a[0] >= b[0] and a[1] <= b[1] and a[2] >= b[2] and a[3] <= b[3]

    def _is_tracked(self, ap):
        sp = str(ap.space)
        return ("SB" in sp.upper()) or ("PSUM" in sp.upper())

    def op(self, eng, fn, reads=(), writes=(), dma=False):
        o = _Op()
        o.eng = eng
        o.fn = fn
        o.dma = dma
        o.deps = {}
        o.ddeps = []
        o.mark = False
        o.cnt = 0
        o.throttle = None
        prods = []
        for ap in reads:
            if ap is None or isinstance(ap, (int, float)) or not self._is_tracked(ap):
                continue
            nm, r = self._rect(ap)
            ents = self.trk.setdefault(nm, [])
            is_ps = "PSUM" in str(ap.space).upper()
            for e in ents:
                if self._ov(e[0], r):
                    if e[1] is not None:
                        prods.append(e[1])
                    if is_ps:
                        for rd in e[2]:
                            if rd.eng != eng:
                                prods.append(rd)
                    e[2].append(o)
        for ap in writes:
            if ap is None or not self._is_tracked(ap):
                continue
            nm, r = self._rect(ap)
            ents = self.trk.setdefault(nm, [])
            keep = []
            for e in ents:
                if self._ov(e[0], r):
                    if e[1] is not None:
                        prods.append(e[1])
                    prods.extend(e[2])
                    if self._inside(e[0], r):
                        continue
                keep.append(e)
            keep.append([r, o, []])
            self.trk[nm] = keep
        for p in prods:
            if p is o:
                continue
            if p.dma:
                if p not in o.ddeps:
                    o.ddeps.append(p)
            else:
                if p.eng == "pe" and eng == "pe" and not dma:
                    continue
                cur = o.deps.get(p.eng)
                if cur is None or p.idx > cur.idx:
                    o.deps[p.eng] = p
        if dma:
            i = self.ndma[eng]
            self.ndma[eng] = i + 1
            o.slot = i % self.NSLOT
            o.val = 16 * (i // self.NSLOT + 1)
            if i >= self.NSLOT:
                o.throttle = self.dma_hist[eng][i - self.NSLOT]
            self.dma_hist[eng].append(o)
        else:
            self.last[eng] = o
        o.idx = len(self.ops)
        self.ops.append(o)
        return o

    def barrier(self):
        lasts = [self.last[e] for e in self.ENGS if self.last[e] is not None]
        dm = []
        for e in self.ENGS:
            dm.extend(self.dma_hist[e][-self.NSLOT:])
        for e in self.ENGS:
            o = _Op()
            o.eng = e
            o.fn = None
            o.dma = False
            o.deps = {}
            for p in lasts:
                if p.eng == e and e == "pe":
                    continue
                o.deps[p.eng] = p
            o.ddeps = list(dm)
            o.mark = False
            o.cnt = 0
            o.throttle = None
            o.idx = len(self.ops)
            self.ops.append(o)
        self.trk = {}
        self.last = {e: None for e in self.ENGS}
        self.flush()

    def _csem(self, e, b):
        lst = self.csem.setdefault(e, [])
        while len(lst) <= b:
            lst.append(self.stack.enter_context(self.nc.semaphore(f"c_{e}_{len(lst)}")))
        return lst[b]

    def _dsem(self, e, s):
        if e not in self.dsem:
            self.dsem[e] = [self.stack.enter_context(self.nc.semaphore(f"d_{e}_{i}")) for i in range(self.NSLOT)]
        return self.dsem[e][s]

    def _wait_dma(self, E, e, d):
        key = (d.eng, d.slot)
        if self.dwaited[e].get(key, 0) < d.val:
            E.wait_ge(self._dsem(d.eng, d.slot), d.val)
            self.dwaited[e][key] = d.val

    def flush(self):
        nc = self.nc
        if not hasattr(self, "counts"):
            self.counts = {e: 0 for e in self.ENGS}
            self.csem = {}
            self.dsem = {}
            self.waited = {e: {f: 0 for f in self.ENGS} for e in self.ENGS}
            self.dwaited = {e: {} for e in self.ENGS}
            self.pos = 0
        engobj = {"pe": nc.tensor, "act": nc.scalar, "dve": nc.vector, "pool": nc.gpsimd, "sync": nc.sync}
        batch = self.ops[self.pos:]
        for o in batch:
            for p in o.deps.values():
                assert p.idx >= self.pos, "dependency on an already-flushed op"
                p.mark = True
        for o in batch:
            if o.fn is not None and not o.dma and o.mark:
                self.counts[o.eng] += 1
                o.cnt = self.counts[o.eng]
        for o in batch:
            e = o.eng
            E = engobj[e]
            for f, p in o.deps.items():
                if p.cnt > self.waited[e][f]:
                    b = (p.cnt - 1) // self.EP
                    E.wait_ge(self._csem(f, b), (p.cnt - 1) % self.EP + 1)
                    self.waited[e][f] = p.cnt
            for d in o.ddeps:
                self._wait_dma(E, e, d)
            if o.throttle is not None:
                self._wait_dma(E, e, o.throttle)
            if o.fn is None:
                continue
            ins = o.fn(E)
            if o.dma:
                ins.then_inc(self._dsem(e, o.slot), 16)
            elif o.mark:
                b = (o.cnt - 1) // self.EP
                ins.then_inc(self._csem(e, b), 1)
            o.fn = None
        self.pos = len(self.ops)

    def finish(self):
        self.flush()
        E = self.nc.sync
        for q in self.ENGS:
            for d in self.dma_hist[q][-self.NSLOT:]:
                self._wait_dma(E, "sync", d)

    def mm(self, out, lhsT, rhs, start=True, stop=True):
        return self.op("pe", lambda E: E.matmul(out, lhsT, rhs, start=start, stop=stop),
                       reads=[lhsT, rhs], writes=[out])

    def tr(self, out, in_, ident):
        return self.op("pe", lambda E: E.transpose(out, in_, ident), reads=[in_, ident], writes=[out])

    def act(self, out, in_, func, bias=None, scale=None, accum_out=None, eng="act"):
        kw = {}
        rd = [in_]
        if bias is not None:
            kw["bias"] = bias
            rd.append(bias)
        if scale is not None:
            kw["scale"] = scale
            rd.append(scale)
        wr = [out]
        if accum_out is not None:
            kw["accum_out"] = accum_out
            wr.append(accum_out)
        return self.op(eng, lambda E: E.activation(out, in_, func, **kw), reads=rd, writes=wr)

    def tt(self, out, in0, in1, op, eng="dve"):
        return self.op(eng, lambda E: E.tensor_tensor(out, in0, in1, op), reads=[in0, in1], writes=[out])

    def ts(self, out, in0, s1, s2, op0, op1=None, eng="dve", accum_out=None):
        rd = [in0, s1, s2]
        wr = [out]
        if accum_out is not None:
            wr.append(accum_out)

        def f(E):
            kw = {}
            if accum_out is not None:
                kw["accum_out"] = accum_out
            if op1 is None:
                return E.tensor_scalar(out, in0, s1, None, op0, **kw)
            return E.tensor_scalar(out, in0, s1, s2, op0, op1, **kw)
        return self.op(eng, f, reads=rd, writes=wr)

    def stt(self, out, in0, scalar, in1, op0, op1, eng="dve"):
        return self.op(eng, lambda E: E.scalar_tensor_tensor(out, in0, scalar, in1, op0, op1),
                       reads=[in0, scalar, in1], writes=[out])

    def copy(self, out, in_, eng="dve"):
        if eng == "act":
            return self.op("act", lambda E: E.copy(out, in_), reads=[in_], writes=[out])
        return self.op(eng, lambda E: E.tensor_copy(out, in_), reads=[in_], writes=[out])

    def red(self, out, in_, op=None, eng="dve"):
        op = op or ALU.add
        return self.op(eng, lambda E: E.tensor_reduce(out, in_, AX.X, op), reads=[in_], writes=[out])

    def recip(self, out, in_):
        return self.op("dve", lambda E: E.reciprocal(out, in_), reads=[in_], writes=[out])

    def memset(self, ap, val, eng="dve"):
        return self.op(eng, lambda E: E.memset(ap, val), reads=[], writes=[ap])

    def dma(self, out, in_, q="sync", slow=False):
        kw = {}
        if slow:
            kw["allow_slow_non_contiguous"] = True
        return self.op(q, lambda E: E.dma_start(out=out, in_=in_, **kw), reads=[in_], writes=[out], dma=True)


FF_SPLITS = (4, 4, 4, 4, 3, 3)


def build_program(dbg=None):
    from contextlib import ExitStack
    nc = bass.Bass("TRN2", target_bir_lowering=False)
    stack = ExitStack()
    P = Prog(nc, stack)

    def din(name, shape, dt=F32):
        return nc.dram_tensor(name, list(shape), dt, kind="ExternalInput").ap()

    x_d = din("x", [S, D])
    p_d = din("p", [S, PLE])
    pos_d = din("positions", [S], I32)
    wn = {}
    for nm, shp in (("ffn1_norm", [D]), ("ffn1_w_gate", [D, DFF]), ("ffn1_w_up", [D, DFF]), ("ffn1_w_down", [DFF, D]),
                    ("mix_norm", [D]), ("w_in", [D, 6144]), ("rwkv_mu", [1792]), ("rwkv_w0", [512]),
                    ("rwkv_w2", [64, 512]), ("rwkv_a0", [512]), ("rwkv_a2", [64, 512]), ("rwkv_g2", [128, 512]),
                    ("rwkv_k_k", [512]), ("rwkv_k_a", [512]), ("rwkv_r_k", [512]), ("rwkv_gn_w", [512]),
                    ("rwkv_gn_b", [512]), ("q_norm", [64]), ("k_norm", [64]), ("w_br_rwkv", [512, D]),
                    ("w_br_attn", [256, D]), ("w_out", [D, D]), ("ffn2_norm", [D]), ("ffn2_w_gate", [D, DFF]),
                    ("ffn2_w_up", [D, DFF]), ("ffn2_w_down", [DFF, D]), ("ple_norm", [D]), ("ple_w_gate", [D, D]),
                    ("ple_w_proj", [PLE, D])):
        wn[nm] = din(nm, shp)
    ident_d = din("c_ident", [128, 128])
    cst = {}
    for nm, shp in (("c_maskc", [128, 128]), ("c_maskp", [128, 128]), ("c_onespad", [128, 2, 128]),
                    ("c_invfreq", [128, 32]), ("c_maskuu", [128, 256]), ("c_masksl", [128, 128]),
                    ("c_blockones", [128, 128])):
        cst[nm] = din(nm, shp)
    if dbg is not None:
        dbg = {"ya": nc.dram_tensor("dbg_ya", [128, 2, S], F32, kind="ExternalOutput").ap(),
               "yr": nc.dram_tensor("dbg_yr", [128, 4, S], F32, kind="ExternalOutput").ap()}
    out_d = nc.dram_tensor("out", [S, D], F32, kind="ExternalOutput").ap()

    X = P.sb([128, NT, D], F32, "X")
    ident = P.sb([128, 128], BF16, "ident")
    P.dma(ident[:], ident_d[:, :], q="pool")
    xv = x_d.rearrange("(t p) d -> p t d", p=128)
    for i in range(4):
        P.dma(X[:, 4 * i:4 * i + 4, :], xv[:, 4 * i:4 * i + 4, :], q="sync")

    from contextlib import ExitStack as _ES

    def norm_to_hT(ctx, hT, gain_d, tp_banks):
        gain_bc = P.sb([128, D], F32, "gain_bc", ctx)
        P.dma(gain_bc[:], gain_d.partition_broadcast(128), q="sync")
        ss = P.sb([128, NT], F32, "ss", ctx)
        rstd = P.sb([128, NT], F32, "rstd", ctx)
        junk = P.sb([128, D], BF16, "junk", ctx)
        xn = [P.sb([128, D], BF16, "xn", ctx) for _ in range(2)]
        P.memset(ss[:], 0.0)
        for t in range(NT):
            P.act(junk[:], X[:, t, :], AF.Square, accum_out=ss[:, t:t + 1])
        P.ts(rstd[:], ss[:], 1.0 / D, RMS_EPS, ALU.mult, ALU.add)
        P.act(rstd[:], rstd[:], AF.Sqrt)
        P.recip(rstd[:], rstd[:])
        for t in range(NT):
            xt = xn[t % 2]
            P.stt(xt[:], X[:, t, :], rstd[:, t:t + 1], gain_bc[:], ALU.mult, ALU.mult)
            pt = tp_banks[t % 2]
            for kc in range(8):
                P.tr(pt[:, kc * 128:(kc + 1) * 128], xt[:, kc * 128:(kc + 1) * 128], ident[:])
            P.copy(hT[:, :, t * 128:(t + 1) * 128], pt[:].rearrange("p (k c) -> p k c", k=8),
                   eng=("act" if t % 2 else "dve"))

    def ffn(ctx, hT, wg_d, wu_d, wd_d, psum):
        wgv = wg_d.rearrange("(k p) c -> p k c", p=128)
        wuv = wu_d.rearrange("(k p) c -> p k c", p=128)
        wdv = wd_d.rearrange("(f p) d -> p f d", p=128)
        Wg = [P.sb([128, 8, 512], BF16, "Wg", ctx) for _ in range(2)]
        Wu = [P.sb([128, 8, 512], BF16, "Wu", ctx) for _ in range(2)]
        Wd = [P.sb([128, 4, D], BF16, "Wd", ctx) for _ in range(2)]
        aT = [P.sb([128, 4, S], BF16, "aT", ctx) for _ in range(2)]
        sg = [P.sb([128, 512], BF16, "sg", ctx) for _ in range(2)]
        pg, pu, pd = psum["g"], psum["u"], psum["d"]
        offs = []
        f0 = 0
        for nf in FF_SPLITS:
            offs.append((f0, nf))
            f0 += nf
        cnt = {"gu": 0, "d": 0}

        def load(si):
            f0, nf = offs[si]
            sl = si % 2
            c0, c1 = f0 * 128, (f0 + nf) * 128
            P.dma(Wg[sl][:, :, 0:nf * 128], wgv[:, :, c0:c1], q="pool")
            P.dma(Wu[sl][:, :, 0:nf * 128], wuv[:, :, c0:c1], q="pool")
            P.dma(Wd[sl][:, 0:nf, :], wdv[:, f0:f0 + nf, :], q="pool")

        def gu(si):
            f0, nf = offs[si]
            sl = si % 2
            for fl in range(nf):
                for tg in range(4):
                    i = cnt["gu"] % 2
                    cnt["gu"] += 1
                    for kc in range(8):
                        P.mm(pg[i][:], Wg[sl][:, kc, fl * 128:(fl + 1) * 128], hT[:, kc, tg * 512:(tg + 1) * 512],
                             start=(kc == 0), stop=(kc == 7))
                    for kc in range(8):
                        P.mm(pu[i][:], Wu[sl][:, kc, fl * 128:(fl + 1) * 128], hT[:, kc, tg * 512:(tg + 1) * 512],
                             start=(kc == 0), stop=(kc == 7))
                    P.act(sg[i][:], pg[i][:], AF.Silu)
                    P.tt(aT[sl][:, fl, tg * 512:(tg + 1) * 512], sg[i][:], pu[i][:], ALU.mult)

        def down(si):
            f0, nf = offs[si]
            sl = si % 2
            for t in range(NT):
                for dh in range(2):
                    i = cnt["d"] % 2
                    cnt["d"] += 1
                    for fl in range(nf):
                        P.mm(pd[i][:], aT[sl][:, fl, t * 128:(t + 1) * 128], Wd[sl][:, fl, dh * 512:(dh + 1) * 512],
                             start=(fl == 0), stop=(fl == nf - 1))
                    xs = X[:, t, dh * 512:(dh + 1) * 512]
                    P.stt(xs, pd[i][:], 0.5, xs, ALU.mult, ALU.add)

        ns = len(offs)
        load(0)
        load(1)
        gu(0)
        for si in range(1, ns):
            gu(si)
            down(si - 1)
            if si + 1 < ns:
                load(si + 1)
        down(ns - 1)

    psA = [P.ps([128, 512], F32, "psA") for _ in range(6)]
    tpb = [P.ps([128, 1024], BF16, "tpb") for _ in range(2)]
    psum_ffn = {"g": psA[0:2], "u": psA[2:4], "d": psA[4:6]}

    def ffn_phase(norm_d, wg_d, wu_d, wd_d):
        with _ES() as ctx:
            hT = P.sb([128, 8, S], BF16, "hT", ctx)
            norm_to_hT(ctx, hT, norm_d, tpb)
            ffn(ctx, hT, wg_d, wu_d, wd_d, psum_ffn)
            P.barrier()

    def ple_phase():
        with _ES() as ctx:
            hT = P.sb([128, 8, S], BF16, "hT", ctx)
            Wpg = P.sb([128, 8, D], BF16, "Wpg", ctx)
            Wpp = P.sb([128, 2, D], BF16, "Wpp", ctx)
            P.dma(Wpg[:], wn["ple_w_gate"].rearrange("(k p) c -> p k c", p=128), q="pool")
            P.dma(Wpp[:], wn["ple_w_proj"].rearrange("(k p) c -> p k c", p=128), q="pool")
            pin = P.sb([128, NT, PLE], BF16, "pin", ctx)
            P.dma(pin[:], p_d.rearrange("(t p) c -> p t c", p=128), q="pool")
            pT = P.sb([128, 2, S], BF16, "pT", ctx)
            norm_to_hT(ctx, hT, wn["ple_norm"], tpb)
            for t in range(NT):
                pt = tpb[t % 2]
                for k in range(2):
                    P.tr(pt[:, k * 128:(k + 1) * 128], pin[:, t, k * 128:(k + 1) * 128], ident[:])
                P.copy(pT[:, :, t * 128:(t + 1) * 128], pt[:, 0:256].rearrange("p (k c) -> p k c", k=2),
                       eng=("act" if t % 2 else "dve"))
            sgs = [P.sb([128, 512], F32, "sgs", ctx) for _ in range(2)]
            n = 0
            ov = out_d.rearrange("(t p) d -> p t d", p=128)
            for t in range(NT):
                for dh in range(2):
                    i = n % 2
                    n += 1
                    pG, pP = psA[i], psA[2 + i]
                    for kc in range(8):
                        P.mm(pG[:], hT[:, kc, t * 128:(t + 1) * 128], Wpg[:, kc, dh * 512:(dh + 1) * 512],
                             start=(kc == 0), stop=(kc == 7))
                    for k in range(2):
                        P.mm(pP[:], pT[:, k, t * 128:(t + 1) * 128], Wpp[:, k, dh * 512:(dh + 1) * 512],
                             start=(k == 0), stop=(k == 1))
                    P.act(sgs[i][:], pG[:], AF.Sigmoid)
                    P.tt(sgs[i][:], sgs[i][:], pP[:], ALU.mult)
                    xs = X[:, t, dh * 512:(dh + 1) * 512]
                    P.tt(xs, xs, sgs[i][:], ALU.add)
                P.dma(ov[:, t, :], X[:, t, :], q="sync")
            P.barrier()


    TWO_PI = 6.283185307179586
    C1 = 6.28125
    C2 = TWO_PI - C1
    PI = 3.141592653589793

    def tokslice(hT, kc, t0, d):
        if d == 1:
            return hT[:, kc, t0:t0 + 128]
        return hT[:, kc, t0:t0 + 127 * d + 1:d]

    def merge_branch(ctx0, hT, yT, nyc, wbr_d, gate_col0):
        if SKIPMERGE:
            return
        with _ES() as ctx:
            Wbr = P.sb([128, nyc, D], BF16, "Wbr", ctx)
            P.dma(Wbr[:], wbr_d.rearrange("(c p) d -> p c d", p=128), q="pool")
            Wg = P.sb([128, 8, D], BF16, "Wgt", ctx)
            wiv = wn["w_in"].rearrange("(k p) c -> p k c", p=128)
            P.dma(Wg[:], wiv[:, :, gate_col0:gate_col0 + D], q="pool")
            Wo = P.sb([128, 8, D], BF16, "Wo", ctx)
            P.dma(Wo[:], wn["w_out"].rearrange("(k p) c -> p k c", p=128), q="pool")
            mT = P.sb([128, 8, S], BF16, "mT", ctx)
            sgm = [P.sb([128, 512], F32, "sgm", ctx) for _ in range(2)]
            n = 0
            for dc in range(8):
                for tg in range(4):
                    i = n % 2
                    n += 1
                    pG, pB = psA[i], psA[2 + i]
                    ts_ = slice(tg * 512, (tg + 1) * 512)
                    for kc in range(8):
                        P.mm(pG[:], Wg[:, kc, dc * 128:(dc + 1) * 128], hT[:, kc, ts_], start=(kc == 0), stop=(kc == 7))
                    for c in range(nyc):
                        P.mm(pB[:], Wbr[:, c, dc * 128:(dc + 1) * 128], yT[:, c, ts_], start=(c == 0), stop=(c == nyc - 1))
                    P.act(sgm[i][:], pG[:], AF.Sigmoid)
                    P.tt(mT[:, dc, ts_], sgm[i][:], pB[:], ALU.mult)
            n = 0
            for t in range(NT):
                for dh in range(2):
                    pd_ = psA[4 + n % 2]
                    n += 1
                    for dc in range(8):
                        P.mm(pd_[:], mT[:, dc, t * 128:(t + 1) * 128], Wo[:, dc, dh * 512:(dh + 1) * 512],
                             start=(dc == 0), stop=(dc == 7))
                    xs = X[:, t, dh * 512:(dh + 1) * 512]
                    P.tt(xs, xs, pd_[:], ALU.add)
            P.barrier()

    def attn_branch(hT):
        with _ES() as ctx:
            yaT = P.sb([128, 2, S], BF16, "yaT", ctx)
            with _ES() as c2:
                maskC = P.sb([128, 128], BF16, "maskC", c2)
                maskP = P.sb([128, 128], BF16, "maskP", c2)
                onesp = P.sb([128, 2, 128], BF16, "onesp", c2)
                P.dma(maskC[:], cst["c_maskc"][:, :], q="pool")
                P.dma(maskP[:], cst["c_maskp"][:, :], q="pool")
                P.dma(onesp[:], cst["c_onespad"][:, :, :], q="pool")
                invf = P.sb([128, 32], F32, "invf", c2)
                P.dma(invf[:], cst["c_invfreq"][:, :], q="sync")
                gain4 = P.sb([128, 4, 64], F32, "gain4", c2)
                for hh in range(4):
                    P.dma(gain4[:, hh, :], (wn["q_norm"] if hh < 2 else wn["k_norm"]).partition_broadcast(128), q="sync")
                cosT = [P.sb([128, 16 * 32], F32, "cosT", c2) for _ in range(3)]
                sinT = [P.sb([128, 16 * 32], F32, "sinT", c2) for _ in range(3)]
                with _ES() as c3:
                    posi = P.sb([128, 16], I32, "posi", c3)
                    posf = P.sb([128, 16], F32, "posf", c3)
                    ang = P.sb([128, 512], F32, "ang", c3)
                    aa = P.sb([128, 512], F32, "aa", c3)
                    kf = P.sb([128, 512], F32, "kf", c3)
                    ki = P.sb([128, 512], I32, "ki", c3)
                    rr = P.sb([128, 512], F32, "rr", c3)
                    mk = P.sb([128, 512], F32, "mk", c3)
                    for g, d in enumerate((1, 4, 16)):
                        pv = pos_d.rearrange("(ib p j) -> p j ib", p=128, j=d)
                        nib = 16 // d
                        for m in range(16):
                            P.dma(posi[:, m:m + 1], pv[:, m // nib, (m % nib):(m % nib) + 1], q="sync", slow=True)
                        P.copy(posf[:], posi[:])
                        for m in range(16):
                            P.ts(ang[:, m * 32:(m + 1) * 32], invf[:], posf[:, m:m + 1], None, ALU.mult)
                        for dst, shift in ((sinT[g], 0.0), (cosT[g], PI / 2)):
                            P.ts(aa[:], ang[:], shift, None, ALU.add)
                            P.ts(kf[:], aa[:], 1.0 / TWO_PI, None, ALU.mult)
                            P.copy(ki[:], kf[:])
                            P.copy(kf[:], ki[:])
                            P.stt(rr[:], kf[:], -C1, aa[:], ALU.mult, ALU.add)
                            P.stt(rr[:], kf[:], -C2, rr[:], ALU.mult, ALU.add)
                            P.ts(mk[:], rr[:], PI, None, ALU.is_gt)
                            P.stt(rr[:], mk[:], -TWO_PI, rr[:], ALU.mult, ALU.add)
                            P.ts(mk[:], rr[:], -PI, None, ALU.is_lt)
                            P.stt(rr[:], mk[:], TWO_PI, rr[:], ALU.mult, ALU.add)
                            P.act(dst[:], rr[:], AF.Sin)
                    P.barrier()
                Wq = [[P.sb([128, 8, 128], BF16, "Wq", c2) for _ in range(3)] for _ in range(2)]
                qkT = P.sb([128, 2, S], BF16, "qkT", c2)
                Vpad = P.sb([128, 16, 2, 128], BF16, "Vpad", c2)
                acc = P.sb([128, 2, S], F32, "acc", c2)
                qkb = P.sb([128, 4, 256], F32, "qkb", c2)
                sqb = P.sb([128, 4, 256], F32, "sqb", c2)
                qnb = P.sb([128, 4, 256], F32, "qnb", c2)
                ss16 = P.sb([128, 16], F32, "ss16", c2)
                t1 = P.sb([128, 512], F32, "t1", c2)
                t2 = P.sb([128, 512], F32, "t2", c2)
                t3 = P.sb([128, 512], F32, "t3", c2)
                t4 = P.sb([128, 512], F32, "t4", c2)
                qrb = P.sb([128, 4, 256], BF16, "qrb", c2)
                PT = [P.sb([128, 4, 128], BF16, "PT", c2) for _ in range(2)]
                if _os.environ.get('KPRINTMEM'):
                    print('ATTN sbuf remaining', nc.sbuf_bytes_remaining)
                P.memset(Vpad[:], 0.0)
                wiv = wn["w_in"].rearrange("(k p) c -> p k c", p=128)
                it = 0
                for pr in range(ATT_PR):
                    for g, d in enumerate((1, 4, 16)):
                        if g not in ATT_G:
                            continue
                        L = S // d
                        W = Wq[it % 2]
                        it += 1
                        qc0 = 1792 + (g * 4 + pr * 2) * 64
                        for j3 in range(3):
                            P.dma(W[j3][:], wiv[:, :, qc0 + 768 * j3:qc0 + 768 * j3 + 128], q="pool")
                        TB = 4
                        for mb in range(0, 16, TB):
                            for mi in range(TB):
                                m = mb + mi
                                j, i0 = divmod(m * 128, L)
                                t0 = i0 * d + j
                                bank = psA[3 + m % 2]
                                for j3 in range(3):
                                    for kc in range(8):
                                        P.mm(bank[:, j3 * 128:(j3 + 1) * 128], tokslice(hT, kc, t0, d), W[j3][:, kc, :],
                                             start=(kc == 0), stop=(kc == 7))
                                P.act(qkb[:, mi, :], bank[:, 0:256], AF.Copy)
                                P.copy(Vpad[:, m, 0, 0:64], bank[:, 256:320])
                                P.copy(Vpad[:, m, 1, 64:128], bank[:, 320:384])
                            P.tt(sqb[:], qkb[:], qkb[:], ALU.mult)
                            P.red(ss16[:], sqb[:].rearrange("p a (h d) -> p (a h) d", d=64))
                            P.ts(ss16[:], ss16[:], 1.0 / 64, RMS_EPS, ALU.mult, ALU.add)
                            P.act(ss16[:], ss16[:], AF.Sqrt)
                            P.recip(ss16[:], ss16[:])
                            P.tt(qnb[:].rearrange("p a (h d) -> p (a h) d", d=64), qkb[:].rearrange("p a (h d) -> p (a h) d", d=64),
                                 ss16[:].unsqueeze(2).to_broadcast([128, 16, 64]), ALU.mult)
                            P.tt(qnb[:], qnb[:], gain4[:].rearrange("p h d -> p (h d)").unsqueeze(1).to_broadcast([128, TB, 256]), ALU.mult)
                            q5 = qnb[:].rearrange("p a (h t i) -> p a h t i", t=2, i=32)
                            x1, x2 = q5[:, :, :, 0, :], q5[:, :, :, 1, :]
                            cb = cosT[g][:, mb * 32:(mb + TB) * 32].rearrange("p (a i) -> p a i", i=32).unsqueeze(2).to_broadcast([128, TB, 4, 32])
                            sb_ = sinT[g][:, mb * 32:(mb + TB) * 32].rearrange("p (a i) -> p a i", i=32).unsqueeze(2).to_broadcast([128, TB, 4, 32])
                            v4 = lambda tl: tl[:].rearrange("p (a h i) -> p a h i", h=4, i=32)
                            P.tt(v4(t1), x1, cb, ALU.mult)
                            P.tt(v4(t2), x2, sb_, ALU.mult)
                            P.tt(v4(t3), x2, cb, ALU.mult)
                            P.tt(v4(t4), x1, sb_, ALU.mult)
                            r5 = qrb[:].rearrange("p a (h t i) -> p a h t i", t=2, i=32)
                            P.tt(r5[:, :, :, 0, :], v4(t1), v4(t2), ALU.subtract)
                            P.tt(r5[:, :, :, 1, :], v4(t3), v4(t4), ALU.add)
                            tp = tpb[(mb // TB) % 2]
                            for mi in range(TB):
                                for a in range(2):
                                    P.tr(tp[:, a * 512 + mi * 128:a * 512 + (mi + 1) * 128], qrb[:, mi, a * 128:(a + 1) * 128], ident[:])
                            P.copy(qkT[:, :, mb * 128:(mb + TB) * 128], tp[:].rearrange("p (a c) -> p a c", a=2), eng="act")
                        for m in range(ATT_M if ATT_STAGE >= 2 else 0):
                            j, i0 = divmod(m * 128, L)
                            t0 = i0 * d + j
                            has_prev = (i0 // 128) > 0
                            sc = psA[m % 2][:].rearrange("p (a c) -> p a c", a=4)
                            qs = slice(m * 128, (m + 1) * 128)
                            for e in range(2):
                                pse = slice(e * 64, (e + 1) * 64)
                                P.mm(sc[:, e, :], qkT[pse, 1, qs], qkT[pse, 0, qs], start=True, stop=False)
                                P.mm(sc[:, e, :], ident[:], maskC[:], start=False, stop=True)
                                if has_prev:
                                    P.mm(sc[:, 2 + e, :], qkT[pse, 1, (m - 1) * 128:m * 128], qkT[pse, 0, qs], start=True, stop=False)
                                    P.mm(sc[:, 2 + e, :], ident[:], maskP[:], start=False, stop=True)
                            nk = 4 if has_prev else 2
                            pt_ = PT[m % 2]
                            P.act(pt_[:, 0:nk, :], sc[:, 0:nk, :], AF.Exp, scale=0.125)
                            OL = psA[2][:, (m % 2) * 256:(m % 2) * 256 + 256].rearrange("p (a c) -> p a c", a=2)
                            seq = [(kb, e) for kb in range(nk // 2) for e in range(2)]
                            for ii, (kb, e) in enumerate(seq):
                                P.mm(OL[:, 0, :], Vpad[:, m - kb, e, :], pt_[:, kb * 2 + e, :], start=(ii == 0), stop=(ii == len(seq) - 1))
                            for ii, (kb, e) in enumerate(seq):
                                P.mm(OL[:, 1, :], onesp[:, e, :], pt_[:, kb * 2 + e, :], start=(ii == 0), stop=(ii == len(seq) - 1))
                            if d == 1:
                                av = acc[:, :, t0:t0 + 128]
                            else:
                                av = acc[:, :, t0:t0 + 127 * d + 1:d]
                            if g == 0:
                                P.copy(av, OL)
                            else:
                                P.tt(av, av, OL, ALU.add)
                    P.recip(acc[:, 1, :], acc[:, 1, :])
                    P.tt(yaT[:, pr, :], acc[:, 0, :], acc[:, 1, :], ALU.mult)
                if dbg is not None:
                    P.dma(dbg["ya"][:, :, :], yaT[:], q="pool")
                P.barrier()
            merge_branch(ctx, hT, yaT, 2, wn["w_br_attn"], 4096 + 1024)

    def rwkv_branch(hT):
        with _ES() as ctx:
            yrT = P.sb([128, 4, S], BF16, "yrT", ctx)
            with _ES() as c2:
                sbt = lambda shp, dt, nm: P.sb(shp, dt, nm, c2)
                maskuu = sbt([128, 256], BF16, "maskuu")
                masksl = sbt([128, 128], BF16, "masksl")
                bones = sbt([128, 128], BF16, "bones")
                P.dma(maskuu[:], cst["c_maskuu"][:, :], q="pool")
                P.dma(masksl[:], cst["c_masksl"][:, :], q="pool")
                P.dma(bones[:], cst["c_blockones"][:, :], q="pool")
                muT = sbt([128, 14], F32, "muT")
                for c in range(14):
                    P.dma(muT[:, c:c + 1], wn["rwkv_mu"][c * 128:(c + 1) * 128].rearrange("(p o) -> p o", o=1), q="sync", slow=True)
                cols = {}
                for nm in ("rwkv_w0", "rwkv_a0", "rwkv_k_k", "rwkv_k_a", "rwkv_r_k", "rwkv_gn_w", "rwkv_gn_b"):
                    cols[nm] = sbt([128, 4], F32, nm)
                    for c in range(4):
                        P.dma(cols[nm][:, c:c + 1], wn[nm][c * 128:(c + 1) * 128].rearrange("(p o) -> p o", o=1), q="sync", slow=True)
                omk = sbt([128, 4], F32, "omk")
                P.ts(omk[:], cols["rwkv_k_a"][:], -1.0, 1.0, ALU.mult, ALU.add)
                w2b = sbt([128, 512], BF16, "w2b")
                a2b = sbt([128, 512], BF16, "a2b")
                g2b = sbt([128, 512], BF16, "g2b")
                P.dma(w2b[0:64, :], wn["rwkv_w2"][:, :], q="pool")
                P.dma(a2b[64:128, :], wn["rwkv_a2"][:, :], q="pool")
                P.dma(g2b[:], wn["rwkv_g2"][:, :], q="pool")
                Wst = [sbt([128, 8, 256], BF16, "Wst") for _ in range(2)]
                zt = sbt([128, 14, 128], F32, "zt")
                zcb = [sbt([128, 129], F32, "zc") for _ in range(2)]
                carry = sbt([128, 14], F32, "carry")
                P.memset(carry[:], 0.0)
                ones_b = sbt([128, 128], BF16, "ones_b")
                P.memset(ones_b[:], 1.0)
                dz = sbt([128, 128], F32, "dz")
                th = sbt([128, 128], BF16, "th")
                adb = sbt([128, 128], BF16, "adb")
                sgd = sbt([128, 128], BF16, "sgd")
                f32t = lambda nm: sbt([128, 128], F32, nm)
                NSET = 4
                tset = []
                for _i in range(NSET):
                    tset.append({nm: f32t(nm) for nm in ("s_", "a_", "L0", "E1", "E2", "E3", "E4", "kk", "bv", "tA", "kp", "rn")})
                    tset[-1]["kk2"] = sbt([128, 128], BF16, "kk2")
                    tset[-1]["t2b"] = sbt([128, 128], BF16, "t2b")
                gCt = sbt([128, 4], F32, "gCt")
                AR = sbt([128, 4, 256], BF16, "AR")
                Bt = sbt([128, 4, 128], BF16, "Bt")
                Kt = sbt([128, 4, 128], BF16, "Kt")
                gb = sbt([128, 4, 128], BF16, "gb")
                bonus = sbt([128, 4, 128], F32, "bonus")
                Bh = sbt([128, 4, 128], BF16, "Bh")
                Kh = sbt([128, 4, 128], BF16, "Kh")
                vb = sbt([128, 4, 128], BF16, "vb")
                VT = sbt([128, 512], BF16, "VT")
                KhT = sbt([128, 512], BF16, "KhT")
                BhT = sbt([128, 512], BF16, "BhT")
                Ytok = sbt([128, 512], F32, "Ytok")
                yc = sbt([128, 512], F32, "yc")
                ynb = sbt([128, 512], BF16, "ynb")
                sm = sbt([128, 8], F32, "sm"); sm2 = sbt([128, 8], F32, "sm2"); mean = sbt([128, 8], F32, "mean")
                msq = sbt([128, 8], F32, "msq"); var = sbt([128, 8], F32, "var")
                yo = [f32t("yo") for _ in range(2)]
                MB = [sbt([128, 4, 256], BF16, "MB") for _ in range(2)]
                MKt = [sbt([128, 4, 256], BF16, "MKt") for _ in range(2)]
                Tfw = [sbt([128, 4, 128], BF16, "Tfw") for _ in range(2)]
                TA = [[sbt([128, 4, 256], BF16, "TA") for _ in range(2)] for _ in range(2)]
                ATT = [[sbt([128, 4, 128], BF16, "ATT") for _ in range(2)] for _ in range(2)]
                Sf = sbt([128, 4, 64], F32, "Sf")
                Sb = sbt([128, 4, 64], BF16, "Sb")
                W1 = sbt([128, 512], BF16, "W1")
                UT = sbt([128, 512], BF16, "UT")
                if _os.environ.get('KPRINTMEM'):
                    print('RWKV sbuf remaining', nc.sbuf_bytes_remaining)
                P.memset(Sf[:], 0.0)
                P.memset(Sb[:], 0.0)
                wiv = wn["w_in"].rearrange("(k p) c -> p k c", p=128)
                GN_EPS = 64e-5
                for n in range(RWKV_N):
                    tl = slice(n * 128, (n + 1) * 128)
                    for pc in range(7):
                        Wp = Wst[pc % 2]
                        P.dma(Wp[:], wiv[:, :, pc * 256:(pc + 1) * 256], q="pool")
                        for cc in range(2):
                            c = 2 * pc + cc
                            for kc in range(8):
                                P.mm(psA[c // 4][:, (c % 4) * 128:(c % 4 + 1) * 128], Wp[:, kc, cc * 128:(cc + 1) * 128], hT[:, kc, tl],
                                     start=(kc == 0), stop=(kc == 7))
                    for c in range(14):
                        zc = zcb[c % 2]
                        pz = psA[c // 4][:, (c % 4) * 128:(c % 4 + 1) * 128]
                        P.copy(zc[:, 0:1], carry[:, c:c + 1])
                        P.act(zc[:, 1:129], pz, AF.Copy)
                        P.copy(carry[:, c:c + 1], zc[:, 128:129])
                        P.tt(dz[:], zc[:, 0:128], zc[:, 1:129], ALU.subtract)
                        P.stt(zt[:, c, :], dz[:], muT[:, c:c + 1], zc[:, 1:129], ALU.mult, ALU.add)
                    if RW_STAGE < 2:
                        continue
                    P.act(th[0:64, :], zt[0:64, 12, :], AF.Tanh)
                    P.act(adb[64:128, :], zt[64:128, 12, :], AF.Copy)
                    P.act(sgd[:], zt[:, 13, :], AF.Sigmoid)
                    def r2_chain(c4):
                            T_ = tset[c4 % NSET]
                            s_, a_, L0, E1, E2, E3, E4 = T_["s_"], T_["a_"], T_["L0"], T_["E1"], T_["E2"], T_["E3"], T_["E4"]
                            kk, bv, tA, kp, rn, kk2, t2b = T_["kk"], T_["bv"], T_["tA"], T_["kp"], T_["rn"], T_["kk2"], T_["t2b"]
                            ld = s_
                            Lx = E3
                            kkn = kk
                            r_ = zt[:, c4, :]
                            k_ = zt[:, 4 + c4, :]
                            v_ = zt[:, 8 + c4, :]
                            cs = slice(c4 * 128, (c4 + 1) * 128)
                            pU = psA[0][:, cs]; pAa = psA[1][:, cs]; pG = psA[2][:, cs]; pN = psA[3][:, cs]
                            pBn = psA[4][:, cs]
                            P.mm(pU, w2b[0:64, cs], th[0:64, :])
                            yield
                            P.mm(pAa, a2b[64:128, cs], adb[64:128, :])
                            yield
                            P.mm(pG, g2b[:, cs], sgd[:])
                            yield
                            P.act(s_[:], pU, AF.Sigmoid, bias=cols["rwkv_w0"][:, c4:c4 + 1])
                            yield
                            P.act(a_[:], pAa, AF.Sigmoid, bias=cols["rwkv_a0"][:, c4:c4 + 1])
                            yield
                            P.act(gb[:, c4, :], pG, AF.Copy)
                            yield
                            if RW_SUB < 2:
                                return
                            P.ts(ld[:], s_[:], -0.6065306597126334, None, ALU.mult)
                            yield
                            P.op("dve", lambda E, o_=L0[:], d_=ld[:]: E.tensor_tensor_scan(o_, ones_b[:], d_, 0.0, ALU.mult, ALU.add),
                                 reads=[ones_b[:], ld[:]], writes=[L0[:]])
                            yield
                            Lc = L0
                            P.act(E1[:], Lc[:], AF.Exp)
                            yield
                            P.act(E2[:], Lc[:], AF.Exp, scale=-1.0)
                            yield
                            P.tt(Lx[:], Lc[:], ld[:], ALU.subtract)
                            yield
                            P.act(E3[:], Lx[:], AF.Exp)
                            yield
                            P.act(E4[:], Lc[:], AF.Exp, scale=-1.0, bias=Lc[:, 127:128])
                            yield
                            P.copy(gCt[:, c4:c4 + 1], E1[:, 127:128])
                            yield
                            if RW_SUB < 3:
                                return
                            P.ts(kk[:], k_, cols["rwkv_k_k"][:, c4:c4 + 1], None, ALU.mult)
                            yield
                            P.tt(kk2[:], kk[:], kk[:], ALU.mult)
                            yield
                            P.mm(pN, bones[:], kk2[:])
                            yield
                            P.ts(rn[:], pN, 1e-24, None, ALU.max)
                            yield
                            P.act(rn[:], rn[:], AF.Sqrt)
                            yield
                            P.recip(rn[:], rn[:])
                            yield
                            P.tt(kkn[:], kk[:], rn[:], ALU.mult)
                            yield
                            P.tt(bv[:], kkn[:], a_[:], ALU.mult)
                            yield
                            if RW_SUB < 4:
                                return
                            P.ts(tA[:], a_[:], cols["rwkv_k_a"][:, c4:c4 + 1], omk[:, c4:c4 + 1], ALU.mult, ALU.add)
                            yield
                            P.tt(kp[:], k_, tA[:], ALU.mult)
                            yield
                            P.tt(tA[:], r_, kp[:], ALU.mult)
                            yield
                            P.ts(t2b[:], tA[:], cols["rwkv_r_k"][:, c4:c4 + 1], None, ALU.mult)
                            yield
                            P.mm(pBn, bones[:], t2b[:])
                            yield
                            P.tt(bonus[:, c4, :], pBn, v_, ALU.mult)
                            yield
                            if RW_SUB < 5:
                                return
                            P.stt(AR[:, c4, 0:128], kkn[:], -1.0, E3[:], ALU.mult, ALU.mult)
                            yield
                            P.tt(AR[:, c4, 128:256], r_, E1[:], ALU.mult)
                            yield
                            P.tt(Bt[:, c4, :], bv[:], E2[:], ALU.mult)
                            yield
                            P.tt(Kt[:, c4, :], kp[:], E2[:], ALU.mult)
                            yield
                            P.tt(Bh[:, c4, :], bv[:], E4[:], ALU.mult)
                            yield
                            P.tt(Kh[:, c4, :], kp[:], E4[:], ALU.mult)
                            yield
                            P.copy(vb[:, c4, :], v_, eng="act")
                            yield
                    gens_ = [r2_chain(c4) for c4 in range(4)]
                    while gens_:
                        for g_ in list(gens_):
                            try:
                                next(g_)
                            except StopIteration:
                                gens_.remove(g_)
                    if RW_SUB < 6:
                        continue
                    for i3, (srcf, dstt) in enumerate(((vb, VT), (Kh, KhT), (Bh, BhT))):
                        tp = tpb[i3 % 2]
                        for c4 in range(4):
                            P.tr(tp[:, c4 * 128:(c4 + 1) * 128], srcf[:, c4, :], ident[:])
                        P.copy(dstt[:], tp[:, 0:512], eng=("act" if i3 % 2 else "dve"))
                    if RW_STAGE < 3:
                        continue
                    def head_info(w, hi):
                        h = 4 * w + hi
                        c4, e = divmod(h, 2)
                        return h, c4, e, slice(e * 64, (e + 1) * 64)

                    ubc = maskuu[:, :].unsqueeze(1).to_broadcast([128, 2, 256])
                    slbc = masksl[:, :].unsqueeze(1).to_broadcast([128, 4, 128])
                    idbc = ident[:, :].unsqueeze(1).to_broadcast([128, 4, 128])
                    for w in range(2):
                        Pb = [psA[3 * w + 0][:].rearrange("p (a c) -> p a c", a=2), psA[3 * w + 1][:].rearrange("p (a c) -> p a c", a=2)]
                        P2 = psA[3 * w + 2][:].rearrange("p (a c) -> p a c", a=4)
                        for hi in range(4):
                            h, c4, e, pse = head_info(w, hi)
                            P.mm(Pb[e][:, hi // 2, :], Bt[pse, c4, :], AR[pse, c4, :])
                        for e in range(2):
                            P.tt(MB[w][:, e::2, :], Pb[e], ubc, ALU.mult)
                        for hi in (0, 2):
                            h, c4, e, pse = head_info(w, hi)
                            P.mm(P2[:, hi, :], AR[pse, c4, 0:128], Bt[pse, c4, :])
                        for hi in (1, 3, 0, 2):
                            h, c4, e, pse = head_info(w, hi)
                            P.mm(Pb[e][:, hi // 2, :], Kt[pse, c4, :], AR[pse, c4, :])
                        for hi in (1, 3):
                            h, c4, e, pse = head_info(w, hi)
                            P.mm(P2[:, hi, :], AR[pse, c4, 0:128], Bt[pse, c4, :])
                        for e in range(2):
                            P.tt(MKt[w][:, e::2, :], Pb[e], ubc, ALU.mult)
                        P.tt(ATT[w][0][:], P2, slbc, ALU.mult)
                    for j in range(1, 7):
                        for w in range(2):
                            Pb = [psA[3 * w + 0][:].rearrange("p (a c) -> p a c", a=2), psA[3 * w + 1][:].rearrange("p (a c) -> p a c", a=2)]
                            P2 = psA[3 * w + 2][:].rearrange("p (a c) -> p a c", a=4)
                            src_ta = TA[w][(j - 1) % 2]
                            dst_ta = TA[w][j % 2]
                            At = ATT[w][(j - 1) % 2]
                            for hi in range(4):
                                e = hi % 2
                                if j == 1:
                                    P.mm(Pb[e][:, hi // 2, 128:256], At[:, hi, :], MB[w][:, hi, 0:128])
                                    P.mm(P2[:, hi, :], MB[w][:, hi, 0:128], At[:, hi, :])
                                else:
                                    P.mm(Pb[e][:, hi // 2, :], At[:, hi, :], src_ta[:, hi, :])
                                    P.mm(P2[:, hi, :], src_ta[:, hi, 128:256], At[:, hi, :])
                            if j == 1:
                                P.tt(dst_ta[:, :, 0:128], MB[w][:, :, 0:128], idbc, ALU.add)
                            else:
                                for e in range(2):
                                    P.tt(dst_ta[:, e::2, 0:128], Pb[e][:, :, 0:128], src_ta[:, e::2, 0:128], ALU.add)
                            if j < 6:
                                for e in range(2):
                                    P.act(dst_ta[:, e::2, 128:256], Pb[e][:, :, 128:256], AF.Copy)
                            P.act(ATT[w][j % 2][:], P2, AF.Copy)
                    for w in range(2):
                        P2 = psA[3 * w + 2][:].rearrange("p (a c) -> p a c", a=4)
                        for hi in range(4):
                            P.mm(P2[:, hi, :], ATT[w][0][:, hi, :], TA[w][0][:, hi, 0:128])
                        P.tt(Tfw[w][:], P2, TA[w][0][:, :, 0:128], ALU.add)
                    if RW_STAGE < 4:
                        continue
                    W1ps = psA[4][:].rearrange("p (h v) -> p h v", v=64)
                    Ups = psA[5][:].rearrange("p (h v) -> p h v", v=64)
                    Yps = psA[3][:].rearrange("p (h v) -> p h v", v=64)
                    Sps = psA[2][:].rearrange("p (c v) -> p c v", v=128)
                    hinfo = []
                    for h in range(8):
                        c4, e = divmod(h, 2)
                        hinfo.append((h, c4, slice(e * 64, (e + 1) * 64), slice(h * 64, (h + 1) * 64)))
                    for h, c4, pse, hsl in hinfo:
                        P.mm(W1ps[:, h, :], AR[pse, c4, 0:128], Sb[pse, c4, :], start=True, stop=False)
                        P.mm(W1ps[:, h, :], MKt[h // 4][:, h % 4, 0:128], VT[:, hsl], start=False, stop=True)
                    P.act(W1[:], psA[4][:], AF.Copy)
                    for h, c4, pse, hsl in hinfo:
                        P.mm(Ups[:, h, :], Tfw[h // 4][:, h % 4, :], W1[:, hsl])
                    P.copy(UT[:], psA[5][:])
                    for h, c4, pse, hsl in hinfo:
                        P.mm(Yps[:, h, :], AR[pse, c4, 128:256], Sb[pse, c4, :], start=True, stop=False)
                        P.mm(Yps[:, h, :], MB[h // 4][:, h % 4, 128:256], UT[:, hsl], start=False, stop=False)
                        P.mm(Yps[:, h, :], MKt[h // 4][:, h % 4, 128:256], VT[:, hsl], start=False, stop=True)
                    P.act(Ytok[:], psA[3][:], AF.Copy)
                    for c4 in range(4):
                        cs = slice(c4 * 128, (c4 + 1) * 128)
                        P.mm(Sps[:, c4, :], BhT[:, cs], UT[:, cs], start=True, stop=False)
                        P.mm(Sps[:, c4, :], KhT[:, cs], VT[:, cs], start=False, stop=True)
                    for e in range(2):
                        pse = slice(e * 64, (e + 1) * 64)
                        P.tt(Sf[pse, :, :], Sf[pse, :, :], gCt[pse, :].unsqueeze(2).to_broadcast([64, 4, 64]), ALU.mult)
                        P.tt(Sf[pse, :, :], Sf[pse, :, :], Sps[pse, :, e * 64:(e + 1) * 64], ALU.add)
                    P.act(Sb[:], Sf[:], AF.Copy)
                    if RW_STAGE < 5:
                        continue
                    y3 = Ytok[:].rearrange("p (h d) -> p h d", d=64)
                    P.red(sm[:], y3)
                    P.act(yc[:], Ytok[:], AF.Square)
                    P.red(sm2[:], yc[:].rearrange("p (h d) -> p h d", d=64))
                    P.ts(mean[:], sm[:], 1.0 / 64, None, ALU.mult)
                    P.tt(msq[:], mean[:], mean[:], ALU.mult)
                    P.stt(var[:], sm2[:], 1.0 / 64, msq[:], ALU.mult, ALU.subtract)
                    P.ts(var[:], var[:], GN_EPS, None, ALU.add)
                    P.act(var[:], var[:], AF.Sqrt)
                    P.recip(var[:], var[:])
                    P.tt(yc[:].rearrange("p (h d) -> p h d", d=64), y3, mean[:].unsqueeze(2).to_broadcast([128, 8, 64]), ALU.subtract)
                    P.tt(ynb[:].rearrange("p (h d) -> p h d", d=64), yc[:].rearrange("p (h d) -> p h d", d=64),
                         var[:].unsqueeze(2).to_broadcast([128, 8, 64]), ALU.mult)
                    tp = tpb[n % 2]
                    for c4 in range(4):
                        P.tr(tp[:, c4 * 128:(c4 + 1) * 128], ynb[:, c4 * 128:(c4 + 1) * 128], ident[:])
                    for c4 in range(4):
                        yo_ = yo[c4 % 2]
                        P.ts(yo_[:], tp[:, c4 * 128:(c4 + 1) * 128], cols["rwkv_gn_w"][:, c4:c4 + 1], cols["rwkv_gn_b"][:, c4:c4 + 1],
                             ALU.mult, ALU.add)
                        P.tt(yo_[:], yo_[:], bonus[:, c4, :], ALU.add)
                        P.tt(yrT[:, c4, tl], yo_[:], gb[:, c4, :], ALU.mult)
                if dbg is not None:
                    P.dma(dbg["yr"][:, :, :], yrT[:], q="pool")
                P.barrier()
            merge_branch(ctx, hT, yrT, 4, wn["w_br_rwkv"], 4096)

    def mix_phase():
        with _ES() as ctx:
            hT = P.sb([128, 8, S], BF16, "hT", ctx)
            with _ES() as c1:
                norm_to_hT(c1, hT, wn["mix_norm"], tpb)
                P.barrier()
            if MIX_ATTN:
                attn_branch(hT)
            if MIX_RWKV:
                rwkv_branch(hT)
            P.barrier()

    P.barrier()
    if not SKIPFFN:
        ffn_phase(wn["ffn1_norm"], wn["ffn1_w_gate"], wn["ffn1_w_up"], wn["ffn1_w_down"])
    mix_phase()
    if not SKIPFFN:
        ffn_phase(wn["ffn2_norm"], wn["ffn2_w_gate"], wn["ffn2_w_up"], wn["ffn2_w_down"])
    ple_phase()
    P.finish()
    stack.close()
    return nc


_CACHE = {}


import os as _os
DEBUG = bool(_os.environ.get("KDEBUG"))
MIX_ATTN = not _os.environ.get("KNOATTN")
MIX_RWKV = not _os.environ.get("KNORWKV")
SKIPFFN = bool(_os.environ.get("KSKIPFFN"))
ATT_M = int(_os.environ.get("KATT_M", "16"))
ATT_PR = int(_os.environ.get("KATT_PR", "2"))
RWKV_N = int(_os.environ.get("KRWKV_N", "16"))
SKIPMERGE = bool(_os.environ.get("KSKIPMERGE"))
ATT_G = [int(c) for c in _os.environ.get("KATT_G", "012")]
ATT_STAGE = int(_os.environ.get("KATT_STAGE", "9"))
ATT_SUB = int(_os.environ.get("KATT_SUB", "9"))
RW_STAGE = int(_os.environ.get("KRW_STAGE", "9"))
RW_SUB = int(_os.environ.get("KRW_SUB", "9"))


def _consts():
    p = np.arange(128)[:, None]
    c = np.arange(128)[None, :]
    NEG = -30000.0
    onespad = np.zeros((128, 2, 128), np.float32)
    onespad[:, 0, 0:64] = 1.0
    onespad[:, 1, 64:128] = 1.0
    invf = (1.0 / (np.float32(10000.0) ** (np.arange(0, 64, 2, dtype=np.float32) / np.float32(64)))).astype(np.float32)
    bo = np.zeros((128, 128), np.float32)
    bo[0:64, 0:64] = 1.0
    bo[64:128, 64:128] = 1.0
    su = (p < c).astype(np.float32)
    iu = (p <= c).astype(np.float32)
    return {"c_ident": np.eye(128, dtype=np.float32),
            "c_maskc": np.where(p <= c, 0.0, NEG).astype(np.float32),
            "c_maskp": np.where(p >= c, 0.0, NEG).astype(np.float32),
            "c_onespad": onespad,
            "c_invfreq": np.ascontiguousarray(np.broadcast_to(invf[None, :], (128, 32))),
            "c_maskuu": np.concatenate([su, iu], axis=1),
            "c_masksl": (c < p).astype(np.float32),
            "c_blockones": bo}


def kernel(**inputs):
    if "nc" not in _CACHE:
        _CACHE["nc"] = build_program(dbg=({} if DEBUG else None))
    nc = _CACHE["nc"]
    consts = _consts()
    in_maps = []
    for b in range(8):
        m = dict(consts)
        m["x"] = np.ascontiguousarray(inputs["x"][b])
        m["p"] = np.ascontiguousarray(inputs["p"][0, b])
        m["positions"] = np.ascontiguousarray(inputs["positions"][b])
        for k, v in inputs.items():
            if k in ("x", "p", "positions"):
                continue
            a = np.asarray(v)[0]
            m[k] = np.ascontiguousarray(a.reshape(-1) if k == "rwkv_r_k" else a)
        in_maps.append(m)
    res = run_bass_kernel_spmd(nc, in_maps, core_ids=list(range(8)))
    if DEBUG:
        _CACHE["dbg"] = [{k: np.asarray(v) for k, v in r.items() if k.startswith("dbg_")} for r in res.results]
    return np.stack([np.asarray(r["out"]) for r in res.results], axis=0).astype(np.float32)
```

```python
import numpy as np
import concourse.bass as bass
import concourse.mybir as mybir
from concourse.bass_utils import run_bass_kernel_spmd

F32 = mybir.dt.float32
BF16 = mybir.dt.bfloat16
I32 = mybir.dt.int32
AF = mybir.ActivationFunctionType
ALU = mybir.AluOpType
AX = mybir.AxisListType

S = 2048
D = 1024
NT = 16
DFF = 2816
PLE = 256
RMS_EPS = 1e-6


class _Op:
    __slots__ = ("eng", "fn", "dma", "deps", "ddeps", "mark", "cnt", "slot", "val", "idx", "throttle")


class Prog:
    ENGS = ("pe", "act", "dve", "pool", "sync")
    EP = 2000
    NSLOT = 8

    def __init__(self, nc, stack):
        self.nc = nc
        self.stack = stack
        self.ops = []
        self.trk = {}
        self.last = {e: None for e in self.ENGS}
        self.ndma = {e: 0 for e in self.ENGS}
        self.dma_hist = {e: [] for e in self.ENGS}
        self.uid = 0

    def sb(self, shape, dt, name=None, ctx=None):
        self.uid += 1
        nm = f"{name or 't'}_{self.uid}"
        return (ctx or self.stack).enter_context(self.nc.sbuf_tensor(nm, list(shape), dt))

    def ps(self, shape, dt, name=None, ctx=None):
        self.uid += 1
        nm = f"{name or 'p'}_{self.uid}"
        return (ctx or self.stack).enter_context(self.nc.psum_tensor(nm, list(shape), dt))

    @staticmethod
    def _rect(ap):
        a = ap.ap
        off = int(ap.offset)
        pstep = int(a[0][0])
        npart = int(a[0][1])
        if pstep == 0:
            pstep = 1 << 40
        p0 = off // pstep
        f0 = off % pstep
        ext = 1
        for s, c in a[1:]:
            ext += (int(c) - 1) * abs(int(s))
        if "PSUM" in str(ap.space).upper():
            return ap.tensor.name, ((p0 // 32) * 32, ((p0 + npart + 31) // 32) * 32, 0, 1 << 30)
        return ap.tensor.name, (p0, p0 + npart, f0, f0 + ext)

    @staticmethod
    def _ov(a, b):
        return a[0] < b[1] and b[0] < a[1] and a[2] < b[3] and b[2] < a[3]

    @staticmethod
    def _inside(a, b):
        return a[0] >= b[0] and a[1] <= b[1] and a[2] >= b[2] and a[3] <= b[3]

    def _is_tracked(self, ap):
        sp = str(ap.space)
        return ("SB" in sp.upper()) or ("PSUM" in sp.upper())

    def op(self, eng, fn, reads=(), writes=(), dma=False):
        o = _Op()
        o.eng = eng
        o.fn = fn
        o.dma = dma
        o.deps = {}
        o.ddeps = []
        o.mark = False
        o.cnt = 0
        o.throttle = None
        prods = []
        for ap in reads:
            if ap is None or isinstance(ap, (int, float)) or not self._is_tracked(ap):
                continue
            nm, r = self._rect(ap)
            ents = self.trk.setdefault(nm, [])
            is_ps = "PSUM" in str(ap.space).upper()
            for e in ents:
                if self._ov(e[0], r):
                    if e[1] is not None:
                        prods.append(e[1])
                    if is_ps:
                        for rd in e[2]:
                            if rd.eng != eng:
                                prods.append(rd)
                    e[2].append(o)
        for ap in writes:
            if ap is None or not self._is_tracked(ap):
                continue
            nm, r = self._rect(ap)
            ents = self.trk.setdefault(nm, [])
            keep = []
            for e in ents:
                if self._ov(e[0], r):
                    if e[1] is not None:
                        prods.append(e[1])
                    prods.extend(e[2])
                    if self._inside(e[0], r):
                        continue
                keep.append(e)
            keep.append([r, o, []])
            self.trk[nm] = keep
        for p in prods:
            if p is o:
                continue
            if p.dma:
                if p not in o.ddeps:
                    o.ddeps.append(p)
            else:
                if p.eng == "pe" and eng == "pe" and not dma:
                    continue
                cur = o.deps.get(p.eng)
                if cur is None or p.idx > cur.idx:
                    o.deps[p.eng] = p
        if dma:
            i = self.ndma[eng]
            self.ndma[eng] = i + 1
            o.slot = i % self.NSLOT
            o.val = 16 * (i // self.NSLOT + 1)
            if i >= self.NSLOT:
                o.throttle = self.dma_hist[eng][i - self.NSLOT]
            self.dma_hist[eng].append(o)
        else:
            self.last[eng] = o
        o.idx = len(self.ops)
        self.ops.append(o)
        return o

    def barrier(self):
        lasts = [self.last[e] for e in self.ENGS if self.last[e] is not None]
        dm = []
        for e in self.ENGS:
            dm.extend(self.dma_hist[e][-self.NSLOT:])
        for e in self.ENGS:
            o = _Op()
            o.eng = e
            o.fn = None
            o.dma = False
            o.deps = {}
            for p in lasts:
                if p.eng == e and e == "pe":
                    continue
                o.deps[p.eng] = p
            o.ddeps = list(dm)
            o.mark = False
            o.cnt = 0
            o.throttle = None
            o.idx = len(self.ops)
            self.ops.append(o)
        self.trk = {}
        self.last = {e: None for e in self.ENGS}
        self.flush()

    def _csem(self, e, b):
        lst = self.csem.setdefault(e, [])
        while len(lst) <= b:
            lst.append(self.stack.enter_context(self.nc.semaphore(f"c_{e}_{len(lst)}")))
        return lst[b]

    def _dsem(self, e, s):
        if e not in self.dsem:
            self.dsem[e] = [self.stack.enter_context(self.nc.semaphore(f"d_{e}_{i}")) for i in range(self.NSLOT)]
        return self.dsem[e][s]

    def _wait_dma(self, E, e, d):
        key = (d.eng, d.slot)
        if self.dwaited[e].get(key, 0) < d.val:
            E.wait_ge(self._dsem(d.eng, d.slot), d.val)
            self.dwaited[e][key] = d.val

    def flush(self):
        nc = self.nc
        if not hasattr(self, "counts"):
            self.counts = {e: 0 for e in self.ENGS}
            self.csem = {}
            self.dsem = {}
            self.waited = {e: {f: 0 for f in self.ENGS} for e in self.ENGS}
            self.dwaited = {e: {} for e in self.ENGS}
            self.pos = 0
        engobj = {"pe": nc.tensor, "act": nc.scalar, "dve": nc.vector, "pool": nc.gpsimd, "sync": nc.sync}
        batch = self.ops[self.pos:]
        for o in batch:
            for p in o.deps.values():
                assert p.idx >= self.pos, "dependency on an already-flushed op"
                p.mark = True
        for o in batch:
            if o.fn is not None and not o.dma and o.mark:
                self.counts[o.eng] += 1
                o.cnt = self.counts[o.eng]
        for o in batch:
            e = o.eng
            E = engobj[e]
            for f, p in o.deps.items():
                if p.cnt > self.waited[e][f]:
                    b = (p.cnt - 1) // self.EP
                    E.wait_ge(self._csem(f, b), (p.cnt - 1) % self.EP + 1)
                    self.waited[e][f] = p.cnt
            for d in o.ddeps:
                self._wait_dma(E, e, d)
            if o.throttle is not None:
                self._wait_dma(E, e, o.throttle)
            if o.fn is None:
                continue
            ins = o.fn(E)
            if o.dma:
                ins.then_inc(self._dsem(e, o.slot), 16)
            elif o.mark:
                b = (o.cnt - 1) // self.EP
                ins.then_inc(self._csem(e, b), 1)
            o.fn = None
        self.pos = len(self.ops)

    def finish(self):
        self.flush()
        E = self.nc.sync
        for q in self.ENGS:
            for d in self.dma_hist[q][-self.NSLOT:]:
                self._wait_dma(E, "sync", d)

    def mm(self, out, lhsT, rhs, start=True, stop=True):
        return self.op("pe", lambda E: E.matmul(out, lhsT, rhs, start=start, stop=stop),
                       reads=[lhsT, rhs], writes=[out])

    def tr(self, out, in_, ident):
        return self.op("pe", lambda E: E.transpose(out, in_, ident), reads=[in_, ident], writes=[out])

    def act(self, out, in_, func, bias=None, scale=None, accum_out=None, eng="act"):
        kw = {}
        rd = [in_]
        if bias is not None:
            kw["bias"] = bias
            rd.append(bias)
        if scale is not None:
            kw["scale"] = scale
            rd.append(scale)
        wr = [out]
        if accum_out is not None:
            kw["accum_out"] = accum_out
            wr.append(accum_out)
        return self.op(eng, lambda E: E.activation(out, in_, func, **kw), reads=rd, writes=wr)

    def tt(self, out, in0, in1, op, eng="dve"):
        return self.op(eng, lambda E: E.tensor_tensor(out, in0, in1, op), reads=[in0, in1], writes=[out])

    def ts(self, out, in0, s1, s2, op0, op1=None, eng="dve", accum_out=None):
        rd = [in0, s1, s2]
        wr = [out]
        if accum_out is not None:
            wr.append(accum_out)

        def f(E):
            kw = {}
            if accum_out is not None:
                kw["accum_out"] = accum_out
            if op1 is None:
                return E.tensor_scalar(out, in0, s1, None, op0, **kw)
            return E.tensor_scalar(out, in0, s1, s2, op0, op1, **kw)
        return self.op(eng, f, reads=rd, writes=wr)

    def stt(self, out, in0, scalar, in1, op0, op1, eng="dve"):
        return self.op(eng, lambda E: E.scalar_tensor_tensor(out, in0, scalar, in1, op0, op1),
                       reads=[in0, scalar, in1], writes=[out])

    def copy(self, out, in_, eng="dve"):
        if eng == "act":
            return self.op("act", lambda E: E.copy(out, in_), reads=[in_], writes=[out])
        return self.op(eng, lambda E: E.tensor_copy(out, in_), reads=[in_], writes=[out])

    def red(self, out, in_, op=None, eng="dve"):
        op = op or ALU.add
        return self.op(eng, lambda E: E.tensor_reduce(out, in_, AX.X, op), reads=[in_], writes=[out])

    def recip(self, out, in_):
        return self.op("dve", lambda E: E.reciprocal(out, in_), reads=[in_], writes=[out])

    def memset(self, ap, val, eng="dve"):
        return self.op(eng, lambda E: E.memset(ap, val), reads=[], writes=[ap])

    def dma(self, out, in_, q="sync", slow=False):
        kw = {}
        if slow:
            kw["allow_slow_non_contiguous"] = True
        return self.op(q, lambda E: E.dma_start(out=out, in_=in_, **kw), reads=[in_], writes=[out], dma=True)


FF_SPLITS = (4, 4, 4, 4, 3, 3)


def build_program(dbg=None):
    from contextlib import ExitStack
    nc = bass.Bass("TRN2", target_bir_lowering=False)
    stack = ExitStack()
    P = Prog(nc, stack)

    def din(name, shape, dt=F32):
        return nc.dram_tensor(name, list(shape), dt, kind="ExternalInput").ap()

    x_d = din("x", [S, D])
    p_d = din("p", [S, PLE])
    pos_d = din("positions", [S], I32)
    wn = {}
    for nm, shp in (("ffn1_norm", [D]), ("ffn1_w_gate", [D, DFF]), ("ffn1_w_up", [D, DFF]), ("ffn1_w_down", [DFF, D]),
                    ("mix_norm", [D]), ("w_in", [D, 6144]), ("rwkv_mu", [1792]), ("rwkv_w0", [512]),
                    ("rwkv_w2", [64, 512]), ("rwkv_a0", [512]), ("rwkv_a2", [64, 512]), ("rwkv_g2", [128, 512]),
                    ("rwkv_k_k", [512]), ("rwkv_k_a", [512]), ("rwkv_r_k", [512]), ("rwkv_gn_w", [512]),
                    ("rwkv_gn_b", [512]), ("q_norm", [64]), ("k_norm", [64]), ("w_br_rwkv", [512, D]),
                    ("w_br_attn", [256, D]), ("w_out", [D, D]), ("ffn2_norm", [D]), ("ffn2_w_gate", [D, DFF]),
                    ("ffn2_w_up", [D, DFF]), ("ffn2_w_down", [DFF, D]), ("ple_norm", [D]), ("ple_w_gate", [D, D]),
                    ("ple_w_proj", [PLE, D])):
        wn[nm] = din(nm, shp)
    ident_d = din("c_ident", [128, 128])
    cst = {}
    for nm, shp in (("c_maskc", [128, 128]), ("c_maskp", [128, 128]), ("c_onespad", [128, 2, 128]),
                    ("c_invfreq", [128, 32]), ("c_maskuu", [128, 256]), ("c_masksl", [128, 128]),
                    ("c_blockones", [128, 128])):
        cst[nm] = din(nm, shp)
    if dbg is not None:
        dbg = {"ya": nc.dram_tensor("dbg_ya", [128, 2, S], F32, kind="ExternalOutput").ap(),
               "yr": nc.dram_tensor("dbg_yr", [128, 4, S], F32, kind="ExternalOutput").ap()}
    out_d = nc.dram_tensor("out", [S, D], F32, kind="ExternalOutput").ap()

    X = P.sb([128, NT, D], F32, "X")
    ident = P.sb([128, 128], BF16, "ident")
    P.dma(ident[:], ident_d[:, :], q="pool")
    xv = x_d.rearrange("(t p) d -> p t d", p=128)
    for i in range(4):
        P.dma(X[:, 4 * i:4 * i + 4, :], xv[:, 4 * i:4 * i + 4, :], q="sync")

    from contextlib import ExitStack as _ES

    def norm_to_hT(ctx, hT, gain_d, tp_banks):
        gain_bc = P.sb([128, D], F32, "gain_bc", ctx)
        P.dma(gain_bc[:], gain_d.partition_broadcast(128), q="sync")
        ss = P.sb([128, NT], F32, "ss", ctx)
        rstd = P.sb([128, NT], F32, "rstd", ctx)
        junk = P.sb([128, D], BF16, "junk", ctx)
        xn = [P.sb([128, D], BF16, "xn", ctx) for _ in range(2)]
        P.memset(ss[:], 0.0)
        for t in range(NT):
            P.act(junk[:], X[:, t, :], AF.Square, accum_out=ss[:, t:t + 1])
        P.ts(rstd[:], ss[:], 1.0 / D, RMS_EPS, ALU.mult, ALU.add)
        P.act(rstd[:], rstd[:], AF.Sqrt)
        P.recip(rstd[:], rstd[:])
        for t in range(NT):
            xt = xn[t % 2]
            P.stt(xt[:], X[:, t, :], rstd[:, t:t + 1], gain_bc[:], ALU.mult, ALU.mult)
            pt = tp_banks[t % 2]
            for kc in range(8):
                P.tr(pt[:, kc * 128:(kc + 1) * 128], xt[:, kc * 128:(kc + 1) * 128], ident[:])
            P.copy(hT[:, :, t * 128:(t + 1) * 128], pt[:].rearrange("p (k c) -> p k c", k=8),
                   eng=("act" if t % 2 else "dve"))

    def ffn(ctx, hT, wg_d, wu_d, wd_d, psum):
        wgv = wg_d.rearrange("(k p) c -> p k c", p=128)
        wuv = wu_d.rearrange("(k p) c -> p k c", p=128)
        wdv = wd_d.rearrange("(f p) d -> p f d", p=128)
        Wg = [P.sb([128, 8, 512], BF16, "Wg", ctx) for _ in range(2)]
        Wu = [P.sb([128, 8, 512], BF16, "Wu", ctx) for _ in range(2)]
        Wd = [P.sb([128, 4, D], BF16, "Wd", ctx) for _ in range(2)]
        aT = [P.sb([128, 4, S], BF16, "aT", ctx) for _ in range(2)]
        sg = [P.sb([128, 512], BF16, "sg", ctx) for _ in range(2)]
        pg, pu, pd = psum["g"], psum["u"], psum["d"]
        offs = []
        f0 = 0
        for nf in FF_SPLITS:
            offs.append((f0, nf))
            f0 += nf
        cnt = {"gu": 0, "d": 0}

        def load(si):
            f0, nf = offs[si]
            sl = si % 2
            c0, c1 = f0 * 128, (f0 + nf) * 128
            P.dma(Wg[sl][:, :, 0:nf * 128], wgv[:, :, c0:c1], q="pool")
            P.dma(Wu[sl][:, :, 0:nf * 128], wuv[:, :, c0:c1], q="pool")
            P.dma(Wd[sl][:, 0:nf, :], wdv[:, f0:f0 + nf, :], q="pool")

        def gu(si):
            f0, nf = offs[si]
            sl = si % 2
            for fl in range(nf):
                for tg in range(4):
                    i = cnt["gu"] % 2
                    cnt["gu"] += 1
                    for kc in range(8):
                        P.mm(pg[i][:], Wg[sl][:, kc, fl * 128:(fl + 1) * 128], hT[:, kc, tg * 512:(tg + 1) * 512],
                             start=(kc == 0), stop=(kc == 7))
                    for kc in range(8):
                        P.mm(pu[i][:], Wu[sl][:, kc, fl * 128:(fl + 1) * 128], hT[:, kc, tg * 512:(tg + 1) * 512],
                             start=(kc == 0), stop=(kc == 7))
                    P.act(sg[i][:], pg[i][:], AF.Silu)
                    P.tt(aT[sl][:, fl, tg * 512:(tg + 1) * 512], sg[i][:], pu[i][:], ALU.mult)

        def down(si):
            f0, nf = offs[si]
            sl = si % 2
            for t in range(NT):
                for dh in range(2):
                    i = cnt["d"] % 2
                    cnt["d"] += 1
                    for fl in range(nf):
                        P.mm(pd[i][:], aT[sl][:, fl, t * 128:(t + 1) * 128], Wd[sl][:, fl, dh * 512:(dh + 1) * 512],
                             start=(fl == 0), stop=(fl == nf - 1))
                    xs = X[:, t, dh * 512:(dh + 1) * 512]
                    P.stt(xs, pd[i][:], 0.5, xs, ALU.mult, ALU.add)

        ns = len(offs)
        load(0)
        load(1)
        gu(0)
        for si in range(1, ns):
            gu(si)
            down(si - 1)
            if si + 1 < ns:
                load(si + 1)
        down(ns - 1)

    psA = [P.ps([128, 512], F32, "psA") for _ in range(6)]
    tpb = [P.ps([128, 1024], BF16, "tpb") for _ in range(2)]
    psum_ffn = {"g": psA[0:2], "u": psA[2:4], "d": psA[4:6]}

    def ffn_phase(norm_d, wg_d, wu_d, wd_d):
        with _ES() as ctx:
            hT = P.sb([128, 8, S], BF16, "hT", ctx)
            norm_to_hT(ctx, hT, norm_d, tpb)
            ffn(ctx, hT, wg_d, wu_d, wd_d, psum_ffn)
            P.barrier()

    def ple_phase():
        with _ES() as ctx:
            hT = P.sb([128, 8, S], BF16, "hT", ctx)
            Wpg = P.sb([128, 8, D], BF16, "Wpg", ctx)
            Wpp = P.sb([128, 2, D], BF16, "Wpp", ctx)
            P.dma(Wpg[:], wn["ple_w_gate"].rearrange("(k p) c -> p k c", p=128), q="pool")
            P.dma(Wpp[:], wn["ple_w_proj"].rearrange("(k p) c -> p k c", p=128), q="pool")
            pin = P.sb([128, NT, PLE], BF16, "pin", ctx)
            P.dma(pin[:], p_d.rearrange("(t p) c -> p t c", p=128), q="pool")
            pT = P.sb([128, 2, S], BF16, "pT", ctx)
            norm_to_hT(ctx, hT, wn["ple_norm"], tpb)
            for t in range(NT):
                pt = tpb[t % 2]
                for k in range(2):
                    P.tr(pt[:, k * 128:(k + 1) * 128], pin[:, t, k * 128:(k + 1) * 128], ident[:])
                P.copy(pT[:, :, t * 128:(t + 1) * 128], pt[:, 0:256].rearrange("p (k c) -> p k c", k=2),
                       eng=("act" if t % 2 else "dve"))
            sgs = [P.sb([128, 512], F32, "sgs", ctx) for _ in range(2)]
            n = 0
            ov = out_d.rearrange("(t p) d -> p t d", p=128)
            for t in range(NT):
                for dh in range(2):
                    i = n % 2
                    n += 1
                    pG, pP = psA[i], psA[2 + i]
                    for kc in range(8):
                        P.mm(pG[:], hT[:, kc, t * 128:(t + 1) * 128], Wpg[:, kc, dh * 512:(dh + 1) * 512],
                             start=(kc == 0), stop=(kc == 7))
                    for k in range(2):
                        P.mm(pP[:], pT[:, k, t * 128:(t + 1) * 128], Wpp[:, k, dh * 512:(dh + 1) * 512],
                             start=(k == 0), stop=(k == 1))
                    P.act(sgs[i][:], pG[:], AF.Sigmoid)
                    P.tt(sgs[i][:], sgs[i][:], pP[:], ALU.mult)
                    xs = X[:, t, dh * 512:(dh + 1) * 512]
                    P.tt(xs, xs, sgs[i][:], ALU.add)
                P.dma(ov[:, t, :], X[:, t, :], q="sync")
            P.barrier()


    TWO_PI = 6.283185307179586
    C1 = 6.28125
    C2 = TWO_PI - C1
    PI = 3.141592653589793

    def run_interleaved(gens):
        gens = list(gens)
        while gens:
            for g_ in list(gens):
                try:
                    next(g_)
                except StopIteration:
                    gens.remove(g_)

    def tokslice(hT, kc, t0, d):
        if d == 1:
            return hT[:, kc, t0:t0 + 128]
        return hT[:, kc, t0:t0 + 127 * d + 1:d]

    def merge_branch(ctx0, hT, yT, nyc, wbr_d, gate_col0):
        if SKIPMERGE:
            return
        with _ES() as ctx:
            Wbr = P.sb([128, nyc, D], BF16, "Wbr", ctx)
            P.dma(Wbr[:], wbr_d.rearrange("(c p) d -> p c d", p=128), q="pool")
            Wg = P.sb([128, 8, D], BF16, "Wgt", ctx)
            wiv = wn["w_in"].rearrange("(k p) c -> p k c", p=128)
            P.dma(Wg[:], wiv[:, :, gate_col0:gate_col0 + D], q="pool")
            Wo = P.sb([128, 8, D], BF16, "Wo", ctx)
            P.dma(Wo[:], wn["w_out"].rearrange("(k p) c -> p k c", p=128), q="pool")
            mT = P.sb([128, 8, S], BF16, "mT", ctx)
            sgm = [P.sb([128, 512], F32, "sgm", ctx) for _ in range(2)]
            n = 0
            for dc in range(8):
                for tg in range(4):
                    i = n % 2
                    n += 1
                    pG, pB = psA[i], psA[2 + i]
                    ts_ = slice(tg * 512, (tg + 1) * 512)
                    for kc in range(8):
                        P.mm(pG[:], Wg[:, kc, dc * 128:(dc + 1) * 128], hT[:, kc, ts_], start=(kc == 0), stop=(kc == 7))
                    for c in range(nyc):
                        P.mm(pB[:], Wbr[:, c, dc * 128:(dc + 1) * 128], yT[:, c, ts_], start=(c == 0), stop=(c == nyc - 1))
                    P.act(sgm[i][:], pG[:], AF.Sigmoid)
                    P.tt(mT[:, dc, ts_], sgm[i][:], pB[:], ALU.mult)
            n = 0
            for t in range(NT):
                for dh in range(2):
                    pd_ = psA[4 + n % 2]
                    n += 1
                    for dc in range(8):
                        P.mm(pd_[:], mT[:, dc, t * 128:(t + 1) * 128], Wo[:, dc, dh * 512:(dh + 1) * 512],
                             start=(dc == 0), stop=(dc == 7))
                    xs = X[:, t, dh * 512:(dh + 1) * 512]
                    P.tt(xs, xs, pd_[:], ALU.add)
            P.barrier()

    def attn_branch(hT):
        with _ES() as ctx:
            yaT = P.sb([128, 2, S], BF16, "yaT", ctx)
            with _ES() as c2:
                maskC = P.sb([128, 128], BF16, "maskC", c2)
                maskP = P.sb([128, 128], BF16, "maskP", c2)
                onesp = P.sb([128, 2, 128], BF16, "onesp", c2)
                P.dma(maskC[:], cst["c_maskc"][:, :], q="pool")
                P.dma(maskP[:], cst["c_maskp"][:, :], q="pool")
                P.dma(onesp[:], cst["c_onespad"][:, :, :], q="pool")
                invf = P.sb([128, 32], F32, "invf", c2)
                P.dma(invf[:], cst["c_invfreq"][:, :], q="sync")
                gain4 = P.sb([128, 4, 64], F32, "gain4", c2)
                for hh in range(4):
                    P.dma(gain4[:, hh, :], (wn["q_norm"] if hh < 2 else wn["k_norm"]).partition_broadcast(128), q="sync")
                cosT = [P.sb([128, 16 * 32], F32, "cosT", c2) for _ in range(3)]
                sinT = [P.sb([128, 16 * 32], F32, "sinT", c2) for _ in range(3)]
                with _ES() as c3:
                    posi = P.sb([128, 16], I32, "posi", c3)
                    posf = P.sb([128, 16], F32, "posf", c3)
                    ang = P.sb([128, 512], F32, "ang", c3)
                    aa = P.sb([128, 512], F32, "aa", c3)
                    kf = P.sb([128, 512], F32, "kf", c3)
                    ki = P.sb([128, 512], I32, "ki", c3)
                    rr = P.sb([128, 512], F32, "rr", c3)
                    mk = P.sb([128, 512], F32, "mk", c3)
                    for g, d in enumerate((1, 4, 16)):
                        pv = pos_d.rearrange("(ib p j) -> p j ib", p=128, j=d)
                        nib = 16 // d
                        for m in range(16):
                            P.dma(posi[:, m:m + 1], pv[:, m // nib, (m % nib):(m % nib) + 1], q="sync", slow=True)
                        P.copy(posf[:], posi[:])
                        for m in range(16):
                            P.ts(ang[:, m * 32:(m + 1) * 32], invf[:], posf[:, m:m + 1], None, ALU.mult)
                        for dst, shift in ((sinT[g], 0.0), (cosT[g], PI / 2)):
                            P.ts(aa[:], ang[:], shift, None, ALU.add)
                            P.ts(kf[:], aa[:], 1.0 / TWO_PI, None, ALU.mult)
                            P.copy(ki[:], kf[:])
                            P.copy(kf[:], ki[:])
                            P.stt(rr[:], kf[:], -C1, aa[:], ALU.mult, ALU.add)
                            P.stt(rr[:], kf[:], -C2, rr[:], ALU.mult, ALU.add)
                            P.ts(mk[:], rr[:], PI, None, ALU.is_gt)
                            P.stt(rr[:], mk[:], -TWO_PI, rr[:], ALU.mult, ALU.add)
                            P.ts(mk[:], rr[:], -PI, None, ALU.is_lt)
                            P.stt(rr[:], mk[:], TWO_PI, rr[:], ALU.mult, ALU.add)
                            P.act(dst[:], rr[:], AF.Sin)
                    P.barrier()
                Wq = [[P.sb([128, 8, 128], BF16, "Wq", c2) for _ in range(3)] for _ in range(2)]
                qkT = P.sb([128, 2, S], BF16, "qkT", c2)
                Vpad = P.sb([128, 16, 2, 128], BF16, "Vpad", c2)
                acc = P.sb([128, 2, S], F32, "acc", c2)
                abuf = []
                for _i in range(2):
                    abuf.append((P.sb([128, 4, 256], F32, "qkb", c2), P.sb([128, 4, 256], F32, "qnb", c2),
                                 P.sb([128, 16], F32, "ss16", c2), P.sb([128, 512], F32, "t1", c2),
                                 P.sb([128, 512], F32, "t2", c2), P.sb([128, 512], F32, "t3", c2),
                                 P.sb([128, 512], F32, "t4", c2), P.sb([128, 4, 256], BF16, "qrb", c2)))
                PT = [P.sb([128, 4, 128], BF16, "PT", c2) for _ in range(2)]
                if _os.environ.get('KPRINTMEM'):
                    print('ATTN sbuf remaining', nc.sbuf_bytes_remaining)
                P.memset(Vpad[:], 0.0)
                wiv = wn["w_in"].rearrange("(k p) c -> p k c", p=128)
                it = 0
                for pr in range(ATT_PR):
                    for g, d in enumerate((1, 4, 16)):
                        if g not in ATT_G:
                            continue
                        L = S // d
                        W = Wq[it % 2]
                        it += 1
                        qc0 = 1792 + (g * 4 + pr * 2) * 64
                        for j3 in range(3):
                            P.dma(W[j3][:], wiv[:, :, qc0 + 768 * j3:qc0 + 768 * j3 + 128], q="pool")
                        TB = 4

                        def st1_chain(mb, bs):
                            qkb, qnb, ss16, t1, t2, t3, t4, qrb = abuf[bs]
                            bank = psA[3 + bs]
                            for mi in range(TB):
                                m = mb + mi
                                j, i0 = divmod(m * 128, L)
                                t0 = i0 * d + j
                                for j3 in range(3):
                                    for kc in range(8):
                                        P.mm(bank[:, j3 * 128:(j3 + 1) * 128], tokslice(hT, kc, t0, d), W[j3][:, kc, :],
                                             start=(kc == 0), stop=(kc == 7))
                                    yield
                                P.act(qkb[:, mi, :], bank[:, 0:256], AF.Copy)
                                yield
                                P.copy(Vpad[:, m, 0, 0:64], bank[:, 256:320])
                                yield
                                P.copy(Vpad[:, m, 1, 64:128], bank[:, 320:384])
                                yield
                            P.tt(qnb[:], qkb[:], qkb[:], ALU.mult)
                            yield
                            P.red(ss16[:], qnb[:].rearrange("p a (h d) -> p (a h) d", d=64))
                            yield
                            P.ts(ss16[:], ss16[:], 1.0 / 64, RMS_EPS, ALU.mult, ALU.add)
                            yield
                            P.act(ss16[:], ss16[:], AF.Sqrt)
                            yield
                            P.recip(ss16[:], ss16[:])
                            yield
                            P.tt(qnb[:].rearrange("p a (h d) -> p (a h) d", d=64), qkb[:].rearrange("p a (h d) -> p (a h) d", d=64),
                                 ss16[:].unsqueeze(2).to_broadcast([128, 16, 64]), ALU.mult)
                            yield
                            P.tt(qnb[:], qnb[:], gain4[:].rearrange("p h d -> p (h d)").unsqueeze(1).to_broadcast([128, TB, 256]), ALU.mult)
                            yield
                            q5 = qnb[:].rearrange("p a (h t i) -> p a h t i", t=2, i=32)
                            x1, x2 = q5[:, :, :, 0, :], q5[:, :, :, 1, :]
                            cb = cosT[g][:, mb * 32:(mb + TB) * 32].rearrange("p (a i) -> p a i", i=32).unsqueeze(2).to_broadcast([128, TB, 4, 32])
                            sb_ = sinT[g][:, mb * 32:(mb + TB) * 32].rearrange("p (a i) -> p a i", i=32).unsqueeze(2).to_broadcast([128, TB, 4, 32])
                            v4 = lambda tl: tl[:].rearrange("p (a h i) -> p a h i", h=4, i=32)
                            P.tt(v4(t1), x1, cb, ALU.mult)
                            yield
                            P.tt(v4(t2), x2, sb_, ALU.mult)
                            yield
                            P.tt(v4(t3), x2, cb, ALU.mult)
                            yield
                            P.tt(v4(t4), x1, sb_, ALU.mult)
                            yield
                            r5 = qrb[:].rearrange("p a (h t i) -> p a h t i", t=2, i=32)
                            P.tt(r5[:, :, :, 0, :], v4(t1), v4(t2), ALU.subtract)
                            yield
                            P.tt(r5[:, :, :, 1, :], v4(t3), v4(t4), ALU.add)
                            yield
                            tp = tpb[bs]
                            for mi in range(TB):
                                for a in range(2):
                                    P.tr(tp[:, a * 512 + mi * 128:a * 512 + (mi + 1) * 128], qrb[:, mi, a * 128:(a + 1) * 128], ident[:])
                                yield
                            P.copy(qkT[:, :, mb * 128:(mb + TB) * 128], tp[:].rearrange("p (a c) -> p a c", a=2), eng="act")
                            yield

                        for mb2 in range(0, 16, 2 * TB):
                            run_interleaved([st1_chain(mb2, 0), st1_chain(mb2 + TB, 1)])
                        for m in range(ATT_M if ATT_STAGE >= 2 else 0):
                            j, i0 = divmod(m * 128, L)
                            t0 = i0 * d + j
                            has_prev = (i0 // 128) > 0
                            sc = psA[m % 2][:].rearrange("p (a c) -> p a c", a=4)
                            qs = slice(m * 128, (m + 1) * 128)
                            for e in range(2):
                                pse = slice(e * 64, (e + 1) * 64)
                                P.mm(sc[:, e, :], qkT[pse, 1, qs], qkT[pse, 0, qs], start=True, stop=False)
                                P.mm(sc[:, e, :], ident[:], maskC[:], start=False, stop=True)
                                if has_prev:
                                    P.mm(sc[:, 2 + e, :], qkT[pse, 1, (m - 1) * 128:m * 128], qkT[pse, 0, qs], start=True, stop=False)
                                    P.mm(sc[:, 2 + e, :], ident[:], maskP[:], start=False, stop=True)
                            nk = 4 if has_prev else 2
                            pt_ = PT[m % 2]
                            P.act(pt_[:, 0:nk, :], sc[:, 0:nk, :], AF.Exp, scale=0.125)
                            OL = psA[2][:, (m % 2) * 256:(m % 2) * 256 + 256].rearrange("p (a c) -> p a c", a=2)
                            seq = [(kb, e) for kb in range(nk // 2) for e in range(2)]
                            for ii, (kb, e) in enumerate(seq):
                                P.mm(OL[:, 0, :], Vpad[:, m - kb, e, :], pt_[:, kb * 2 + e, :], start=(ii == 0), stop=(ii == len(seq) - 1))
                            for ii, (kb, e) in enumerate(seq):
                                P.mm(OL[:, 1, :], onesp[:, e, :], pt_[:, kb * 2 + e, :], start=(ii == 0), stop=(ii == len(seq) - 1))
                            if d == 1:
                                av = acc[:, :, t0:t0 + 128]
                            else:
                                av = acc[:, :, t0:t0 + 127 * d + 1:d]
                            if g == 0:
                                P.copy(av, OL)
                            else:
                                P.tt(av, av, OL, ALU.add)
                    P.recip(acc[:, 1, :], acc[:, 1, :])
                    P.tt(yaT[:, pr, :], acc[:, 0, :], acc[:, 1, :], ALU.mult)
                if dbg is not None:
                    P.dma(dbg["ya"][:, :, :], yaT[:], q="pool")
                P.barrier()
            merge_branch(ctx, hT, yaT, 2, wn["w_br_attn"], 4096 + 1024)

    def rwkv_branch(hT):
        with _ES() as ctx:
            yrT = P.sb([128, 4, S], BF16, "yrT", ctx)
            with _ES() as c2:
                sbt = lambda shp, dt, nm: P.sb(shp, dt, nm, c2)
                maskuu = sbt([128, 256], BF16, "maskuu")
                masksl = sbt([128, 128], BF16, "masksl")
                bones = sbt([128, 128], BF16, "bones")
                P.dma(maskuu[:], cst["c_maskuu"][:, :], q="pool")
                P.dma(masksl[:], cst["c_masksl"][:, :], q="pool")
                P.dma(bones[:], cst["c_blockones"][:, :], q="pool")
                muT = sbt([128, 14], F32, "muT")
                for c in range(14):
                    P.dma(muT[:, c:c + 1], wn["rwkv_mu"][c * 128:(c + 1) * 128].rearrange("(p o) -> p o", o=1), q="sync", slow=True)
                cols = {}
                for nm in ("rwkv_w0", "rwkv_a0", "rwkv_k_k", "rwkv_k_a", "rwkv_r_k", "rwkv_gn_w", "rwkv_gn_b"):
                    cols[nm] = sbt([128, 4], F32, nm)
                    for c in range(4):
                        P.dma(cols[nm][:, c:c + 1], wn[nm][c * 128:(c + 1) * 128].rearrange("(p o) -> p o", o=1), q="sync", slow=True)
                omk = sbt([128, 4], F32, "omk")
                P.ts(omk[:], cols["rwkv_k_a"][:], -1.0, 1.0, ALU.mult, ALU.add)
                w2b = sbt([128, 512], BF16, "w2b")
                a2b = sbt([128, 512], BF16, "a2b")
                g2b = sbt([128, 512], BF16, "g2b")
                P.dma(w2b[0:64, :], wn["rwkv_w2"][:, :], q="pool")
                P.dma(a2b[64:128, :], wn["rwkv_a2"][:, :], q="pool")
                P.dma(g2b[:], wn["rwkv_g2"][:, :], q="pool")
                Wst = [sbt([128, 8, 256], BF16, "Wst") for _ in range(2)]
                zt = sbt([128, 14, 128], F32, "zt")
                zcb = [sbt([128, 129], F32, "zc") for _ in range(4)]
                dzb = [sbt([128, 128], F32, "dzb") for _ in range(4)]
                carry = sbt([128, 14], F32, "carry")
                P.memset(carry[:], 0.0)
                ones_b = sbt([128, 128], BF16, "ones_b")
                P.memset(ones_b[:], 1.0)
                th = sbt([128, 128], BF16, "th")
                adb = sbt([128, 128], BF16, "adb")
                sgd = sbt([128, 128], BF16, "sgd")
                f32t = lambda nm: sbt([128, 128], F32, nm)
                NSET = 4
                tset = []
                for _i in range(NSET):
                    tset.append({nm: f32t(nm) for nm in ("s_", "a_", "L0", "E1", "E2", "E3", "E4", "kk", "bv", "tA", "kp", "rn")})
                    tset[-1]["kk2"] = sbt([128, 128], BF16, "kk2")
                    tset[-1]["t2b"] = sbt([128, 128], BF16, "t2b")
                gCt = sbt([128, 4], F32, "gCt")
                AR = sbt([128, 4, 256], BF16, "AR")
                Bt = sbt([128, 4, 128], BF16, "Bt")
                Kt = sbt([128, 4, 128], BF16, "Kt")
                gb = sbt([128, 4, 128], BF16, "gb")
                bonus = sbt([128, 4, 128], F32, "bonus")
                Bh = sbt([128, 4, 128], BF16, "Bh")
                Kh = sbt([128, 4, 128], BF16, "Kh")
                vb = sbt([128, 4, 128], BF16, "vb")
                VT = sbt([128, 512], BF16, "VT")
                KhT = sbt([128, 512], BF16, "KhT")
                BhT = sbt([128, 512], BF16, "BhT")
                Ytok = sbt([128, 512], F32, "Ytok")
                yc = sbt([128, 512], F32, "yc")
                ynb = sbt([128, 512], BF16, "ynb")
                sm = sbt([128, 8], F32, "sm"); sm2 = sbt([128, 8], F32, "sm2"); mean = sbt([128, 8], F32, "mean")
                msq = sbt([128, 8], F32, "msq"); var = sbt([128, 8], F32, "var")
                yo = [f32t("yo") for _ in range(2)]
                MB = [sbt([128, 4, 256], BF16, "MB") for _ in range(2)]
                MKt = [sbt([128, 4, 256], BF16, "MKt") for _ in range(2)]
                Tfw = [sbt([128, 4, 128], BF16, "Tfw") for _ in range(2)]
                TA = [[sbt([128, 4, 256], BF16, "TA") for _ in range(2)] for _ in range(2)]
                ATT = [[sbt([128, 4, 128], BF16, "ATT") for _ in range(2)] for _ in range(2)]
                Sf = sbt([128, 4, 64], F32, "Sf")
                Sb = sbt([128, 4, 64], BF16, "Sb")
                W1 = sbt([128, 512], BF16, "W1")
                UT = sbt([128, 512], BF16, "UT")
                if _os.environ.get('KPRINTMEM'):
                    print('RWKV sbuf remaining', nc.sbuf_bytes_remaining)
                P.memset(Sf[:], 0.0)
                P.memset(Sb[:], 0.0)
                wiv = wn["w_in"].rearrange("(k p) c -> p k c", p=128)
                GN_EPS = 64e-5
                for n in range(RWKV_N):
                    tl = slice(n * 128, (n + 1) * 128)
                    for pc in range(7):
                        Wp = Wst[pc % 2]
                        P.dma(Wp[:], wiv[:, :, pc * 256:(pc + 1) * 256], q="pool")
                        for cc in range(2):
                            c = 2 * pc + cc
                            for kc in range(8):
                                P.mm(psA[c // 4][:, (c % 4) * 128:(c % 4 + 1) * 128], Wp[:, kc, cc * 128:(cc + 1) * 128], hT[:, kc, tl],
                                     start=(kc == 0), stop=(kc == 7))
                    def r1_chain(c):
                        zc = zcb[c % 4]
                        dz_ = dzb[c % 4]
                        pz = psA[c // 4][:, (c % 4) * 128:(c % 4 + 1) * 128]
                        P.copy(zc[:, 0:1], carry[:, c:c + 1])
                        yield
                        P.act(zc[:, 1:129], pz, AF.Copy)
                        yield
                        P.copy(carry[:, c:c + 1], zc[:, 128:129])
                        yield
                        P.tt(dz_[:], zc[:, 0:128], zc[:, 1:129], ALU.subtract)
                        yield
                        P.stt(zt[:, c, :], dz_[:], muT[:, c:c + 1], zc[:, 1:129], ALU.mult, ALU.add)
                        yield
                    for base in range(0, 14, 4):
                        run_interleaved([r1_chain(c) for c in range(base, min(base + 4, 14))])
                    if RW_STAGE < 2:
                        continue
                    P.act(th[0:64, :], zt[0:64, 12, :], AF.Tanh)
                    P.act(adb[64:128, :], zt[64:128, 12, :], AF.Copy)
                    P.act(sgd[:], zt[:, 13, :], AF.Sigmoid)
                    def r2_chain(c4):
                            T_ = tset[c4 % NSET]
                            s_, a_, L0, E1, E2, E3, E4 = T_["s_"], T_["a_"], T_["L0"], T_["E1"], T_["E2"], T_["E3"], T_["E4"]
                            kk, bv, tA, kp, rn, kk2, t2b = T_["kk"], T_["bv"], T_["tA"], T_["kp"], T_["rn"], T_["kk2"], T_["t2b"]
                            ld = s_
                            Lx = E3
                            kkn = kk
                            r_ = zt[:, c4, :]
                            k_ = zt[:, 4 + c4, :]
                            v_ = zt[:, 8 + c4, :]
                            cs = slice(c4 * 128, (c4 + 1) * 128)
                            pU = psA[0][:, cs]; pAa = psA[1][:, cs]; pG = psA[2][:, cs]; pN = psA[3][:, cs]
                            pBn = psA[4][:, cs]
                            P.mm(pU, w2b[0:64, cs], th[0:64, :])
                            yield
                            P.mm(pAa, a2b[64:128, cs], adb[64:128, :])
                            yield
                            P.mm(pG, g2b[:, cs], sgd[:])
                            yield
                            P.act(s_[:], pU, AF.Sigmoid, bias=cols["rwkv_w0"][:, c4:c4 + 1])
                            yield
                            P.act(a_[:], pAa, AF.Sigmoid, bias=cols["rwkv_a0"][:, c4:c4 + 1])
                            yield
                            P.act(gb[:, c4, :], pG, AF.Copy)
                            yield
                            if RW_SUB < 2:
                                return
                            P.ts(ld[:], s_[:], -0.6065306597126334, None, ALU.mult)
                            yield
                            P.op("dve", lambda E, o_=L0[:], d_=ld[:]: E.tensor_tensor_scan(o_, ones_b[:], d_, 0.0, ALU.mult, ALU.add),
                                 reads=[ones_b[:], ld[:]], writes=[L0[:]])
                            yield
                            Lc = L0
                            P.act(E1[:], Lc[:], AF.Exp)
                            yield
                            P.act(E2[:], Lc[:], AF.Exp, scale=-1.0)
                            yield
                            P.tt(Lx[:], Lc[:], ld[:], ALU.subtract)
                            yield
                            P.act(E3[:], Lx[:], AF.Exp)
                            yield
                            P.act(E4[:], Lc[:], AF.Exp, scale=-1.0, bias=Lc[:, 127:128])
                            yield
                            P.copy(gCt[:, c4:c4 + 1], E1[:, 127:128])
                            yield
                            if RW_SUB < 3:
                                return
                            P.ts(kk[:], k_, cols["rwkv_k_k"][:, c4:c4 + 1], None, ALU.mult)
                            yield
                            P.tt(kk2[:], kk[:], kk[:], ALU.mult)
                            yield
                            P.mm(pN, bones[:], kk2[:])
                            yield
                            P.ts(rn[:], pN, 1e-24, None, ALU.max)
                            yield
                            P.act(rn[:], rn[:], AF.Sqrt)
                            yield
                            P.recip(rn[:], rn[:])
                            yield
                            P.tt(kkn[:], kk[:], rn[:], ALU.mult)
                            yield
                            P.tt(bv[:], kkn[:], a_[:], ALU.mult)
                            yield
                            if RW_SUB < 4:
                                return
                            P.ts(tA[:], a_[:], cols["rwkv_k_a"][:, c4:c4 + 1], omk[:, c4:c4 + 1], ALU.mult, ALU.add)
                            yield
                            P.tt(kp[:], k_, tA[:], ALU.mult)
                            yield
                            P.tt(tA[:], r_, kp[:], ALU.mult)
                            yield
                            P.ts(t2b[:], tA[:], cols["rwkv_r_k"][:, c4:c4 + 1], None, ALU.mult)
                            yield
                            P.mm(pBn, bones[:], t2b[:])
                            yield
                            P.tt(bonus[:, c4, :], pBn, v_, ALU.mult)
                            yield
                            if RW_SUB < 5:
                                return
                            P.stt(AR[:, c4, 0:128], kkn[:], -1.0, E3[:], ALU.mult, ALU.mult)
                            yield
                            P.tt(AR[:, c4, 128:256], r_, E1[:], ALU.mult)
                            yield
                            P.tt(Bt[:, c4, :], bv[:], E2[:], ALU.mult)
                            yield
                            P.tt(Kt[:, c4, :], kp[:], E2[:], ALU.mult)
                            yield
                            P.tt(Bh[:, c4, :], bv[:], E4[:], ALU.mult)
                            yield
                            P.tt(Kh[:, c4, :], kp[:], E4[:], ALU.mult)
                            yield
                            P.copy(vb[:, c4, :], v_, eng="act")
                            yield
                    gens_ = [r2_chain(c4) for c4 in range(4)]
                    while gens_:
                        for g_ in list(gens_):
                            try:
                                next(g_)
                            except StopIteration:
                                gens_.remove(g_)
                    if RW_SUB < 6:
                        continue
                    for i3, (srcf, dstt) in enumerate(((vb, VT), (Kh, KhT), (Bh, BhT))):
                        tp = tpb[i3 % 2]
                        for c4 in range(4):
                            P.tr(tp[:, c4 * 128:(c4 + 1) * 128], srcf[:, c4, :], ident[:])
                        P.copy(dstt[:], tp[:, 0:512], eng=("act" if i3 % 2 else "dve"))
                    if RW_STAGE < 3:
                        continue
                    def head_info(w, hi):
                        h = 4 * w + hi
                        c4, e = divmod(h, 2)
                        return h, c4, e, slice(e * 64, (e + 1) * 64)

                    ubc = maskuu[:, :].unsqueeze(1).to_broadcast([128, 2, 256])
                    slbc = masksl[:, :].unsqueeze(1).to_broadcast([128, 4, 128])
                    idbc = ident[:, :].unsqueeze(1).to_broadcast([128, 4, 128])
                    for w in range(2):
                        Pb = [psA[3 * w + 0][:].rearrange("p (a c) -> p a c", a=2), psA[3 * w + 1][:].rearrange("p (a c) -> p a c", a=2)]
                        P2 = psA[3 * w + 2][:].rearrange("p (a c) -> p a c", a=4)
                        for hi in range(4):
                            h, c4, e, pse = head_info(w, hi)
                            P.mm(Pb[e][:, hi // 2, :], Bt[pse, c4, :], AR[pse, c4, :])
                        for e in range(2):
                            P.tt(MB[w][:, e::2, :], Pb[e], ubc, ALU.mult)
                        for hi in (0, 2):
                            h, c4, e, pse = head_info(w, hi)
                            P.mm(P2[:, hi, :], AR[pse, c4, 0:128], Bt[pse, c4, :])
                        for hi in (1, 3, 0, 2):
                            h, c4, e, pse = head_info(w, hi)
                            P.mm(Pb[e][:, hi // 2, :], Kt[pse, c4, :], AR[pse, c4, :])
                        for hi in (1, 3):
                            h, c4, e, pse = head_info(w, hi)
                            P.mm(P2[:, hi, :], AR[pse, c4, 0:128], Bt[pse, c4, :])
                        for e in range(2):
                            P.tt(MKt[w][:, e::2, :], Pb[e], ubc, ALU.mult)
                        P.tt(ATT[w][0][:], P2, slbc, ALU.mult)
                    for j in range(1, 7):
                        for w in range(2):
                            Pb = [psA[3 * w + 0][:].rearrange("p (a c) -> p a c", a=2), psA[3 * w + 1][:].rearrange("p (a c) -> p a c", a=2)]
                            P2 = psA[3 * w + 2][:].rearrange("p (a c) -> p a c", a=4)
                            src_ta = TA[w][(j - 1) % 2]
                            dst_ta = TA[w][j % 2]
                            At = ATT[w][(j - 1) % 2]
                            for hi in range(4):
                                e = hi % 2
                                if j == 1:
                                    P.mm(Pb[e][:, hi // 2, 128:256], At[:, hi, :], MB[w][:, hi, 0:128])
                                    P.mm(P2[:, hi, :], MB[w][:, hi, 0:128], At[:, hi, :])
                                else:
                                    P.mm(Pb[e][:, hi // 2, :], At[:, hi, :], src_ta[:, hi, :])
                                    P.mm(P2[:, hi, :], src_ta[:, hi, 128:256], At[:, hi, :])
                            if j == 1:
                                P.tt(dst_ta[:, :, 0:128], MB[w][:, :, 0:128], idbc, ALU.add)
                            else:
                                for e in range(2):
                                    P.tt(dst_ta[:, e::2, 0:128], Pb[e][:, :, 0:128], src_ta[:, e::2, 0:128], ALU.add)
                            if j < 6:
                                for e in range(2):
                                    P.act(dst_ta[:, e::2, 128:256], Pb[e][:, :, 128:256], AF.Copy)
                            P.act(ATT[w][j % 2][:], P2, AF.Copy)
                    for w in range(2):
                        P2 = psA[3 * w + 2][:].rearrange("p (a c) -> p a c", a=4)
                        for hi in range(4):
                            P.mm(P2[:, hi, :], ATT[w][0][:, hi, :], TA[w][0][:, hi, 0:128])
                        P.tt(Tfw[w][:], P2, TA[w][0][:, :, 0:128], ALU.add)
                    if RW_STAGE < 4:
                        continue
                    W1ps = psA[4][:].rearrange("p (h v) -> p h v", v=64)
                    Ups = psA[5][:].rearrange("p (h v) -> p h v", v=64)
                    Yps = psA[3][:].rearrange("p (h v) -> p h v", v=64)
                    Sps = psA[2][:].rearrange("p (c v) -> p c v", v=128)
                    hinfo = []
                    for h in range(8):
                        c4, e = divmod(h, 2)
                        hinfo.append((h, c4, slice(e * 64, (e + 1) * 64), slice(h * 64, (h + 1) * 64)))
                    for h, c4, pse, hsl in hinfo:
                        P.mm(W1ps[:, h, :], AR[pse, c4, 0:128], Sb[pse, c4, :], start=True, stop=False)
                        P.mm(W1ps[:, h, :], MKt[h // 4][:, h % 4, 0:128], VT[:, hsl], start=False, stop=True)
                    P.act(W1[:], psA[4][:], AF.Copy)
                    for h, c4, pse, hsl in hinfo:
                        P.mm(Ups[:, h, :], Tfw[h // 4][:, h % 4, :], W1[:, hsl])
                    P.copy(UT[:], psA[5][:])
                    for h, c4, pse, hsl in hinfo:
                        P.mm(Yps[:, h, :], AR[pse, c4, 128:256], Sb[pse, c4, :], start=True, stop=False)
                        P.mm(Yps[:, h, :], MB[h // 4][:, h % 4, 128:256], UT[:, hsl], start=False, stop=False)
                        P.mm(Yps[:, h, :], MKt[h // 4][:, h % 4, 128:256], VT[:, hsl], start=False, stop=True)
                    P.act(Ytok[:], psA[3][:], AF.Copy)
                    for c4 in range(4):
                        cs = slice(c4 * 128, (c4 + 1) * 128)
                        P.mm(Sps[:, c4, :], BhT[:, cs], UT[:, cs], start=True, stop=False)
                        P.mm(Sps[:, c4, :], KhT[:, cs], VT[:, cs], start=False, stop=True)
                    for e in range(2):
                        pse = slice(e * 64, (e + 1) * 64)
                        P.tt(Sf[pse, :, :], Sf[pse, :, :], gCt[pse, :].unsqueeze(2).to_broadcast([64, 4, 64]), ALU.mult)
                        P.tt(Sf[pse, :, :], Sf[pse, :, :], Sps[pse, :, e * 64:(e + 1) * 64], ALU.add)
                    P.act(Sb[:], Sf[:], AF.Copy)
                    if RW_STAGE < 5:
                        continue
                    y3 = Ytok[:].rearrange("p (h d) -> p h d", d=64)
                    P.red(sm[:], y3)
                    P.act(yc[:], Ytok[:], AF.Square)
                    P.red(sm2[:], yc[:].rearrange("p (h d) -> p h d", d=64))
                    P.ts(mean[:], sm[:], 1.0 / 64, None, ALU.mult)
                    P.tt(msq[:], mean[:], mean[:], ALU.mult)
                    P.stt(var[:], sm2[:], 1.0 / 64, msq[:], ALU.mult, ALU.subtract)
                    P.ts(var[:], var[:], GN_EPS, None, ALU.add)
                    P.act(var[:], var[:], AF.Sqrt)
                    P.recip(var[:], var[:])
                    P.tt(yc[:].rearrange("p (h d) -> p h d", d=64), y3, mean[:].unsqueeze(2).to_broadcast([128, 8, 64]), ALU.subtract)
                    P.tt(ynb[:].rearrange("p (h d) -> p h d", d=64), yc[:].rearrange("p (h d) -> p h d", d=64),
                         var[:].unsqueeze(2).to_broadcast([128, 8, 64]), ALU.mult)
                    tp = tpb[n % 2]
                    for c4 in range(4):
                        P.tr(tp[:, c4 * 128:(c4 + 1) * 128], ynb[:, c4 * 128:(c4 + 1) * 128], ident[:])
                    for c4 in range(4):
                        yo_ = yo[c4 % 2]
                        P.ts(yo_[:], tp[:, c4 * 128:(c4 + 1) * 128], cols["rwkv_gn_w"][:, c4:c4 + 1], cols["rwkv_gn_b"][:, c4:c4 + 1],
                             ALU.mult, ALU.add)
                        P.tt(yo_[:], yo_[:], bonus[:, c4, :], ALU.add)
                        P.tt(yrT[:, c4, tl], yo_[:], gb[:, c4, :], ALU.mult)
                if dbg is not None:
                    P.dma(dbg["yr"][:, :, :], yrT[:], q="pool")
                P.barrier()
            merge_branch(ctx, hT, yrT, 4, wn["w_br_rwkv"], 4096)

    def mix_phase():
        with _ES() as ctx:
            hT = P.sb([128, 8, S], BF16, "hT", ctx)
            with _ES() as c1:
                norm_to_hT(c1, hT, wn["mix_norm"], tpb)
                P.barrier()
            if MIX_ATTN:
                attn_branch(hT)
            if MIX_RWKV:
                rwkv_branch(hT)
            P.barrier()

    P.barrier()
    if not SKIPFFN:
        ffn_phase(wn["ffn1_norm"], wn["ffn1_w_gate"], wn["ffn1_w_up"], wn["ffn1_w_down"])
    mix_phase()
    if not SKIPFFN:
        ffn_phase(wn["ffn2_norm"], wn["ffn2_w_gate"], wn["ffn2_w_up"], wn["ffn2_w_down"])
    ple_phase()
    P.finish()
    stack.close()
    return nc


_CACHE = {}


import os as _os
DEBUG = bool(_os.environ.get("KDEBUG"))
MIX_ATTN = not _os.environ.get("KNOATTN")
MIX_RWKV = not _os.environ.get("KNORWKV")
SKIPFFN = bool(_os.environ.get("KSKIPFFN"))
ATT_M = int(_os.environ.get("KATT_M", "16"))
ATT_PR = int(_os.environ.get("KATT_PR", "2"))
RWKV_N = int(_os.environ.get("KRWKV_N", "16"))
SKIPMERGE = bool(_os.environ.get("KSKIPMERGE"))
ATT_G = [int(c) for c in _os.environ.get("KATT_G", "012")]
ATT_STAGE = int(_os.environ.get("KATT_STAGE", "9"))
ATT_SUB = int(_os.environ.get("KATT_SUB", "9"))
RW_STAGE = int(_os.environ.get("KRW_STAGE", "9"))
RW_SUB = int(_os.environ.get("KRW_SUB", "9"))


def _consts():
    p = np.arange(128)[:, None]
    c = np.arange(128)[None, :]
    NEG = -30000.0
    onespad = np.zeros((128, 2, 128), np.float32)
    onespad[:, 0, 0:64] = 1.0
    onespad[:, 1, 64:128] = 1.0
    invf = (1.0 / (np.float32(10000.0) ** (np.arange(0, 64, 2, dtype=np.float32) / np.float32(64)))).astype(np.float32)
    bo = np.zeros((128, 128), np.float32)
    bo[0:64, 0:64] = 1.0
    bo[64:128, 64:128] = 1.0
    su = (p < c).astype(np.float32)
    iu = (p <= c).astype(np.float32)
    return {"c_ident": np.eye(128, dtype=np.float32),
            "c_maskc": np.where(p <= c, 0.0, NEG).astype(np.float32),
            "c_maskp": np.where(p >= c, 0.0, NEG).astype(np.float32),
            "c_onespad": onespad,
            "c_invfreq": np.ascontiguousarray(np.broadcast_to(invf[None, :], (128, 32))),
            "c_maskuu": np.concatenate([su, iu], axis=1),
            "c_masksl": (c < p).astype(np.float32),
            "c_blockones": bo}


def kernel(**inputs):
    if "nc" not in _CACHE:
        _CACHE["nc"] = build_program(dbg=({} if DEBUG else None))
    nc = _CACHE["nc"]
    consts = _consts()
    in_maps = []
    for b in range(8):
        m = dict(consts)
        m["x"] = np.ascontiguousarray(inputs["x"][b])
        m["p"] = np.ascontiguousarray(inputs["p"][0, b])
        m["positions"] = np.ascontiguousarray(inputs["positions"][b])
        for k, v in inputs.items():
            if k in ("x", "p", "positions"):
                continue
            a = np.asarray(v)[0]
            m[k] = np.ascontiguousarray(a.reshape(-1) if k == "rwkv_r_k" else a)
        in_maps.append(m)
    res = run_bass_kernel_spmd(nc, in_maps, core_ids=list(range(8)))
    if DEBUG:
        _CACHE["dbg"] = [{k: np.asarray(v) for k, v in r.items() if k.startswith("dbg_")} for r in res.results]
    return np.stack([np.asarray(r["out"]) for r in res.results], axis=0).astype(np.float32)
```

```python
import numpy as np
import concourse.bass as bass
import concourse.mybir as mybir
from concourse.bass_utils import run_bass_kernel_spmd

F32 = mybir.dt.float32
BF16 = mybir.dt.bfloat16
I32 = mybir.dt.int32
AF = mybir.ActivationFunctionType
ALU = mybir.AluOpType
AX = mybir.AxisListType

S = 2048
D = 1024
NT = 16
DFF = 2816
PLE = 256
RMS_EPS = 1e-6


class _Op:
    __slots__ = ("eng", "fn", "dma", "deps", "ddeps", "mark", "cnt", "slot", "val", "idx", "throttle")


class Prog:
    ENGS = ("pe", "act", "dve", "pool", "sync")
    EP = 2000
    NSLOT = 8

    def __init__(self, nc, stack):
        self.nc = nc
        self.stack = stack
        self.ops = []
        self.trk = {}
        self.last = {e: None for e in self.ENGS}
        self.ndma = {e: 0 for e in self.ENGS}
        self.dma_hist = {e: [] for e in self.ENGS}
        self.uid = 0

    def sb(self, shape, dt, name=None, ctx=None):
        self.uid += 1
        nm = f"{name or 't'}_{self.uid}"
        return (ctx or self.stack).enter_context(self.nc.sbuf_tensor(nm, list(shape), dt))

    def ps(self, shape, dt, name=None, ctx=None):
        self.uid += 1
        nm = f"{name or 'p'}_{self.uid}"
        return (ctx or self.stack).enter_context(self.nc.psum_tensor(nm, list(shape), dt))

    @staticmethod
    def _rect(ap):
        a = ap.ap
        off = int(ap.offset)
        pstep = int(a[0][0])
        npart = int(a[0][1])
        if pstep == 0:
            pstep = 1 << 40
        p0 = off // pstep
        f0 = off % pstep
        ext = 1
        for s, c in a[1:]:
            ext += (int(c) - 1) * abs(int(s))
        if "PSUM" in str(ap.space).upper():
            return ap.tensor.name, ((p0 // 32) * 32, ((p0 + npart + 31) // 32) * 32, 0, 1 << 30)
        return ap.tensor.name, (p0, p0 + npart, f0, f0 + ext)

    @staticmethod
    def _ov(a, b):
        return a[0] < b[1] and b[0] < a[1] and a[2] < b[3] and b[2] < a[3]

    @staticmethod
    def _inside(a, b):
        return a[0] >= b[0] and a[1] <= b[1] and a[2] >= b[2] and a[3] <= b[3]

    def _is_tracked(self, ap):
        sp = str(ap.space)
        return ("SB" in sp.upper()) or ("PSUM" in sp.upper())

    def op(self, eng, fn, reads=(), writes=(), dma=False):
        o = _Op()
        o.eng = eng
        o.fn = fn
        o.dma = dma
        o.deps = {}
        o.ddeps = []
        o.mark = False
        o.cnt = 0
        o.throttle = None
        prods = []
        for ap in reads:
            if ap is None or isinstance(ap, (int, float)) or not self._is_tracked(ap):
                continue
            nm, r = self._rect(ap)
            ents = self.trk.setdefault(nm, [])
            is_ps = "PSUM" in str(ap.space).upper()
            for e in ents:
                if self._ov(e[0], r):
                    if e[1] is not None:
                        prods.append(e[1])
                    if is_ps:
                        for rd in e[2]:
                            if rd.eng != eng:
                                prods.append(rd)
                    e[2].append(o)
        for ap in writes:
            if ap is None or not self._is_tracked(ap):
                continue
            nm, r = self._rect(ap)
            ents = self.trk.setdefault(nm, [])
            keep = []
            for e in ents:
                if self._ov(e[0], r):
                    if e[1] is not None:
                        prods.append(e[1])
                    prods.extend(e[2])
                    if self._inside(e[0], r):
                        continue
                keep.append(e)
            keep.append([r, o, []])
            self.trk[nm] = keep
        for p in prods:
            if p is o:
                continue
            if p.dma:
                if p not in o.ddeps:
                    o.ddeps.append(p)
            else:
                if p.eng == "pe" and eng == "pe" and not dma:
                    continue
                cur = o.deps.get(p.eng)
                if cur is None or p.idx > cur.idx:
                    o.deps[p.eng] = p
        if dma:
            i = self.ndma[eng]
            self.ndma[eng] = i + 1
            o.slot = i % self.NSLOT
            o.val = 16 * (i // self.NSLOT + 1)
            if i >= self.NSLOT:
                o.throttle = self.dma_hist[eng][i - self.NSLOT]
            self.dma_hist[eng].append(o)
        else:
            self.last[eng] = o
        o.idx = len(self.ops)
        self.ops.append(o)
        return o

    def barrier(self):
        lasts = [self.last[e] for e in self.ENGS if self.last[e] is not None]
        dm = []
        for e in self.ENGS:
            dm.extend(self.dma_hist[e][-self.NSLOT:])
        for e in self.ENGS:
            o = _Op()
            o.eng = e
            o.fn = None
            o.dma = False
            o.deps = {}
            for p in lasts:
                if p.eng == e and e == "pe":
                    continue
                o.deps[p.eng] = p
            o.ddeps = list(dm)
            o.mark = False
            o.cnt = 0
            o.throttle = None
            o.idx = len(self.ops)
            self.ops.append(o)
        self.trk = {}
        self.last = {e: None for e in self.ENGS}
        self.flush()

    def _csem(self, e, b):
        lst = self.csem.setdefault(e, [])
        while len(lst) <= b:
            lst.append(self.stack.enter_context(self.nc.semaphore(f"c_{e}_{len(lst)}")))
        return lst[b]

    def _dsem(self, e, s):
        if e not in self.dsem:
            self.dsem[e] = [self.stack.enter_context(self.nc.semaphore(f"d_{e}_{i}")) for i in range(self.NSLOT)]
        return self.dsem[e][s]

    def _wait_dma(self, E, e, d):
        key = (d.eng, d.slot)
        if self.dwaited[e].get(key, 0) < d.val:
            E.wait_ge(self._dsem(d.eng, d.slot), d.val)
            self.dwaited[e][key] = d.val

    def flush(self):
        nc = self.nc
        if not hasattr(self, "counts"):
            self.counts = {e: 0 for e in self.ENGS}
            self.csem = {}
            self.dsem = {}
            self.waited = {e: {f: 0 for f in self.ENGS} for e in self.ENGS}
            self.dwaited = {e: {} for e in self.ENGS}
            self.pos = 0
        engobj = {"pe": nc.tensor, "act": nc.scalar, "dve": nc.vector, "pool": nc.gpsimd, "sync": nc.sync}
        batch = self.ops[self.pos:]
        for o in batch:
            for p in o.deps.values():
                assert p.idx >= self.pos, "dependency on an already-flushed op"
                p.mark = True
        for o in batch:
            if o.fn is not None and not o.dma and o.mark:
                self.counts[o.eng] += 1
                o.cnt = self.counts[o.eng]
        for o in batch:
            e = o.eng
            E = engobj[e]
            for f, p in o.deps.items():
                if p.cnt > self.waited[e][f]:
                    b = (p.cnt - 1) // self.EP
                    E.wait_ge(self._csem(f, b), (p.cnt - 1) % self.EP + 1)
                    self.waited[e][f] = p.cnt
            for d in o.ddeps:
                self._wait_dma(E, e, d)
            if o.throttle is not None:
                self._wait_dma(E, e, o.throttle)
            if o.fn is None:
                continue
            ins = o.fn(E)
            if o.dma:
                ins.then_inc(self._dsem(e, o.slot), 16)
            elif o.mark:
                b = (o.cnt - 1) // self.EP
                ins.then_inc(self._csem(e, b), 1)
            o.fn = None
        self.pos = len(self.ops)

    def finish(self):
        self.flush()
        E = self.nc.sync
        for q in self.ENGS:
            for d in self.dma_hist[q][-self.NSLOT:]:
                self._wait_dma(E, "sync", d)

    def mm(self, out, lhsT, rhs, start=True, stop=True):
        return self.op("pe", lambda E: E.matmul(out, lhsT, rhs, start=start, stop=stop),
                       reads=[lhsT, rhs], writes=[out])

    def tr(self, out, in_, ident):
        return self.op("pe", lambda E: E.transpose(out, in_, ident), reads=[in_, ident], writes=[out])

    def act(self, out, in_, func, bias=None, scale=None, accum_out=None, eng="act"):
        kw = {}
        rd = [in_]
        if bias is not None:
            kw["bias"] = bias
            rd.append(bias)
        if scale is not None:
            kw["scale"] = scale
            rd.append(scale)
        wr = [out]
        if accum_out is not None:
            kw["accum_out"] = accum_out
            wr.append(accum_out)
        return self.op(eng, lambda E: E.activation(out, in_, func, **kw), reads=rd, writes=wr)

    def tt(self, out, in0, in1, op, eng="dve"):
        return self.op(eng, lambda E: E.tensor_tensor(out, in0, in1, op), reads=[in0, in1], writes=[out])

    def ts(self, out, in0, s1, s2, op0, op1=None, eng="dve", accum_out=None):
        rd = [in0, s1, s2]
        wr = [out]
        if accum_out is not None:
            wr.append(accum_out)

        def f(E):
            kw = {}
            if accum_out is not None:
                kw["accum_out"] = accum_out
            if op1 is None:
                return E.tensor_scalar(out, in0, s1, None, op0, **kw)
            return E.tensor_scalar(out, in0, s1, s2, op0, op1, **kw)
        return self.op(eng, f, reads=rd, writes=wr)

    def stt(self, out, in0, scalar, in1, op0, op1, eng="dve"):
        return self.op(eng, lambda E: E.scalar_tensor_tensor(out, in0, scalar, in1, op0, op1),
                       reads=[in0, scalar, in1], writes=[out])

    def copy(self, out, in_, eng="dve"):
        if eng == "act":
            return self.op("act", lambda E: E.copy(out, in_), reads=[in_], writes=[out])
        return self.op(eng, lambda E: E.tensor_copy(out, in_), reads=[in_], writes=[out])

    def red(self, out, in_, op=None, eng="dve"):
        op = op or ALU.add
        return self.op(eng, lambda E: E.tensor_reduce(out, in_, AX.X, op), reads=[in_], writes=[out])

    def recip(self, out, in_):
        return self.op("dve", lambda E: E.reciprocal(out, in_), reads=[in_], writes=[out])

    def memset(self, ap, val, eng="dve"):
        return self.op(eng, lambda E: E.memset(ap, val), reads=[], writes=[ap])

    def dma(self, out, in_, q="sync", slow=False):
        kw = {}
        if slow:
            kw["allow_slow_non_contiguous"] = True
        return self.op(q, lambda E: E.dma_start(out=out, in_=in_, **kw), reads=[in_], writes=[out], dma=True)


FF_SPLITS = (4, 4, 4, 4, 3, 3)


def build_program(dbg=None):
    from contextlib import ExitStack
    nc = bass.Bass("TRN2", target_bir_lowering=False)
    stack = ExitStack()
    P = Prog(nc, stack)

    def din(name, shape, dt=F32):
        return nc.dram_tensor(name, list(shape), dt, kind="ExternalInput").ap()

    x_d = din("x", [S, D])
    p_d = din("p", [S, PLE])
    pos_d = din("positions", [S], I32)
    wn = {}
    for nm, shp in (("ffn1_norm", [D]), ("ffn1_w_gate", [D, DFF]), ("ffn1_w_up", [D, DFF]), ("ffn1_w_down", [DFF, D]),
                    ("mix_norm", [D]), ("w_in", [D, 6144]), ("rwkv_mu", [1792]), ("rwkv_w0", [512]),
                    ("rwkv_w2", [64, 512]), ("rwkv_a0", [512]), ("rwkv_a2", [64, 512]), ("rwkv_g2", [128, 512]),
                    ("rwkv_k_k", [512]), ("rwkv_k_a", [512]), ("rwkv_r_k", [512]), ("rwkv_gn_w", [512]),
                    ("rwkv_gn_b", [512]), ("q_norm", [64]), ("k_norm", [64]), ("w_br_rwkv", [512, D]),
                    ("w_br_attn", [256, D]), ("w_out", [D, D]), ("ffn2_norm", [D]), ("ffn2_w_gate", [D, DFF]),
                    ("ffn2_w_up", [D, DFF]), ("ffn2_w_down", [DFF, D]), ("ple_norm", [D]), ("ple_w_gate", [D, D]),
                    ("ple_w_proj", [PLE, D])):
        wn[nm] = din(nm, shp)
    ident_d = din("c_ident", [128, 128])
    cst = {}
    for nm, shp in (("c_maskc", [128, 128]), ("c_maskp", [128, 128]), ("c_onespad", [128, 2, 128]),
                    ("c_invfreq", [128, 32]), ("c_maskuu", [128, 256]), ("c_masksl", [128, 128]),
                    ("c_blockones", [128, 128])):
        cst[nm] = din(nm, shp)
    if dbg is not None:
        dbg = {"ya": nc.dram_tensor("dbg_ya", [128, 2, S], F32, kind="ExternalOutput").ap(),
               "yr": nc.dram_tensor("dbg_yr", [128, 4, S], F32, kind="ExternalOutput").ap()}
    out_d = nc.dram_tensor("out", [S, D], F32, kind="ExternalOutput").ap()

    X = P.sb([128, NT, D], F32, "X")
    ident = P.sb([128, 128], BF16, "ident")
    P.dma(ident[:], ident_d[:, :], q="pool")
    xv = x_d.rearrange("(t p) d -> p t d", p=128)
    for i in range(4):
        P.dma(X[:, 4 * i:4 * i + 4, :], xv[:, 4 * i:4 * i + 4, :], q="sync")

    from contextlib import ExitStack as _ES

    def norm_to_hT(ctx, hT, gain_d, tp_banks):
        gain_bc = P.sb([128, D], F32, "gain_bc", ctx)
        P.dma(gain_bc[:], gain_d.partition_broadcast(128), q="sync")
        ss = P.sb([128, NT], F32, "ss", ctx)
        rstd = P.sb([128, NT], F32, "rstd", ctx)
        junk = P.sb([128, D], BF16, "junk", ctx)
        xn = [P.sb([128, D], BF16, "xn", ctx) for _ in range(2)]
        P.memset(ss[:], 0.0)
        for t in range(NT):
            P.act(junk[:], X[:, t, :], AF.Square, accum_out=ss[:, t:t + 1])
        P.ts(rstd[:], ss[:], 1.0 / D, RMS_EPS, ALU.mult, ALU.add)
        P.act(rstd[:], rstd[:], AF.Sqrt)
        P.recip(rstd[:], rstd[:])
        for t in range(NT):
            xt = xn[t % 2]
            P.stt(xt[:], X[:, t, :], rstd[:, t:t + 1], gain_bc[:], ALU.mult, ALU.mult)
            pt = tp_banks[t % 2]
            for kc in range(8):
                P.tr(pt[:, kc * 128:(kc + 1) * 128], xt[:, kc * 128:(kc + 1) * 128], ident[:])
            P.copy(hT[:, :, t * 128:(t + 1) * 128], pt[:].rearrange("p (k c) -> p k c", k=8),
                   eng=("act" if t % 2 else "dve"))

    def ffn(ctx, hT, wg_d, wu_d, wd_d, psum):
        wgv = wg_d.rearrange("(k p) c -> p k c", p=128)
        wuv = wu_d.rearrange("(k p) c -> p k c", p=128)
        wdv = wd_d.rearrange("(f p) d -> p f d", p=128)
        Wg = [P.sb([128, 8, 512], BF16, "Wg", ctx) for _ in range(2)]
        Wu = [P.sb([128, 8, 512], BF16, "Wu", ctx) for _ in range(2)]
        Wd = [P.sb([128, 4, D], BF16, "Wd", ctx) for _ in range(2)]
        aT = [P.sb([128, 4, S], BF16, "aT", ctx) for _ in range(2)]
        sg = [P.sb([128, 512], BF16, "sg", ctx) for _ in range(2)]
        pg, pu, pd = psum["g"], psum["u"], psum["d"]
        offs = []
        f0 = 0
        for nf in FF_SPLITS:
            offs.append((f0, nf))
            f0 += nf
        cnt = {"gu": 0, "d": 0}

        def load(si):
            f0, nf = offs[si]
            sl = si % 2
            c0, c1 = f0 * 128, (f0 + nf) * 128
            P.dma(Wg[sl][:, :, 0:nf * 128], wgv[:, :, c0:c1], q="pool")
            P.dma(Wu[sl][:, :, 0:nf * 128], wuv[:, :, c0:c1], q="pool")
            P.dma(Wd[sl][:, 0:nf, :], wdv[:, f0:f0 + nf, :], q="pool")

        def gu(si):
            f0, nf = offs[si]
            sl = si % 2
            for fl in range(nf):
                for tg in range(4):
                    i = cnt["gu"] % 2
                    cnt["gu"] += 1
                    for kc in range(8):
                        P.mm(pg[i][:], Wg[sl][:, kc, fl * 128:(fl + 1) * 128], hT[:, kc, tg * 512:(tg + 1) * 512],
                             start=(kc == 0), stop=(kc == 7))
                    for kc in range(8):
                        P.mm(pu[i][:], Wu[sl][:, kc, fl * 128:(fl + 1) * 128], hT[:, kc, tg * 512:(tg + 1) * 512],
                             start=(kc == 0), stop=(kc == 7))
                    P.act(sg[i][:], pg[i][:], AF.Silu)
                    P.tt(aT[sl][:, fl, tg * 512:(tg + 1) * 512], sg[i][:], pu[i][:], ALU.mult)

        def down(si):
            f0, nf = offs[si]
            sl = si % 2
            for t in range(NT):
                for dh in range(2):
                    i = cnt["d"] % 2
                    cnt["d"] += 1
                    for fl in range(nf):
                        P.mm(pd[i][:], aT[sl][:, fl, t * 128:(t + 1) * 128], Wd[sl][:, fl, dh * 512:(dh + 1) * 512],
                             start=(fl == 0), stop=(fl == nf - 1))
                    xs = X[:, t, dh * 512:(dh + 1) * 512]
                    P.stt(xs, pd[i][:], 0.5, xs, ALU.mult, ALU.add)

        ns = len(offs)
        load(0)
        load(1)
        gu(0)
        for si in range(1, ns):
            gu(si)
            down(si - 1)
            if si + 1 < ns:
                load(si + 1)
        down(ns - 1)

    psA = [P.ps([128, 512], F32, "psA") for _ in range(6)]
    tpb = [P.ps([128, 1024], BF16, "tpb") for _ in range(2)]
    psum_ffn = {"g": psA[0:2], "u": psA[2:4], "d": psA[4:6]}

    def ffn_phase(norm_d, wg_d, wu_d, wd_d):
        with _ES() as ctx:
            hT = P.sb([128, 8, S], BF16, "hT", ctx)
            norm_to_hT(ctx, hT, norm_d, tpb)
            ffn(ctx, hT, wg_d, wu_d, wd_d, psum_ffn)
            P.barrier()

    def ple_phase():
        with _ES() as ctx:
            hT = P.sb([128, 8, S], BF16, "hT", ctx)
            Wpg = P.sb([128, 8, D], BF16, "Wpg", ctx)
            Wpp = P.sb([128, 2, D], BF16, "Wpp", ctx)
            P.dma(Wpg[:], wn["ple_w_gate"].rearrange("(k p) c -> p k c", p=128), q="pool")
            P.dma(Wpp[:], wn["ple_w_proj"].rearrange("(k p) c -> p k c", p=128), q="pool")
            pin = P.sb([128, NT, PLE], BF16, "pin", ctx)
            P.dma(pin[:], p_d.rearrange("(t p) c -> p t c", p=128), q="pool")
            pT = P.sb([128, 2, S], BF16, "pT", ctx)
            norm_to_hT(ctx, hT, wn["ple_norm"], tpb)
            for t in range(NT):
                pt = tpb[t % 2]
                for k in range(2):
                    P.tr(pt[:, k * 128:(k + 1) * 128], pin[:, t, k * 128:(k + 1) * 128], ident[:])
                P.copy(pT[:, :, t * 128:(t + 1) * 128], pt[:, 0:256].rearrange("p (k c) -> p k c", k=2),
                       eng=("act" if t % 2 else "dve"))
            sgs = [P.sb([128, 512], F32, "sgs", ctx) for _ in range(2)]
            n = 0
            ov = out_d.rearrange("(t p) d -> p t d", p=128)
            for t in range(NT):
                for dh in range(2):
                    i = n % 2
                    n += 1
                    pG, pP = psA[i], psA[2 + i]
                    for kc in range(8):
                        P.mm(pG[:], hT[:, kc, t * 128:(t + 1) * 128], Wpg[:, kc, dh * 512:(dh + 1) * 512],
                             start=(kc == 0), stop=(kc == 7))
                    for k in range(2):
                        P.mm(pP[:], pT[:, k, t * 128:(t + 1) * 128], Wpp[:, k, dh * 512:(dh + 1) * 512],
                             start=(k == 0), stop=(k == 1))
                    P.act(sgs[i][:], pG[:], AF.Sigmoid)
                    P.tt(sgs[i][:], sgs[i][:], pP[:], ALU.mult)
                    xs = X[:, t, dh * 512:(dh + 1) * 512]
                    P.tt(xs, xs, sgs[i][:], ALU.add)
                P.dma(ov[:, t, :], X[:, t, :], q="sync")
            P.barrier()


    TWO_PI = 6.283185307179586
    C1 = 6.28125
    C2 = TWO_PI - C1
    PI = 3.141592653589793

    def run_interleaved(gens):
        gens = list(gens)
        while gens:
            for g_ in list(gens):
                try:
                    next(g_)
                except StopIteration:
                    gens.remove(g_)

    def tokslice(hT, kc, t0, d):
        if d == 1:
            return hT[:, kc, t0:t0 + 128]
        return hT[:, kc, t0:t0 + 127 * d + 1:d]

    def merge_branch(ctx0, hT, yT, nyc, wbr_d, gate_col0):
        if SKIPMERGE:
            return
        with _ES() as ctx:
            Wbr = P.sb([128, nyc, D], BF16, "Wbr", ctx)
            P.dma(Wbr[:], wbr_d.rearrange("(c p) d -> p c d", p=128), q="pool")
            Wg = P.sb([128, 8, D], BF16, "Wgt", ctx)
            wiv = wn["w_in"].rearrange("(k p) c -> p k c", p=128)
            P.dma(Wg[:], wiv[:, :, gate_col0:gate_col0 + D], q="pool")
            Wo = P.sb([128, 8, D], BF16, "Wo", ctx)
            P.dma(Wo[:], wn["w_out"].rearrange("(k p) c -> p k c", p=128), q="pool")
            mT = P.sb([128, 8, S], BF16, "mT", ctx)
            sgm = [P.sb([128, 512], F32, "sgm", ctx) for _ in range(2)]
            n = 0
            for dc in range(8):
                for tg in range(4):
                    i = n % 2
                    n += 1
                    pG, pB = psA[i], psA[2 + i]
                    ts_ = slice(tg * 512, (tg + 1) * 512)
                    for kc in range(8):
                        P.mm(pG[:], Wg[:, kc, dc * 128:(dc + 1) * 128], hT[:, kc, ts_], start=(kc == 0), stop=(kc == 7))
                    for c in range(nyc):
                        P.mm(pB[:], Wbr[:, c, dc * 128:(dc + 1) * 128], yT[:, c, ts_], start=(c == 0), stop=(c == nyc - 1))
                    P.act(sgm[i][:], pG[:], AF.Sigmoid)
                    P.tt(mT[:, dc, ts_], sgm[i][:], pB[:], ALU.mult)
            n = 0
            for t in range(NT):
                for dh in range(2):
                    pd_ = psA[4 + n % 2]
                    n += 1
                    for dc in range(8):
                        P.mm(pd_[:], mT[:, dc, t * 128:(t + 1) * 128], Wo[:, dc, dh * 512:(dh + 1) * 512],
                             start=(dc == 0), stop=(dc == 7))
                    xs = X[:, t, dh * 512:(dh + 1) * 512]
                    P.tt(xs, xs, pd_[:], ALU.add)
            P.barrier()

    def attn_branch(hT):
        with _ES() as ctx:
            yaT = P.sb([128, 2, S], BF16, "yaT", ctx)
            with _ES() as c2:
                maskC = P.sb([128, 128], BF16, "maskC", c2)
                maskP = P.sb([128, 128], BF16, "maskP", c2)
                onesp = P.sb([128, 2, 128], BF16, "onesp", c2)
                P.dma(maskC[:], cst["c_maskc"][:, :], q="pool")
                P.dma(maskP[:], cst["c_maskp"][:, :], q="pool")
                P.dma(onesp[:], cst["c_onespad"][:, :, :], q="pool")
                invf = P.sb([128, 32], F32, "invf", c2)
                P.dma(invf[:], cst["c_invfreq"][:, :], q="sync")
                gain4 = P.sb([128, 4, 64], F32, "gain4", c2)
                for hh in range(4):
                    P.dma(gain4[:, hh, :], (wn["q_norm"] if hh < 2 else wn["k_norm"]).partition_broadcast(128), q="sync")
                cosT = [P.sb([128, 16 * 32], F32, "cosT", c2) for _ in range(3)]
                sinT = [P.sb([128, 16 * 32], F32, "sinT", c2) for _ in range(3)]
                with _ES() as c3:
                    posi = P.sb([128, 16], I32, "posi", c3)
                    posf = P.sb([128, 16], F32, "posf", c3)
                    ang = P.sb([128, 512], F32, "ang", c3)
                    aa = P.sb([128, 512], F32, "aa", c3)
                    kf = P.sb([128, 512], F32, "kf", c3)
                    ki = P.sb([128, 512], I32, "ki", c3)
                    rr = P.sb([128, 512], F32, "rr", c3)
                    mk = P.sb([128, 512], F32, "mk", c3)
                    for g, d in enumerate((1, 4, 16)):
                        pv = pos_d.rearrange("(ib p j) -> p j ib", p=128, j=d)
                        nib = 16 // d
                        for m in range(16):
                            P.dma(posi[:, m:m + 1], pv[:, m // nib, (m % nib):(m % nib) + 1], q="sync", slow=True)
                        P.copy(posf[:], posi[:])
                        for m in range(16):
                            P.ts(ang[:, m * 32:(m + 1) * 32], invf[:], posf[:, m:m + 1], None, ALU.mult)
                        for dst, shift in ((sinT[g], 0.0), (cosT[g], PI / 2)):
                            P.ts(aa[:], ang[:], shift, None, ALU.add)
                            P.ts(kf[:], aa[:], 1.0 / TWO_PI, None, ALU.mult)
                            P.copy(ki[:], kf[:])
                            P.copy(kf[:], ki[:])
                            P.stt(rr[:], kf[:], -C1, aa[:], ALU.mult, ALU.add)
                            P.stt(rr[:], kf[:], -C2, rr[:], ALU.mult, ALU.add)
                            P.ts(mk[:], rr[:], PI, None, ALU.is_gt)
                            P.stt(rr[:], mk[:], -TWO_PI, rr[:], ALU.mult, ALU.add)
                            P.ts(mk[:], rr[:], -PI, None, ALU.is_lt)
                            P.stt(rr[:], mk[:], TWO_PI, rr[:], ALU.mult, ALU.add)
                            P.act(dst[:], rr[:], AF.Sin)
                    P.barrier()
                Wq = [[P.sb([128, 8, 128], BF16, "Wq", c2) for _ in range(3)] for _ in range(2)]
                qkT = P.sb([128, 2, S], BF16, "qkT", c2)
                Vpad = P.sb([128, 16, 2, 128], BF16, "Vpad", c2)
                acc = P.sb([128, 2, S], F32, "acc", c2)
                abuf = []
                for _i in range(2):
                    abuf.append((P.sb([128, 4, 256], F32, "qkb", c2), P.sb([128, 4, 256], F32, "qnb", c2),
                                 P.sb([128, 16], F32, "ss16", c2), P.sb([128, 512], F32, "t1", c2),
                                 P.sb([128, 512], F32, "t2", c2), P.sb([128, 512], F32, "t3", c2),
                                 P.sb([128, 512], F32, "t4", c2), P.sb([128, 4, 256], BF16, "qrb", c2)))
                PT = [P.sb([128, 4, 128], BF16, "PT", c2) for _ in range(2)]
                if _os.environ.get('KPRINTMEM'):
                    print('ATTN sbuf remaining', nc.sbuf_bytes_remaining)
                P.memset(Vpad[:], 0.0)
                wiv = wn["w_in"].rearrange("(k p) c -> p k c", p=128)
                it = 0
                for pr in range(ATT_PR):
                    for g, d in enumerate((1, 4, 16)):
                        if g not in ATT_G:
                            continue
                        L = S // d
                        W = Wq[it % 2]
                        it += 1
                        qc0 = 1792 + (g * 4 + pr * 2) * 64
                        for j3 in range(3):
                            P.dma(W[j3][:], wiv[:, :, qc0 + 768 * j3:qc0 + 768 * j3 + 128], q="pool")
                        TB = 4

                        def st1_chain(mb, bs):
                            qkb, qnb, ss16, t1, t2, t3, t4, qrb = abuf[bs]
                            bank = psA[3 + bs]
                            for mi in range(TB):
                                m = mb + mi
                                j, i0 = divmod(m * 128, L)
                                t0 = i0 * d + j
                                for j3 in range(3):
                                    for kc in range(8):
                                        P.mm(bank[:, j3 * 128:(j3 + 1) * 128], tokslice(hT, kc, t0, d), W[j3][:, kc, :],
                                             start=(kc == 0), stop=(kc == 7))
                                    yield
                                P.act(qkb[:, mi, :], bank[:, 0:256], AF.Copy)
                                yield
                                P.copy(Vpad[:, m, 0, 0:64], bank[:, 256:320])
                                yield
                                P.copy(Vpad[:, m, 1, 64:128], bank[:, 320:384])
                                yield
                            P.tt(qnb[:], qkb[:], qkb[:], ALU.mult)
                            yield
                            P.red(ss16[:], qnb[:].rearrange("p a (h d) -> p (a h) d", d=64))
                            yield
                            P.ts(ss16[:], ss16[:], 1.0 / 64, RMS_EPS, ALU.mult, ALU.add)
                            yield
                            P.act(ss16[:], ss16[:], AF.Sqrt)
                            yield
                            P.recip(ss16[:], ss16[:])
                            yield
                            P.tt(qnb[:].rearrange("p a (h d) -> p (a h) d", d=64), qkb[:].rearrange("p a (h d) -> p (a h) d", d=64),
                                 ss16[:].unsqueeze(2).to_broadcast([128, 16, 64]), ALU.mult)
                            yield
                            P.tt(qnb[:], qnb[:], gain4[:].rearrange("p h d -> p (h d)").unsqueeze(1).to_broadcast([128, TB, 256]), ALU.mult)
                            yield
                            q5 = qnb[:].rearrange("p a (h t i) -> p a h t i", t=2, i=32)
                            x1, x2 = q5[:, :, :, 0, :], q5[:, :, :, 1, :]
                            cb = cosT[g][:, mb * 32:(mb + TB) * 32].rearrange("p (a i) -> p a i", i=32).unsqueeze(2).to_broadcast([128, TB, 4, 32])
                            sb_ = sinT[g][:, mb * 32:(mb + TB) * 32].rearrange("p (a i) -> p a i", i=32).unsqueeze(2).to_broadcast([128, TB, 4, 32])
                            v4 = lambda tl: tl[:].rearrange("p (a h i) -> p a h i", h=4, i=32)
                            P.tt(v4(t1), x1, cb, ALU.mult)
                            yield
                            P.tt(v4(t2), x2, sb_, ALU.mult)
                            yield
                            P.tt(v4(t3), x2, cb, ALU.mult)
                            yield
                            P.tt(v4(t4), x1, sb_, ALU.mult)
                            yield
                            r5 = qrb[:].rearrange("p a (h t i) -> p a h t i", t=2, i=32)
                            P.tt(r5[:, :, :, 0, :], v4(t1), v4(t2), ALU.subtract)
                            yield
                            P.tt(r5[:, :, :, 1, :], v4(t3), v4(t4), ALU.add)
                            yield
                            tp = tpb[bs]
                            for mi in range(TB):
                                for a in range(2):
                                    P.tr(tp[:, a * 512 + mi * 128:a * 512 + (mi + 1) * 128], qrb[:, mi, a * 128:(a + 1) * 128], ident[:])
                                yield
                            P.copy(qkT[:, :, mb * 128:(mb + TB) * 128], tp[:].rearrange("p (a c) -> p a c", a=2), eng="act")
                            yield

                        for mb2 in range(0, 16, 2 * TB):
                            run_interleaved([st1_chain(mb2, 0), st1_chain(mb2 + TB, 1)])
                        def blk_scores(m):
                            j, i0 = divmod(m * 128, L)
                            has_prev = (i0 // 128) > 0
                            sc = psA[m % 2][:].rearrange("p (a c) -> p a c", a=4)
                            qs = slice(m * 128, (m + 1) * 128)
                            for e in range(2):
                                pse = slice(e * 64, (e + 1) * 64)
                                P.mm(sc[:, e, :], qkT[pse, 1, qs], qkT[pse, 0, qs], start=True, stop=False)
                                P.mm(sc[:, e, :], ident[:], maskC[:], start=False, stop=True)
                                if has_prev:
                                    P.mm(sc[:, 2 + e, :], qkT[pse, 1, (m - 1) * 128:m * 128], qkT[pse, 0, qs], start=True, stop=False)
                                    P.mm(sc[:, 2 + e, :], ident[:], maskP[:], start=False, stop=True)
                            nk = 4 if has_prev else 2
                            P.act(PT[m % 2][:, 0:nk, :], sc[:, 0:nk, :], AF.Exp, scale=0.125)

                        def blk_ol(m):
                            j, i0 = divmod(m * 128, L)
                            t0 = i0 * d + j
                            has_prev = (i0 // 128) > 0
                            nk = 4 if has_prev else 2
                            pt_ = PT[m % 2]
                            OL = psA[2][:, (m % 2) * 256:(m % 2) * 256 + 256].rearrange("p (a c) -> p a c", a=2)
                            seq = [(kb, e) for kb in range(nk // 2) for e in range(2)]
                            for ii, (kb, e) in enumerate(seq):
                                P.mm(OL[:, 0, :], Vpad[:, m - kb, e, :], pt_[:, kb * 2 + e, :], start=(ii == 0), stop=(ii == len(seq) - 1))
                            for ii, (kb, e) in enumerate(seq):
                                P.mm(OL[:, 1, :], onesp[:, e, :], pt_[:, kb * 2 + e, :], start=(ii == 0), stop=(ii == len(seq) - 1))
                            if d == 1:
                                av = acc[:, :, t0:t0 + 128]
                            else:
                                av = acc[:, :, t0:t0 + 127 * d + 1:d]
                            if g == 0:
                                P.copy(av, OL)
                            else:
                                P.tt(av, av, OL, ALU.add)

                        if ATT_STAGE >= 2:
                            for m in range(17):
                                if m < 16:
                                    blk_scores(m)
                                if m >= 1:
                                    blk_ol(m - 1)
                    P.recip(acc[:, 1, :], acc[:, 1, :])
                    P.tt(yaT[:, pr, :], acc[:, 0, :], acc[:, 1, :], ALU.mult)
                if dbg is not None:
                    P.dma(dbg["ya"][:, :, :], yaT[:], q="pool")
                P.barrier()
            merge_branch(ctx, hT, yaT, 2, wn["w_br_attn"], 4096 + 1024)

    def rwkv_branch(hT):
        with _ES() as ctx:
            yrT = P.sb([128, 4, S], BF16, "yrT", ctx)
            with _ES() as c2:
                sbt = lambda shp, dt, nm: P.sb(shp, dt, nm, c2)
                maskuu = sbt([128, 256], BF16, "maskuu")
                masksl = sbt([128, 128], BF16, "masksl")
                bones = sbt([128, 128], BF16, "bones")
                P.dma(maskuu[:], cst["c_maskuu"][:, :], q="pool")
                P.dma(masksl[:], cst["c_masksl"][:, :], q="pool")
                P.dma(bones[:], cst["c_blockones"][:, :], q="pool")
                muT = sbt([128, 14], F32, "muT")
                for c in range(14):
                    P.dma(muT[:, c:c + 1], wn["rwkv_mu"][c * 128:(c + 1) * 128].rearrange("(p o) -> p o", o=1), q="sync", slow=True)
                cols = {}
                for nm in ("rwkv_w0", "rwkv_a0", "rwkv_k_k", "rwkv_k_a", "rwkv_r_k", "rwkv_gn_w", "rwkv_gn_b"):
                    cols[nm] = sbt([128, 4], F32, nm)
                    for c in range(4):
                        P.dma(cols[nm][:, c:c + 1], wn[nm][c * 128:(c + 1) * 128].rearrange("(p o) -> p o", o=1), q="sync", slow=True)
                omk = sbt([128, 4], F32, "omk")
                P.ts(omk[:], cols["rwkv_k_a"][:], -1.0, 1.0, ALU.mult, ALU.add)
                w2b = sbt([128, 512], BF16, "w2b")
                a2b = sbt([128, 512], BF16, "a2b")
                g2b = sbt([128, 512], BF16, "g2b")
                P.dma(w2b[0:64, :], wn["rwkv_w2"][:, :], q="pool")
                P.dma(a2b[64:128, :], wn["rwkv_a2"][:, :], q="pool")
                P.dma(g2b[:], wn["rwkv_g2"][:, :], q="pool")
                Wst = [sbt([128, 8, 256], BF16, "Wst") for _ in range(2)]
                zt = sbt([128, 14, 128], F32, "zt")
                zcb = [sbt([128, 129], F32, "zc") for _ in range(4)]
                dzb = [sbt([128, 128], F32, "dzb") for _ in range(4)]
                carry = sbt([128, 14], F32, "carry")
                P.memset(carry[:], 0.0)
                ones_b = sbt([128, 128], BF16, "ones_b")
                P.memset(ones_b[:], 1.0)
                th = sbt([128, 128], BF16, "th")
                adb = sbt([128, 128], BF16, "adb")
                sgd = sbt([128, 128], BF16, "sgd")
                f32t = lambda nm: sbt([128, 128], F32, nm)
                NSET = 4
                tset = []
                for _i in range(NSET):
                    tset.append({nm: f32t(nm) for nm in ("s_", "a_", "L0", "E1", "E2", "E3", "E4", "kk", "bv", "tA", "kp", "rn")})
                    tset[-1]["kk2"] = sbt([128, 128], BF16, "kk2")
                    tset[-1]["t2b"] = sbt([128, 128], BF16, "t2b")
                gCt = sbt([128, 4], F32, "gCt")
                AR = sbt([128, 4, 256], BF16, "AR")
                Bt = sbt([128, 4, 128], BF16, "Bt")
                Kt = sbt([128, 4, 128], BF16, "Kt")
                gb = sbt([128, 4, 128], BF16, "gb")
                bonus = sbt([128, 4, 128], F32, "bonus")
                Bh = sbt([128, 4, 128], BF16, "Bh")
                Kh = sbt([128, 4, 128], BF16, "Kh")
                vb = sbt([128, 4, 128], BF16, "vb")
                VT = sbt([128, 512], BF16, "VT")
                KhT = sbt([128, 512], BF16, "KhT")
                BhT = sbt([128, 512], BF16, "BhT")
                Ytok = sbt([128, 512], F32, "Ytok")
                yc = sbt([128, 512], F32, "yc")
                ynb = sbt([128, 512], BF16, "ynb")
                sm = sbt([128, 8], F32, "sm"); sm2 = sbt([128, 8], F32, "sm2"); mean = sbt([128, 8], F32, "mean")
                msq = sbt([128, 8], F32, "msq"); var = sbt([128, 8], F32, "var")
                yo = [f32t("yo") for _ in range(2)]
                MB = [sbt([128, 4, 256], BF16, "MB") for _ in range(2)]
                MKt = [sbt([128, 4, 256], BF16, "MKt") for _ in range(2)]
                Tfw = [sbt([128, 4, 128], BF16, "Tfw") for _ in range(2)]
                TA = [[sbt([128, 4, 256], BF16, "TA") for _ in range(2)] for _ in range(2)]
                ATT = [[sbt([128, 4, 128], BF16, "ATT") for _ in range(2)] for _ in range(2)]
                Sf = sbt([128, 4, 64], F32, "Sf")
                Sb = sbt([128, 4, 64], BF16, "Sb")
                W1 = sbt([128, 512], BF16, "W1")
                UT = sbt([128, 512], BF16, "UT")
                if _os.environ.get('KPRINTMEM'):
                    print('RWKV sbuf remaining', nc.sbuf_bytes_remaining)
                P.memset(Sf[:], 0.0)
                P.memset(Sb[:], 0.0)
                wiv = wn["w_in"].rearrange("(k p) c -> p k c", p=128)
                GN_EPS = 64e-5
                for n in range(RWKV_N):
                    tl = slice(n * 128, (n + 1) * 128)
                    for pc in range(7):
                        Wp = Wst[pc % 2]
                        P.dma(Wp[:], wiv[:, :, pc * 256:(pc + 1) * 256], q="pool")
                        for cc in range(2):
                            c = 2 * pc + cc
                            for kc in range(8):
                                P.mm(psA[c // 4][:, (c % 4) * 128:(c % 4 + 1) * 128], Wp[:, kc, cc * 128:(cc + 1) * 128], hT[:, kc, tl],
                                     start=(kc == 0), stop=(kc == 7))
                    def r1_chain(c):
                        zc = zcb[c % 4]
                        dz_ = dzb[c % 4]
                        pz = psA[c // 4][:, (c % 4) * 128:(c % 4 + 1) * 128]
                        P.copy(zc[:, 0:1], carry[:, c:c + 1])
                        yield
                        P.act(zc[:, 1:129], pz, AF.Copy)
                        yield
                        P.copy(carry[:, c:c + 1], zc[:, 128:129])
                        yield
                        P.tt(dz_[:], zc[:, 0:128], zc[:, 1:129], ALU.subtract)
                        yield
                        P.stt(zt[:, c, :], dz_[:], muT[:, c:c + 1], zc[:, 1:129], ALU.mult, ALU.add)
                        yield
                    for base in range(0, 14, 4):
                        run_interleaved([r1_chain(c) for c in range(base, min(base + 4, 14))])
                    if RW_STAGE < 2:
                        continue
                    P.act(th[0:64, :], zt[0:64, 12, :], AF.Tanh)
                    P.act(adb[64:128, :], zt[64:128, 12, :], AF.Copy)
                    P.act(sgd[:], zt[:, 13, :], AF.Sigmoid)
                    def r2_chain(c4):
                            T_ = tset[c4 % NSET]
                            s_, a_, L0, E1, E2, E3, E4 = T_["s_"], T_["a_"], T_["L0"], T_["E1"], T_["E2"], T_["E3"], T_["E4"]
                            kk, bv, tA, kp, rn, kk2, t2b = T_["kk"], T_["bv"], T_["tA"], T_["kp"], T_["rn"], T_["kk2"], T_["t2b"]
                            ld = s_
                            Lx = E3
                            kkn = kk
                            r_ = zt[:, c4, :]
                            k_ = zt[:, 4 + c4, :]
                            v_ = zt[:, 8 + c4, :]
                            cs = slice(c4 * 128, (c4 + 1) * 128)
                            pU = psA[0][:, cs]; pAa = psA[1][:, cs]; pG = psA[2][:, cs]; pN = psA[3][:, cs]
                            pBn = psA[4][:, cs]
                            P.mm(pU, w2b[0:64, cs], th[0:64, :])
                            yield
                            P.mm(pAa, a2b[64:128, cs], adb[64:128, :])
                            yield
                            P.mm(pG, g2b[:, cs], sgd[:])
                            yield
                            P.act(s_[:], pU, AF.Sigmoid, bias=cols["rwkv_w0"][:, c4:c4 + 1])
                            yield
                            P.act(a_[:], pAa, AF.Sigmoid, bias=cols["rwkv_a0"][:, c4:c4 + 1])
                            yield
                            P.act(gb[:, c4, :], pG, AF.Copy)
                            yield
                            if RW_SUB < 2:
                                return
                            P.ts(ld[:], s_[:], -0.6065306597126334, None, ALU.mult)
                            yield
                            P.op("dve", lambda E, o_=L0[:], d_=ld[:]: E.tensor_tensor_scan(o_, ones_b[:], d_, 0.0, ALU.mult, ALU.add),
                                 reads=[ones_b[:], ld[:]], writes=[L0[:]])
                            yield
                            Lc = L0
                            P.act(E1[:], Lc[:], AF.Exp)
                            yield
                            P.act(E2[:], Lc[:], AF.Exp, scale=-1.0)
                            yield
                            P.tt(Lx[:], Lc[:], ld[:], ALU.subtract)
                            yield
                            P.act(E3[:], Lx[:], AF.Exp)
                            yield
                            P.act(E4[:], Lc[:], AF.Exp, scale=-1.0, bias=Lc[:, 127:128])
                            yield
                            P.copy(gCt[:, c4:c4 + 1], E1[:, 127:128])
                            yield
                            if RW_SUB < 3:
                                return
                            P.ts(kk[:], k_, cols["rwkv_k_k"][:, c4:c4 + 1], None, ALU.mult)
                            yield
                            P.tt(kk2[:], kk[:], kk[:], ALU.mult)
                            yield
                            P.mm(pN, bones[:], kk2[:])
                            yield
                            P.ts(rn[:], pN, 1e-24, None, ALU.max)
                            yield
                            P.act(rn[:], rn[:], AF.Sqrt)
                            yield
                            P.recip(rn[:], rn[:])
                            yield
                            P.tt(kkn[:], kk[:], rn[:], ALU.mult)
                            yield
                            P.tt(bv[:], kkn[:], a_[:], ALU.mult)
                            yield
                            if RW_SUB < 4:
                                return
                            P.ts(tA[:], a_[:], cols["rwkv_k_a"][:, c4:c4 + 1], omk[:, c4:c4 + 1], ALU.mult, ALU.add)
                            yield
                            P.tt(kp[:], k_, tA[:], ALU.mult)
                            yield
                            P.tt(tA[:], r_, kp[:], ALU.mult)
                            yield
                            P.ts(t2b[:], tA[:], cols["rwkv_r_k"][:, c4:c4 + 1], None, ALU.mult)
                            yield
                            P.mm(pBn, bones[:], t2b[:])
                            yield
                            P.tt(bonus[:, c4, :], pBn, v_, ALU.mult)
                            yield
                            if RW_SUB < 5:
                                return
                            P.stt(AR[:, c4, 0:128], kkn[:], -1.0, E3[:], ALU.mult, ALU.mult)
                            yield
                            P.tt(AR[:, c4, 128:256], r_, E1[:], ALU.mult)
                            yield
                            P.tt(Bt[:, c4, :], bv[:], E2[:], ALU.mult)
                            yield
                            P.tt(Kt[:, c4, :], kp[:], E2[:], ALU.mult)
                            yield
                            P.tt(Bh[:, c4, :], bv[:], E4[:], ALU.mult)
                            yield
                            P.tt(Kh[:, c4, :], kp[:], E4[:], ALU.mult)
                            yield
                            P.copy(vb[:, c4, :], v_, eng="act")
                            yield
                    gens_ = [r2_chain(c4) for c4 in range(4)]
                    while gens_:
                        for g_ in list(gens_):
                            try:
                                next(g_)
                            except StopIteration:
                                gens_.remove(g_)
                    if RW_SUB < 6:
                        continue
                    for i3, (srcf, dstt) in enumerate(((vb, VT), (Kh, KhT), (Bh, BhT))):
                        tp = tpb[i3 % 2]
                        for c4 in range(4):
                            P.tr(tp[:, c4 * 128:(c4 + 1) * 128], srcf[:, c4, :], ident[:])
                        P.copy(dstt[:], tp[:, 0:512], eng=("act" if i3 % 2 else "dve"))
                    if RW_STAGE < 3:
                        continue
                    def head_info(w, hi):
                        h = 4 * w + hi
                        c4, e = divmod(h, 2)
                        return h, c4, e, slice(e * 64, (e + 1) * 64)

                    ubc = maskuu[:, :].unsqueeze(1).to_broadcast([128, 2, 256])
                    slbc = masksl[:, :].unsqueeze(1).to_broadcast([128, 4, 128])
                    idbc = ident[:, :].unsqueeze(1).to_broadcast([128, 4, 128])
                    for w in range(2):
                        Pb = [psA[3 * w + 0][:].rearrange("p (a c) -> p a c", a=2), psA[3 * w + 1][:].rearrange("p (a c) -> p a c", a=2)]
                        P2 = psA[3 * w + 2][:].rearrange("p (a c) -> p a c", a=4)
                        for hi in range(4):
                            h, c4, e, pse = head_info(w, hi)
                            P.mm(Pb[e][:, hi // 2, :], Bt[pse, c4, :], AR[pse, c4, :])
                        for e in range(2):
                            P.tt(MB[w][:, e::2, :], Pb[e], ubc, ALU.mult)
                        for hi in (0, 2):
                            h, c4, e, pse = head_info(w, hi)
                            P.mm(P2[:, hi, :], AR[pse, c4, 0:128], Bt[pse, c4, :])
                        for hi in (1, 3, 0, 2):
                            h, c4, e, pse = head_info(w, hi)
                            P.mm(Pb[e][:, hi // 2, :], Kt[pse, c4, :], AR[pse, c4, :])
                        for hi in (1, 3):
                            h, c4, e, pse = head_info(w, hi)
                            P.mm(P2[:, hi, :], AR[pse, c4, 0:128], Bt[pse, c4, :])
                        for e in range(2):
                            P.tt(MKt[w][:, e::2, :], Pb[e], ubc, ALU.mult)
                        P.tt(ATT[w][0][:], P2, slbc, ALU.mult)
                    for j in range(1, 7):
                        for w in range(2):
                            Pb = [psA[3 * w + 0][:].rearrange("p (a c) -> p a c", a=2), psA[3 * w + 1][:].rearrange("p (a c) -> p a c", a=2)]
                            P2 = psA[3 * w + 2][:].rearrange("p (a c) -> p a c", a=4)
                            src_ta = TA[w][(j - 1) % 2]
                            dst_ta = TA[w][j % 2]
                            At = ATT[w][(j - 1) % 2]
                            for hi in range(4):
                                e = hi % 2
                                if j == 1:
                                    P.mm(Pb[e][:, hi // 2, 128:256], At[:, hi, :], MB[w][:, hi, 0:128])
                                    P.mm(P2[:, hi, :], MB[w][:, hi, 0:128], At[:, hi, :])
                                else:
                                    P.mm(Pb[e][:, hi // 2, :], At[:, hi, :], src_ta[:, hi, :])
                                    P.mm(P2[:, hi, :], src_ta[:, hi, 128:256], At[:, hi, :])
                            if j == 1:
                                P.tt(dst_ta[:, :, 0:128], MB[w][:, :, 0:128], idbc, ALU.add)
                            else:
                                for e in range(2):
                                    P.tt(dst_ta[:, e::2, 0:128], Pb[e][:, :, 0:128], src_ta[:, e::2, 0:128], ALU.add)
                            if j < 6:
                                for e in range(2):
                                    P.act(dst_ta[:, e::2, 128:256], Pb[e][:, :, 128:256], AF.Copy)
                            P.act(ATT[w][j % 2][:], P2, AF.Copy)
                    for w in range(2):
                        P2 = psA[3 * w + 2][:].rearrange("p (a c) -> p a c", a=4)
                        for hi in range(4):
                            P.mm(P2[:, hi, :], ATT[w][0][:, hi, :], TA[w][0][:, hi, 0:128])
                        P.tt(Tfw[w][:], P2, TA[w][0][:, :, 0:128], ALU.add)
                    if RW_STAGE < 4:
                        continue
                    W1ps = psA[4][:].rearrange("p (h v) -> p h v", v=64)
                    Ups = psA[5][:].rearrange("p (h v) -> p h v", v=64)
                    Yps = psA[3][:].rearrange("p (h v) -> p h v", v=64)
                    Sps = psA[2][:].rearrange("p (c v) -> p c v", v=128)
                    hinfo = []
                    for h in range(8):
                        c4, e = divmod(h, 2)
                        hinfo.append((h, c4, slice(e * 64, (e + 1) * 64), slice(h * 64, (h + 1) * 64)))
                    for h, c4, pse, hsl in hinfo:
                        P.mm(W1ps[:, h, :], AR[pse, c4, 0:128], Sb[pse, c4, :], start=True, stop=False)
                        P.mm(W1ps[:, h, :], MKt[h // 4][:, h % 4, 0:128], VT[:, hsl], start=False, stop=True)
                    P.act(W1[:], psA[4][:], AF.Copy)
                    for h, c4, pse, hsl in hinfo:
                        P.mm(Ups[:, h, :], Tfw[h // 4][:, h % 4, :], W1[:, hsl])
                    P.copy(UT[:], psA[5][:])
                    for h, c4, pse, hsl in hinfo:
                        P.mm(Yps[:, h, :], AR[pse, c4, 128:256], Sb[pse, c4, :], start=True, stop=False)
                        P.mm(Yps[:, h, :], MB[h // 4][:, h % 4, 128:256], UT[:, hsl], start=False, stop=False)
                        P.mm(Yps[:, h, :], MKt[h // 4][:, h % 4, 128:256], VT[:, hsl], start=False, stop=True)
                    P.act(Ytok[:], psA[3][:], AF.Copy)
                    for c4 in range(4):
                        cs = slice(c4 * 128, (c4 + 1) * 128)
                        P.mm(Sps[:, c4, :], BhT[:, cs], UT[:, cs], start=True, stop=False)
                        P.mm(Sps[:, c4, :], KhT[:, cs], VT[:, cs], start=False, stop=True)
                    for e in range(2):
                        pse = slice(e * 64, (e + 1) * 64)
                        P.tt(Sf[pse, :, :], Sf[pse, :, :], gCt[pse, :].unsqueeze(2).to_broadcast([64, 4, 64]), ALU.mult)
                        P.tt(Sf[pse, :, :], Sf[pse, :, :], Sps[pse, :, e * 64:(e + 1) * 64], ALU.add)
                    P.act(Sb[:], Sf[:], AF.Copy)
                    if RW_STAGE < 5:
                        continue
                    y3 = Ytok[:].rearrange("p (h d) -> p h d", d=64)
                    P.red(sm[:], y3)
                    P.act(yc[:], Ytok[:], AF.Square)
                    P.red(sm2[:], yc[:].rearrange("p (h d) -> p h d", d=64))
                    P.ts(mean[:], sm[:], 1.0 / 64, None, ALU.mult)
                    P.tt(msq[:], mean[:], mean[:], ALU.mult)
                    P.stt(var[:], sm2[:], 1.0 / 64, msq[:], ALU.mult, ALU.subtract)
                    P.ts(var[:], var[:], GN_EPS, None, ALU.add)
                    P.act(var[:], var[:], AF.Sqrt)
                    P.recip(var[:], var[:])
                    P.tt(yc[:].rearrange("p (h d) -> p h d", d=64), y3, mean[:].unsqueeze(2).to_broadcast([128, 8, 64]), ALU.subtract)
                    P.tt(ynb[:].rearrange("p (h d) -> p h d", d=64), yc[:].rearrange("p (h d) -> p h d", d=64),
                         var[:].unsqueeze(2).to_broadcast([128, 8, 64]), ALU.mult)
                    tp = tpb[n % 2]
                    for c4 in range(4):
                        P.tr(tp[:, c4 * 128:(c4 + 1) * 128], ynb[:, c4 * 128:(c4 + 1) * 128], ident[:])
                    for c4 in range(4):
                        yo_ = yo[c4 % 2]
                        P.ts(yo_[:], tp[:, c4 * 128:(c4 + 1) * 128], cols["rwkv_gn_w"][:, c4:c4 + 1], cols["rwkv_gn_b"][:, c4:c4 + 1],
                             ALU.mult, ALU.add)
                        P.tt(yo_[:], yo_[:], bonus[:, c4, :], ALU.add)
                        P.tt(yrT[:, c4, tl], yo_[:], gb[:, c4, :], ALU.mult)
                if dbg is not None:
                    P.dma(dbg["yr"][:, :, :], yrT[:], q="pool")
                P.barrier()
            merge_branch(ctx, hT, yrT, 4, wn["w_br_rwkv"], 4096)

    def mix_phase():
        with _ES() as ctx:
            hT = P.sb([128, 8, S], BF16, "hT", ctx)
            with _ES() as c1:
                norm_to_hT(c1, hT, wn["mix_norm"], tpb)
                P.barrier()
            if MIX_ATTN:
                attn_branch(hT)
            if MIX_RWKV:
                rwkv_branch(hT)
            P.barrier()

    P.barrier()
    if not SKIPFFN:
        ffn_phase(wn["ffn1_norm"], wn["ffn1_w_gate"], wn["ffn1_w_up"], wn["ffn1_w_down"])
    mix_phase()
    if not SKIPFFN:
        ffn_phase(wn["ffn2_norm"], wn["ffn2_w_gate"], wn["ffn2_w_up"], wn["ffn2_w_down"])
    ple_phase()
    P.finish()
    stack.close()
    return nc


_CACHE = {}


import os as _os
DEBUG = bool(_os.environ.get("KDEBUG"))
MIX_ATTN = not _os.environ.get("KNOATTN")
MIX_RWKV = not _os.environ.get("KNORWKV")
SKIPFFN = bool(_os.environ.get("KSKIPFFN"))
ATT_M = int(_os.environ.get("KATT_M", "16"))
ATT_PR = int(_os.environ.get("KATT_PR", "2"))
RWKV_N = int(_os.environ.get("KRWKV_N", "16"))
SKIPMERGE = bool(_os.environ.get("KSKIPMERGE"))
ATT_G = [int(c) for c in _os.environ.get("KATT_G", "012")]
ATT_STAGE = int(_os.environ.get("KATT_STAGE", "9"))
ATT_SUB = int(_os.environ.get("KATT_SUB", "9"))
RW_STAGE = int(_os.environ.get("KRW_STAGE", "9"))
RW_SUB = int(_os.environ.get("KRW_SUB", "9"))


def _consts():
    p = np.arange(128)[:, None]
    c = np.arange(128)[None, :]
    NEG = -30000.0
    onespad = np.zeros((128, 2, 128), np.float32)
    onespad[:, 0, 0:64] = 1.0
    onespad[:, 1, 64:128] = 1.0
    invf = (1.0 / (np.float32(10000.0) ** (np.arange(0, 64, 2, dtype=np.float32) / np.float32(64)))).astype(np.float32)
    bo = np.zeros((128, 128), np.float32)
    bo[0:64, 0:64] = 1.0
    bo[64:128, 64:128] = 1.0
    su = (p < c).astype(np.float32)
    iu = (p <= c).astype(np.float32)
    return {"c_ident": np.eye(128, dtype=np.float32),
            "c_maskc": np.where(p <= c, 0.0, NEG).astype(np.float32),
            "c_maskp": np.where(p >= c, 0.0, NEG).astype(np.float32),
            "c_onespad": onespad,
            "c_invfreq": np.ascontiguousarray(np.broadcast_to(invf[None, :], (128, 32))),
            "c_maskuu": np.concatenate([su, iu], axis=1),
            "c_masksl": (c < p).astype(np.float32),
            "c_blockones": bo}


def kernel(**inputs):
    if "nc" not in _CACHE:
        _CACHE["nc"] = build_program(dbg=({} if DEBUG else None))
    nc = _CACHE["nc"]
    consts = _consts()
    in_maps = []
    for b in range(8):
        m = dict(consts)
        m["x"] = np.ascontiguousarray(inputs["x"][b])
        m["p"] = np.ascontiguousarray(inputs["p"][0, b])
        m["positions"] = np.ascontiguousarray(inputs["positions"][b])
        for k, v in inputs.items():
            if k in ("x", "p", "positions"):
                continue
            a = np.asarray(v)[0]
            m[k] = np.ascontiguousarray(a.reshape(-1) if k == "rwkv_r_k" else a)
        in_maps.append(m)
    res = run_bass_kernel_spmd(nc, in_maps, core_ids=list(range(8)))
    if DEBUG:
        _CACHE["dbg"] = [{k: np.asarray(v) for k, v in r.items() if k.startswith("dbg_")} for r in res.results]
    return np.stack([np.asarray(r["out"]) for r in res.results], axis=0).astype(np.float32)
```

```python
import numpy as np
import concourse.bass as bass
import concourse.mybir as mybir
from concourse.bass_utils import run_bass_kernel_spmd

F32 = mybir.dt.float32
BF16 = mybir.dt.bfloat16
I32 = mybir.dt.int32
AF = mybir.ActivationFunctionType
ALU = mybir.AluOpType
AX = mybir.AxisListType

S = 2048
D = 1024
NT = 16
DFF = 2816
PLE = 256
RMS_EPS = 1e-6


class _Op:
    __slots__ = ("eng", "fn", "dma", "deps", "ddeps", "mark", "cnt", "slot", "val", "idx", "throttle")


class Prog:
    ENGS = ("pe", "act", "dve", "pool", "sync")
    EP = 2000
    NSLOT = 8

    def __init__(self, nc, stack):
        self.nc = nc
        self.stack = stack
        self.ops = []
        self.trk = {}
        self.last = {e: None for e in self.ENGS}
        self.ndma = {e: 0 for e in self.ENGS}
        self.dma_hist = {e: [] for e in self.ENGS}
        self.uid = 0

    def sb(self, shape, dt, name=None, ctx=None):
        self.uid += 1
        nm = f"{name or 't'}_{self.uid}"
        return (ctx or self.stack).enter_context(self.nc.sbuf_tensor(nm, list(shape), dt))

    def ps(self, shape, dt, name=None, ctx=None):
        self.uid += 1
        nm = f"{name or 'p'}_{self.uid}"
        return (ctx or self.stack).enter_context(self.nc.psum_tensor(nm, list(shape), dt))

    @staticmethod
    def _rect(ap):
        a = ap.ap
        off = int(ap.offset)
        pstep = int(a[0][0])
        npart = int(a[0][1])
        if pstep == 0:
            pstep = 1 << 40
        p0 = off // pstep
        f0 = off % pstep
        ext = 1
        for s, c in a[1:]:
            ext += (int(c) - 1) * abs(int(s))
        if "PSUM" in str(ap.space).upper():
            return ap.tensor.name, ((p0 // 32) * 32, ((p0 + npart + 31) // 32) * 32, 0, 1 << 30)
        return ap.tensor.name, (p0, p0 + npart, f0, f0 + ext)

    @staticmethod
    def _ov(a, b):
        return a[0] < b[1] and b[0] < a[1] and a[2] < b[3] and b[2] < a[3]

    @staticmethod
    def _inside(a, b):
        return a[0] >= b[0] and a[1] <= b[1] and a[2] >= b[2] and a[3] <= b[3]

    def _is_tracked(self, ap):
        sp = str(ap.space)
        return ("SB" in sp.upper()) or ("PSUM" in sp.upper())

    def op(self, eng, fn, reads=(), writes=(), dma=False):
        o = _Op()
        o.eng = eng
        o.fn = fn
        o.dma = dma
        o.deps = {}
        o.ddeps = []
        o.mark = False
        o.cnt = 0
        o.throttle = None
        prods = []
        for ap in reads:
            if ap is None or isinstance(ap, (int, float)) or not self._is_tracked(ap):
                continue
            nm, r = self._rect(ap)
            ents = self.trk.setdefault(nm, [])
            is_ps = "PSUM" in str(ap.space).upper()
            for e in ents:
                if self._ov(e[0], r):
                    if e[1] is not None:
                        prods.append(e[1])
                    if is_ps:
                        for rd in e[2]:
                            if rd.eng != eng:
                                prods.append(rd)
                    e[2].append(o)
        for ap in writes:
            if ap is None or not self._is_tracked(ap):
                continue
            nm, r = self._rect(ap)
            ents = self.trk.setdefault(nm, [])
            keep = []
            for e in ents:
                if self._ov(e[0], r):
                    if e[1] is not None:
                        prods.append(e[1])
                    prods.extend(e[2])
                    if self._inside(e[0], r):
                        continue
                keep.append(e)
            keep.append([r, o, []])
            self.trk[nm] = keep
        for p in prods:
            if p is o:
                continue
            if p.dma:
                if p not in o.ddeps:
                    o.ddeps.append(p)
            else:
                if p.eng == "pe" and eng == "pe" and not dma:
                    continue
                cur = o.deps.get(p.eng)
                if cur is None or p.idx > cur.idx:
                    o.deps[p.eng] = p
        if dma:
            i = self.ndma[eng]
            self.ndma[eng] = i + 1
            o.slot = i % self.NSLOT
            o.val = 16 * (i // self.NSLOT + 1)
            if i >= self.NSLOT:
                o.throttle = self.dma_hist[eng][i - self.NSLOT]
            self.dma_hist[eng].append(o)
        else:
            self.last[eng] = o
        o.idx = len(self.ops)
        self.ops.append(o)
        return o

    def barrier(self):
        lasts = [self.last[e] for e in self.ENGS if self.last[e] is not None]
        dm = []
        for e in self.ENGS:
            dm.extend(self.dma_hist[e][-self.NSLOT:])
        for e in self.ENGS:
            o = _Op()
            o.eng = e
            o.fn = None
            o.dma = False
            o.deps = {}
            for p in lasts:
                if p.eng == e and e == "pe":
                    continue
                o.deps[p.eng] = p
            o.ddeps = list(dm)
            o.mark = False
            o.cnt = 0
            o.throttle = None
            o.idx = len(self.ops)
            self.ops.append(o)
        self.trk = {}
        self.last = {e: None for e in self.ENGS}
        self.flush()

    def _csem(self, e, b):
        lst = self.csem.setdefault(e, [])
        while len(lst) <= b:
            lst.append(self.stack.enter_context(self.nc.semaphore(f"c_{e}_{len(lst)}")))
        return lst[b]

    def _dsem(self, e, s):
        if e not in self.dsem:
            self.dsem[e] = [self.stack.enter_context(self.nc.semaphore(f"d_{e}_{i}")) for i in range(self.NSLOT)]
        return self.dsem[e][s]

    def _wait_dma(self, E, e, d):
        key = (d.eng, d.slot)
        if self.dwaited[e].get(key, 0) < d.val:
            E.wait_ge(self._dsem(d.eng, d.slot), d.val)
            self.dwaited[e][key] = d.val

    def flush(self):
        nc = self.nc
        if not hasattr(self, "counts"):
            self.counts = {e: 0 for e in self.ENGS}
            self.csem = {}
            self.dsem = {}
            self.waited = {e: {f: 0 for f in self.ENGS} for e in self.ENGS}
            self.dwaited = {e: {} for e in self.ENGS}
            self.pos = 0
        engobj = {"pe": nc.tensor, "act": nc.scalar, "dve": nc.vector, "pool": nc.gpsimd, "sync": nc.sync}
        batch = self.ops[self.pos:]
        for o in batch:
            for p in o.deps.values():
                assert p.idx >= self.pos, "dependency on an already-flushed op"
                p.mark = True
        for o in batch:
            if o.fn is not None and not o.dma and o.mark:
                self.counts[o.eng] += 1
                o.cnt = self.counts[o.eng]
        for o in batch:
            e = o.eng
            E = engobj[e]
            for f, p in o.deps.items():
                if p.cnt > self.waited[e][f]:
                    b = (p.cnt - 1) // self.EP
                    E.wait_ge(self._csem(f, b), (p.cnt - 1) % self.EP + 1)
                    self.waited[e][f] = p.cnt
            for d in o.ddeps:
                self._wait_dma(E, e, d)
            if o.throttle is not None:
                self._wait_dma(E, e, o.throttle)
            if o.fn is None:
                continue
            ins = o.fn(E)
            if o.dma:
                ins.then_inc(self._dsem(e, o.slot), 16)
            elif o.mark:
                b = (o.cnt - 1) // self.EP
                ins.then_inc(self._csem(e, b), 1)
            o.fn = None
        self.pos = len(self.ops)

    def finish(self):
        self.flush()
        E = self.nc.sync
        for q in self.ENGS:
            for d in self.dma_hist[q][-self.NSLOT:]:
                self._wait_dma(E, "sync", d)

    def mm(self, out, lhsT, rhs, start=True, stop=True):
        return self.op("pe", lambda E: E.matmul(out, lhsT, rhs, start=start, stop=stop),
                       reads=[lhsT, rhs], writes=[out])

    def tr(self, out, in_, ident):
        return self.op("pe", lambda E: E.transpose(out, in_, ident), reads=[in_, ident], writes=[out])

    def act(self, out, in_, func, bias=None, scale=None, accum_out=None, eng="act"):
        kw = {}
        rd = [in_]
        if bias is not None:
            kw["bias"] = bias
            rd.append(bias)
        if scale is not None:
            kw["scale"] = scale
            rd.append(scale)
        wr = [out]
        if accum_out is not None:
            kw["accum_out"] = accum_out
            wr.append(accum_out)
        return self.op(eng, lambda E: E.activation(out, in_, func, **kw), reads=rd, writes=wr)

    def tt(self, out, in0, in1, op, eng="dve"):
        return self.op(eng, lambda E: E.tensor_tensor(out, in0, in1, op), reads=[in0, in1], writes=[out])

    def ts(self, out, in0, s1, s2, op0, op1=None, eng="dve", accum_out=None):
        rd = [in0, s1, s2]
        wr = [out]
        if accum_out is not None:
            wr.append(accum_out)

        def f(E):
            kw = {}
            if accum_out is not None:
                kw["accum_out"] = accum_out
            if op1 is None:
                return E.tensor_scalar(out, in0, s1, None, op0, **kw)
            return E.tensor_scalar(out, in0, s1, s2, op0, op1, **kw)
        return self.op(eng, f, reads=rd, writes=wr)

    def stt(self, out, in0, scalar, in1, op0, op1, eng="dve"):
        return self.op(eng, lambda E: E.scalar_tensor_tensor(out, in0, scalar, in1, op0, op1),
                       reads=[in0, scalar, in1], writes=[out])

    def copy(self, out, in_, eng="dve"):
        if eng == "act":
            return self.op("act", lambda E: E.copy(out, in_), reads=[in_], writes=[out])
        return self.op(eng, lambda E: E.tensor_copy(out, in_), reads=[in_], writes=[out])

    def red(self, out, in_, op=None, eng="dve"):
        op = op or ALU.add
        return self.op(eng, lambda E: E.tensor_reduce(out, in_, AX.X, op), reads=[in_], writes=[out])

    def recip(self, out, in_):
        return self.op("dve", lambda E: E.reciprocal(out, in_), reads=[in_], writes=[out])

    def memset(self, ap, val, eng="dve"):
        return self.op(eng, lambda E: E.memset(ap, val), reads=[], writes=[ap])

    def dma(self, out, in_, q="sync", slow=False):
        kw = {}
        if slow:
            kw["allow_slow_non_contiguous"] = True
        return self.op(q, lambda E: E.dma_start(out=out, in_=in_, **kw), reads=[in_], writes=[out], dma=True)


FF_SPLITS = (4, 4, 4, 4, 3, 3)


def build_program(dbg=None):
    from contextlib import ExitStack
    nc = bass.Bass("TRN2", target_bir_lowering=False)
    stack = ExitStack()
    P = Prog(nc, stack)

    def din(name, shape, dt=F32):
        return nc.dram_tensor(name, list(shape), dt, kind="ExternalInput").ap()

    x_d = din("x", [S, D])
    p_d = din("p", [S, PLE])
    pos_d = din("positions", [S], I32)
    wn = {}
    for nm, shp in (("ffn1_norm", [D]), ("ffn1_w_gate", [D, DFF]), ("ffn1_w_up", [D, DFF]), ("ffn1_w_down", [DFF, D]),
                    ("mix_norm", [D]), ("w_in", [D, 6144]), ("rwkv_mu", [1792]), ("rwkv_w0", [512]),
                    ("rwkv_w2", [64, 512]), ("rwkv_a0", [512]), ("rwkv_a2", [64, 512]), ("rwkv_g2", [128, 512]),
                    ("rwkv_k_k", [512]), ("rwkv_k_a", [512]), ("rwkv_r_k", [512]), ("rwkv_gn_w", [512]),
                    ("rwkv_gn_b", [512]), ("q_norm", [64]), ("k_norm", [64]), ("w_br_rwkv", [512, D]),
                    ("w_br_attn", [256, D]), ("w_out", [D, D]), ("ffn2_norm", [D]), ("ffn2_w_gate", [D, DFF]),
                    ("ffn2_w_up", [D, DFF]), ("ffn2_w_down", [DFF, D]), ("ple_norm", [D]), ("ple_w_gate", [D, D]),
                    ("ple_w_proj", [PLE, D])):
        wn[nm] = din(nm, shp)
    ident_d = din("c_ident", [128, 128])
    cst = {}
    for nm, shp in (("c_maskc", [128, 128]), ("c_maskp", [128, 128]), ("c_onespad", [128, 2, 128]),
                    ("c_invfreq", [128, 32]), ("c_maskuu", [128, 256]), ("c_masksl", [128, 128]),
                    ("c_blockones", [128, 128])):
        cst[nm] = din(nm, shp)
    if dbg is not None:
        dbg = {"ya": nc.dram_tensor("dbg_ya", [128, 2, S], F32, kind="ExternalOutput").ap(),
               "yr": nc.dram_tensor("dbg_yr", [128, 4, S], F32, kind="ExternalOutput").ap()}
    out_d = nc.dram_tensor("out", [S, D], F32, kind="ExternalOutput").ap()

    X = P.sb([128, NT, D], F32, "X")
    ident = P.sb([128, 128], BF16, "ident")
    P.dma(ident[:], ident_d[:, :], q="pool")
    xv = x_d.rearrange("(t p) d -> p t d", p=128)
    for i in range(4):
        P.dma(X[:, 4 * i:4 * i + 4, :], xv[:, 4 * i:4 * i + 4, :], q="sync")

    from contextlib import ExitStack as _ES

    def norm_to_hT(ctx, hT, gain_d, tp_banks):
        gain_bc = P.sb([128, D], F32, "gain_bc", ctx)
        P.dma(gain_bc[:], gain_d.partition_broadcast(128), q="sync")
        ss = P.sb([128, NT], F32, "ss", ctx)
        rstd = P.sb([128, NT], F32, "rstd", ctx)
        junk = P.sb([128, D], BF16, "junk", ctx)
        xn = [P.sb([128, D], BF16, "xn", ctx) for _ in range(2)]
        P.memset(ss[:], 0.0)
        for t in range(NT):
            P.act(junk[:], X[:, t, :], AF.Square, accum_out=ss[:, t:t + 1])
        P.ts(rstd[:], ss[:], 1.0 / D, RMS_EPS, ALU.mult, ALU.add)
        P.act(rstd[:], rstd[:], AF.Sqrt)
        P.recip(rstd[:], rstd[:])
        for t in range(NT):
            xt = xn[t % 2]
            P.stt(xt[:], X[:, t, :], rstd[:, t:t + 1], gain_bc[:], ALU.mult, ALU.mult)
            pt = tp_banks[t % 2]
            for kc in range(8):
                P.tr(pt[:, kc * 128:(kc + 1) * 128], xt[:, kc * 128:(kc + 1) * 128], ident[:])
            P.copy(hT[:, :, t * 128:(t + 1) * 128], pt[:].rearrange("p (k c) -> p k c", k=8),
                   eng=("act" if t % 2 else "dve"))

    def ffn(ctx, hT, wg_d, wu_d, wd_d, psum):
        wgv = wg_d.rearrange("(k p) c -> p k c", p=128)
        wuv = wu_d.rearrange("(k p) c -> p k c", p=128)
        wdv = wd_d.rearrange("(f p) d -> p f d", p=128)
        Wg = [P.sb([128, 8, 512], BF16, "Wg", ctx) for _ in range(2)]
        Wu = [P.sb([128, 8, 512], BF16, "Wu", ctx) for _ in range(2)]
        Wd = [P.sb([128, 4, D], BF16, "Wd", ctx) for _ in range(2)]
        aT = [P.sb([128, 4, S], BF16, "aT", ctx) for _ in range(2)]
        sg = [P.sb([128, 512], BF16, "sg", ctx) for _ in range(2)]
        pg, pu, pd = psum["g"], psum["u"], psum["d"]
        offs = []
        f0 = 0
        for nf in FF_SPLITS:
            offs.append((f0, nf))
            f0 += nf
        cnt = {"gu": 0, "d": 0}

        def load(si):
            f0, nf = offs[si]
            sl = si % 2
            c0, c1 = f0 * 128, (f0 + nf) * 128
            P.dma(Wg[sl][:, :, 0:nf * 128], wgv[:, :, c0:c1], q="pool")
            P.dma(Wu[sl][:, :, 0:nf * 128], wuv[:, :, c0:c1], q="pool")
            P.dma(Wd[sl][:, 0:nf, :], wdv[:, f0:f0 + nf, :], q="pool")

        def gu(si):
            f0, nf = offs[si]
            sl = si % 2
            for fl in range(nf):
                for tg in range(4):
                    i = cnt["gu"] % 2
                    cnt["gu"] += 1
                    for kc in range(8):
                        P.mm(pg[i][:], Wg[sl][:, kc, fl * 128:(fl + 1) * 128], hT[:, kc, tg * 512:(tg + 1) * 512],
                             start=(kc == 0), stop=(kc == 7))
                    for kc in range(8):
                        P.mm(pu[i][:], Wu[sl][:, kc, fl * 128:(fl + 1) * 128], hT[:, kc, tg * 512:(tg + 1) * 512],
                             start=(kc == 0), stop=(kc == 7))
                    P.act(sg[i][:], pg[i][:], AF.Silu)
                    P.tt(aT[sl][:, fl, tg * 512:(tg + 1) * 512], sg[i][:], pu[i][:], ALU.mult)

        def down(si):
            f0, nf = offs[si]
            sl = si % 2
            for t in range(NT):
                for dh in range(2):
                    i = cnt["d"] % 2
                    cnt["d"] += 1
                    for fl in range(nf):
                        P.mm(pd[i][:], aT[sl][:, fl, t * 128:(t + 1) * 128], Wd[sl][:, fl, dh * 512:(dh + 1) * 512],
                             start=(fl == 0), stop=(fl == nf - 1))
                    xs = X[:, t, dh * 512:(dh + 1) * 512]
                    P.stt(xs, pd[i][:], 0.5, xs, ALU.mult, ALU.add)

        ns = len(offs)
        load(0)
        load(1)
        gu(0)
        for si in range(1, ns):
            gu(si)
            down(si - 1)
            if si + 1 < ns:
                load(si + 1)
        down(ns - 1)

    psA = [P.ps([128, 512], F32, "psA") for _ in range(6)]
    tpb = [P.ps([128, 1024], BF16, "tpb") for _ in range(2)]
    psum_ffn = {"g": psA[0:2], "u": psA[2:4], "d": psA[4:6]}

    def ffn_phase(norm_d, wg_d, wu_d, wd_d):
        with _ES() as ctx:
            hT = P.sb([128, 8, S], BF16, "hT", ctx)
            norm_to_hT(ctx, hT, norm_d, tpb)
            ffn(ctx, hT, wg_d, wu_d, wd_d, psum_ffn)
            P.barrier()

    def ple_phase():
        with _ES() as ctx:
            hT = P.sb([128, 8, S], BF16, "hT", ctx)
            Wpg = P.sb([128, 8, D], BF16, "Wpg", ctx)
            Wpp = P.sb([128, 2, D], BF16, "Wpp", ctx)
            P.dma(Wpg[:], wn["ple_w_gate"].rearrange("(k p) c -> p k c", p=128), q="pool")
            P.dma(Wpp[:], wn["ple_w_proj"].rearrange("(k p) c -> p k c", p=128), q="pool")
            pin = P.sb([128, NT, PLE], BF16, "pin", ctx)
            P.dma(pin[:], p_d.rearrange("(t p) c -> p t c", p=128), q="pool")
            pT = P.sb([128, 2, S], BF16, "pT", ctx)
            norm_to_hT(ctx, hT, wn["ple_norm"], tpb)
            for t in range(NT):
                pt = tpb[t % 2]
                for k in range(2):
                    P.tr(pt[:, k * 128:(k + 1) * 128], pin[:, t, k * 128:(k + 1) * 128], ident[:])
                P.copy(pT[:, :, t * 128:(t + 1) * 128], pt[:, 0:256].rearrange("p (k c) -> p k c", k=2),
                       eng=("act" if t % 2 else "dve"))
            sgs = [P.sb([128, 512], F32, "sgs", ctx) for _ in range(2)]
            n = 0
            ov = out_d.rearrange("(t p) d -> p t d", p=128)
            for t in range(NT):
                for dh in range(2):
                    i = n % 2
                    n += 1
                    pG, pP = psA[i], psA[2 + i]
                    for kc in range(8):
                        P.mm(pG[:], hT[:, kc, t * 128:(t + 1) * 128], Wpg[:, kc, dh * 512:(dh + 1) * 512],
                             start=(kc == 0), stop=(kc == 7))
                    for k in range(2):
                        P.mm(pP[:], pT[:, k, t * 128:(t + 1) * 128], Wpp[:, k, dh * 512:(dh + 1) * 512],
                             start=(k == 0), stop=(k == 1))
                    P.act(sgs[i][:], pG[:], AF.Sigmoid)
                    P.tt(sgs[i][:], sgs[i][:], pP[:], ALU.mult)
                    xs = X[:, t, dh * 512:(dh + 1) * 512]
                    P.tt(xs, xs, sgs[i][:], ALU.add)
                P.dma(ov[:, t, :], X[:, t, :], q="sync")
            P.barrier()


    TWO_PI = 6.283185307179586
    C1 = 6.28125
    C2 = TWO_PI - C1
    PI = 3.141592653589793

    def run_interleaved(gens):
        gens = list(gens)
        while gens:
            for g_ in list(gens):
                try:
                    next(g_)
                except StopIteration:
                    gens.remove(g_)

    def tokslice(hT, kc, t0, d):
        if d == 1:
            return hT[:, kc, t0:t0 + 128]
        return hT[:, kc, t0:t0 + 127 * d + 1:d]

    def merge_branch(ctx0, hT, yT, nyc, wbr_d, gate_col0):
        if SKIPMERGE:
            return
        with _ES() as ctx:
            Wbr = P.sb([128, nyc, D], BF16, "Wbr", ctx)
            P.dma(Wbr[:], wbr_d.rearrange("(c p) d -> p c d", p=128), q="pool")
            Wg = P.sb([128, 8, D], BF16, "Wgt", ctx)
            wiv = wn["w_in"].rearrange("(k p) c -> p k c", p=128)
            P.dma(Wg[:], wiv[:, :, gate_col0:gate_col0 + D], q="pool")
            Wo = P.sb([128, 8, D], BF16, "Wo", ctx)
            P.dma(Wo[:], wn["w_out"].rearrange("(k p) c -> p k c", p=128), q="pool")
            mT = P.sb([128, 8, S], BF16, "mT", ctx)
            sgm = [P.sb([128, 512], F32, "sgm", ctx) for _ in range(2)]
            n = 0
            for dc in range(8):
                for tg in range(4):
                    i = n % 2
                    n += 1
                    pG, pB = psA[i], psA[2 + i]
                    ts_ = slice(tg * 512, (tg + 1) * 512)
                    for kc in range(8):
                        P.mm(pG[:], Wg[:, kc, dc * 128:(dc + 1) * 128], hT[:, kc, ts_], start=(kc == 0), stop=(kc == 7))
                    for c in range(nyc):
                        P.mm(pB[:], Wbr[:, c, dc * 128:(dc + 1) * 128], yT[:, c, ts_], start=(c == 0), stop=(c == nyc - 1))
                    P.act(sgm[i][:], pG[:], AF.Sigmoid)
                    P.tt(mT[:, dc, ts_], sgm[i][:], pB[:], ALU.mult)
            n = 0
            for t in range(NT):
                for dh in range(2):
                    pd_ = psA[4 + n % 2]
                    n += 1
                    for dc in range(8):
                        P.mm(pd_[:], mT[:, dc, t * 128:(t + 1) * 128], Wo[:, dc, dh * 512:(dh + 1) * 512],
                             start=(dc == 0), stop=(dc == 7))
                    xs = X[:, t, dh * 512:(dh + 1) * 512]
                    P.tt(xs, xs, pd_[:], ALU.add)
            P.barrier()

    def attn_branch(hT):
        with _ES() as ctx:
            yaT = P.sb([128, 2, S], BF16, "yaT", ctx)
            with _ES() as c2:
                maskC = P.sb([128, 128], BF16, "maskC", c2)
                maskP = P.sb([128, 128], BF16, "maskP", c2)
                onesp = P.sb([128, 2, 128], BF16, "onesp", c2)
                P.dma(maskC[:], cst["c_maskc"][:, :], q="pool")
                P.dma(maskP[:], cst["c_maskp"][:, :], q="pool")
                P.dma(onesp[:], cst["c_onespad"][:, :, :], q="pool")
                invf = P.sb([128, 32], F32, "invf", c2)
                P.dma(invf[:], cst["c_invfreq"][:, :], q="sync")
                gain4 = P.sb([128, 4, 64], F32, "gain4", c2)
                for hh in range(4):
                    P.dma(gain4[:, hh, :], (wn["q_norm"] if hh < 2 else wn["k_norm"]).partition_broadcast(128), q="sync")
                cosT = [P.sb([128, 16 * 32], F32, "cosT", c2) for _ in range(3)]
                sinT = [P.sb([128, 16 * 32], F32, "sinT", c2) for _ in range(3)]
                with _ES() as c3:
                    posi = P.sb([128, 16], I32, "posi", c3)
                    posf = P.sb([128, 16], F32, "posf", c3)
                    ang = P.sb([128, 512], F32, "ang", c3)
                    aa = P.sb([128, 512], F32, "aa", c3)
                    kf = P.sb([128, 512], F32, "kf", c3)
                    ki = P.sb([128, 512], I32, "ki", c3)
                    rr = P.sb([128, 512], F32, "rr", c3)
                    mk = P.sb([128, 512], F32, "mk", c3)
                    for g, d in enumerate((1, 4, 16)):
                        pv = pos_d.rearrange("(ib p j) -> p j ib", p=128, j=d)
                        nib = 16 // d
                        for m in range(16):
                            P.dma(posi[:, m:m + 1], pv[:, m // nib, (m % nib):(m % nib) + 1], q="sync", slow=True)
                        P.copy(posf[:], posi[:])
                        for m in range(16):
                            P.ts(ang[:, m * 32:(m + 1) * 32], invf[:], posf[:, m:m + 1], None, ALU.mult)
                        for dst, shift in ((sinT[g], 0.0), (cosT[g], PI / 2)):
                            P.ts(aa[:], ang[:], shift, None, ALU.add)
                            P.ts(kf[:], aa[:], 1.0 / TWO_PI, None, ALU.mult)
                            P.copy(ki[:], kf[:])
                            P.copy(kf[:], ki[:])
                            P.stt(rr[:], kf[:], -C1, aa[:], ALU.mult, ALU.add)
                            P.stt(rr[:], kf[:], -C2, rr[:], ALU.mult, ALU.add)
                            P.ts(mk[:], rr[:], PI, None, ALU.is_gt)
                            P.stt(rr[:], mk[:], -TWO_PI, rr[:], ALU.mult, ALU.add)
                            P.ts(mk[:], rr[:], -PI, None, ALU.is_lt)
                            P.stt(rr[:], mk[:], TWO_PI, rr[:], ALU.mult, ALU.add)
                            P.act(dst[:], rr[:], AF.Sin)
                    P.barrier()
                Wq = [[P.sb([128, 8, 128], BF16, "Wq", c2) for _ in range(3)] for _ in range(2)]
                qkT = P.sb([128, 2, S], BF16, "qkT", c2)
                Vpad = P.sb([128, 16, 2, 128], BF16, "Vpad", c2)
                acc = P.sb([128, 2, S], F32, "acc", c2)
                abuf = []
                for _i in range(2):
                    abuf.append((P.sb([128, 4, 256], F32, "qkb", c2), P.sb([128, 4, 256], F32, "qnb", c2),
                                 P.sb([128, 16], F32, "ss16", c2), P.sb([128, 512], F32, "t1", c2),
                                 P.sb([128, 512], F32, "t2", c2), P.sb([128, 512], F32, "t3", c2),
                                 P.sb([128, 512], F32, "t4", c2), P.sb([128, 4, 256], BF16, "qrb", c2)))
                PT = [P.sb([128, 4, 128], BF16, "PT", c2) for _ in range(2)]
                if _os.environ.get('KPRINTMEM'):
                    print('ATTN sbuf remaining', nc.sbuf_bytes_remaining)
                P.memset(Vpad[:], 0.0)
                wiv = wn["w_in"].rearrange("(k p) c -> p k c", p=128)
                it = 0
                for pr in range(ATT_PR):
                    for g, d in enumerate((1, 4, 16)):
                        if g not in ATT_G:
                            continue
                        L = S // d
                        W = Wq[it % 2]
                        it += 1
                        qc0 = 1792 + (g * 4 + pr * 2) * 64
                        for j3 in range(3):
                            P.dma(W[j3][:], wiv[:, :, qc0 + 768 * j3:qc0 + 768 * j3 + 128], q="pool")
                        TB = 4

                        def st1_chain(mb, bs):
                            qkb, qnb, ss16, t1, t2, t3, t4, qrb = abuf[bs]
                            bank = psA[3 + bs]
                            for mi in range(TB):
                                m = mb + mi
                                j, i0 = divmod(m * 128, L)
                                t0 = i0 * d + j
                                for j3 in range(3):
                                    for kc in range(8):
                                        P.mm(bank[:, j3 * 128:(j3 + 1) * 128], tokslice(hT, kc, t0, d), W[j3][:, kc, :],
                                             start=(kc == 0), stop=(kc == 7))
                                    yield
                                P.act(qkb[:, mi, :], bank[:, 0:256], AF.Copy)
                                yield
                                P.copy(Vpad[:, m, 0, 0:64], bank[:, 256:320])
                                yield
                                P.copy(Vpad[:, m, 1, 64:128], bank[:, 320:384])
                                yield
                            P.tt(qnb[:], qkb[:], qkb[:], ALU.mult)
                            yield
                            P.red(ss16[:], qnb[:].rearrange("p a (h d) -> p (a h) d", d=64))
                            yield
                            P.ts(ss16[:], ss16[:], 1.0 / 64, RMS_EPS, ALU.mult, ALU.add)
                            yield
                            P.act(ss16[:], ss16[:], AF.Sqrt)
                            yield
                            P.recip(ss16[:], ss16[:])
                            yield
                            P.tt(qnb[:].rearrange("p a (h d) -> p (a h) d", d=64), qkb[:].rearrange("p a (h d) -> p (a h) d", d=64),
                                 ss16[:].unsqueeze(2).to_broadcast([128, 16, 64]), ALU.mult)
                            yield
                            P.tt(qnb[:], qnb[:], gain4[:].rearrange("p h d -> p (h d)").unsqueeze(1).to_broadcast([128, TB, 256]), ALU.mult)
                            yield
                            q5 = qnb[:].rearrange("p a (h t i) -> p a h t i", t=2, i=32)
                            x1, x2 = q5[:, :, :, 0, :], q5[:, :, :, 1, :]
                            cb = cosT[g][:, mb * 32:(mb + TB) * 32].rearrange("p (a i) -> p a i", i=32).unsqueeze(2).to_broadcast([128, TB, 4, 32])
                            sb_ = sinT[g][:, mb * 32:(mb + TB) * 32].rearrange("p (a i) -> p a i", i=32).unsqueeze(2).to_broadcast([128, TB, 4, 32])
                            v4 = lambda tl: tl[:].rearrange("p (a h i) -> p a h i", h=4, i=32)
                            P.tt(v4(t1), x1, cb, ALU.mult)
                            yield
                            P.tt(v4(t2), x2, sb_, ALU.mult)
                            yield
                            P.tt(v4(t3), x2, cb, ALU.mult)
                            yield
                            P.tt(v4(t4), x1, sb_, ALU.mult)
                            yield
                            r5 = qrb[:].rearrange("p a (h t i) -> p a h t i", t=2, i=32)
                            P.tt(r5[:, :, :, 0, :], v4(t1), v4(t2), ALU.subtract)
                            yield
                            P.tt(r5[:, :, :, 1, :], v4(t3), v4(t4), ALU.add)
                            yield
                            tp = tpb[bs]
                            for mi in range(TB):
                                for a in range(2):
                                    P.tr(tp[:, a * 512 + mi * 128:a * 512 + (mi + 1) * 128], qrb[:, mi, a * 128:(a + 1) * 128], ident[:])
                                yield
                            P.copy(qkT[:, :, mb * 128:(mb + TB) * 128], tp[:].rearrange("p (a c) -> p a c", a=2), eng="act")
                            yield

                        for mb2 in range(0, 16, 2 * TB):
                            run_interleaved([st1_chain(mb2, 0), st1_chain(mb2 + TB, 1)])
                        def blk_scores(m):
                            j, i0 = divmod(m * 128, L)
                            has_prev = (i0 // 128) > 0
                            sc = psA[m % 2][:].rearrange("p (a c) -> p a c", a=4)
                            qs = slice(m * 128, (m + 1) * 128)
                            for e in range(2):
                                pse = slice(e * 64, (e + 1) * 64)
                                P.mm(sc[:, e, :], qkT[pse, 1, qs], qkT[pse, 0, qs], start=True, stop=False)
                                P.mm(sc[:, e, :], ident[:], maskC[:], start=False, stop=True)
                                if has_prev:
                                    P.mm(sc[:, 2 + e, :], qkT[pse, 1, (m - 1) * 128:m * 128], qkT[pse, 0, qs], start=True, stop=False)
                                    P.mm(sc[:, 2 + e, :], ident[:], maskP[:], start=False, stop=True)
                            nk = 4 if has_prev else 2
                            P.act(PT[m % 2][:, 0:nk, :], sc[:, 0:nk, :], AF.Exp, scale=0.125)

                        def blk_ol(m):
                            j, i0 = divmod(m * 128, L)
                            t0 = i0 * d + j
                            has_prev = (i0 // 128) > 0
                            nk = 4 if has_prev else 2
                            pt_ = PT[m % 2]
                            OL = psA[2][:, (m % 2) * 256:(m % 2) * 256 + 256].rearrange("p (a c) -> p a c", a=2)
                            seq = [(kb, e) for kb in range(nk // 2) for e in range(2)]
                            for ii, (kb, e) in enumerate(seq):
                                P.mm(OL[:, 0, :], Vpad[:, m - kb, e, :], pt_[:, kb * 2 + e, :], start=(ii == 0), stop=(ii == len(seq) - 1))
                            for ii, (kb, e) in enumerate(seq):
                                P.mm(OL[:, 1, :], onesp[:, e, :], pt_[:, kb * 2 + e, :], start=(ii == 0), stop=(ii == len(seq) - 1))
                            if d == 1:
                                av = acc[:, :, t0:t0 + 128]
                            else:
                                av = acc[:, :, t0:t0 + 127 * d + 1:d]
                            if g == 0:
                                P.copy(av, OL)
                            else:
                                P.tt(av, av, OL, ALU.add)

                        if ATT_STAGE >= 2:
                            for m in range(17):
                                if m < 16:
                                    blk_scores(m)
                                if m >= 1:
                                    blk_ol(m - 1)
                    P.recip(acc[:, 1, :], acc[:, 1, :])
                    P.tt(yaT[:, pr, :], acc[:, 0, :], acc[:, 1, :], ALU.mult)
                if dbg is not None:
                    P.dma(dbg["ya"][:, :, :], yaT[:], q="pool")
                P.barrier()
            merge_branch(ctx, hT, yaT, 2, wn["w_br_attn"], 4096 + 1024)

    def rwkv_branch(hT):
        with _ES() as ctx:
            yrT = P.sb([128, 4, S], BF16, "yrT", ctx)
            with _ES() as c2:
                sbt = lambda shp, dt, nm: P.sb(shp, dt, nm, c2)
                maskuu = sbt([128, 256], BF16, "maskuu")
                masksl = sbt([128, 128], BF16, "masksl")
                bones = sbt([128, 128], BF16, "bones")
                P.dma(maskuu[:], cst["c_maskuu"][:, :], q="pool")
                P.dma(masksl[:], cst["c_masksl"][:, :], q="pool")
                P.dma(bones[:], cst["c_blockones"][:, :], q="pool")
                muT = sbt([128, 14], F32, "muT")
                for c in range(14):
                    P.dma(muT[:, c:c + 1], wn["rwkv_mu"][c * 128:(c + 1) * 128].rearrange("(p o) -> p o", o=1), q="sync", slow=True)
                cols = {}
                for nm in ("rwkv_w0", "rwkv_a0", "rwkv_k_k", "rwkv_k_a", "rwkv_r_k", "rwkv_gn_w", "rwkv_gn_b"):
                    cols[nm] = sbt([128, 4], F32, nm)
                    for c in range(4):
                        P.dma(cols[nm][:, c:c + 1], wn[nm][c * 128:(c + 1) * 128].rearrange("(p o) -> p o", o=1), q="sync", slow=True)
                omk = sbt([128, 4], F32, "omk")
                P.ts(omk[:], cols["rwkv_k_a"][:], -1.0, 1.0, ALU.mult, ALU.add)
                w2b = sbt([128, 512], BF16, "w2b")
                a2b = sbt([128, 512], BF16, "a2b")
                g2b = sbt([128, 512], BF16, "g2b")
                P.dma(w2b[0:64, :], wn["rwkv_w2"][:, :], q="pool")
                P.dma(a2b[64:128, :], wn["rwkv_a2"][:, :], q="pool")
                P.dma(g2b[:], wn["rwkv_g2"][:, :], q="pool")
                Wst = [sbt([128, 8, 256], BF16, "Wst") for _ in range(2)]
                zt = sbt([128, 14, 128], F32, "zt")
                zcb = [sbt([128, 129], F32, "zc") for _ in range(4)]
                dzb = [sbt([128, 128], F32, "dzb") for _ in range(4)]
                carry = sbt([128, 14], F32, "carry")
                P.memset(carry[:], 0.0)
                ones_b = sbt([128, 128], BF16, "ones_b")
                P.memset(ones_b[:], 1.0)
                th = sbt([128, 128], BF16, "th")
                adb = sbt([128, 128], BF16, "adb")
                sgd = sbt([128, 128], BF16, "sgd")
                f32t = lambda nm: sbt([128, 128], F32, nm)
                NSET = 4
                tset = []
                for _i in range(NSET):
                    tset.append({nm: f32t(nm) for nm in ("s_", "a_", "L0", "E1", "E2", "E3", "E4", "kk", "bv", "tA", "kp", "rn")})
                    tset[-1]["kk2"] = sbt([128, 128], BF16, "kk2")
                    tset[-1]["t2b"] = sbt([128, 128], BF16, "t2b")
                gCt = sbt([128, 4], F32, "gCt")
                AR = sbt([128, 4, 256], BF16, "AR")
                Bt = sbt([128, 4, 128], BF16, "Bt")
                Kt = sbt([128, 4, 128], BF16, "Kt")
                gb = sbt([128, 4, 128], BF16, "gb")
                bonus = sbt([128, 4, 128], F32, "bonus")
                Bh = sbt([128, 4, 128], BF16, "Bh")
                Kh = sbt([128, 4, 128], BF16, "Kh")
                vb = sbt([128, 4, 128], BF16, "vb")
                VT = sbt([128, 512], BF16, "VT")
                KhT = sbt([128, 512], BF16, "KhT")
                BhT = sbt([128, 512], BF16, "BhT")
                Ytok = sbt([128, 512], F32, "Ytok")
                yc = sbt([128, 512], F32, "yc")
                ynb = sbt([128, 512], BF16, "ynb")
                sm = sbt([128, 8], F32, "sm"); sm2 = sbt([128, 8], F32, "sm2"); mean = sbt([128, 8], F32, "mean")
                msq = sbt([128, 8], F32, "msq"); var = sbt([128, 8], F32, "var")
                yo = [f32t("yo") for _ in range(2)]
                MB = [sbt([128, 4, 256], BF16, "MB") for _ in range(2)]
                MKt = [sbt([128, 4, 256], BF16, "MKt") for _ in range(2)]
                Tfw = [sbt([128, 4, 128], BF16, "Tfw") for _ in range(2)]
                TA = [[sbt([128, 4, 256], BF16, "TA") for _ in range(2)] for _ in range(2)]
                ATT = [[sbt([128, 4, 128], BF16, "ATT") for _ in range(2)] for _ in range(2)]
                Sf = sbt([128, 4, 64], F32, "Sf")
                Sb = sbt([128, 4, 64], BF16, "Sb")
                W1 = sbt([128, 512], BF16, "W1")
                UT = sbt([128, 512], BF16, "UT")
                if _os.environ.get('KPRINTMEM'):
                    print('RWKV sbuf remaining', nc.sbuf_bytes_remaining)
                P.memset(Sf[:], 0.0)
                P.memset(Sb[:], 0.0)
                wiv = wn["w_in"].rearrange("(k p) c -> p k c", p=128)
                GN_EPS = 64e-5
                def do_R1(n):
                    tl = slice(n * 128, (n + 1) * 128)
                    for pc in range(7):
                        Wp = Wst[pc % 2]
                        P.dma(Wp[:], wiv[:, :, pc * 256:(pc + 1) * 256], q="pool")
                        for cc in range(2):
                            c = 2 * pc + cc
                            for kc in range(8):
                                P.mm(psA[c // 4][:, (c % 4) * 128:(c % 4 + 1) * 128], Wp[:, kc, cc * 128:(cc + 1) * 128], hT[:, kc, tl],
                                     start=(kc == 0), stop=(kc == 7))
                    def r1_chain(c):
                        zc = zcb[c % 4]
                        dz_ = dzb[c % 4]
                        pz = psA[c // 4][:, (c % 4) * 128:(c % 4 + 1) * 128]
                        P.copy(zc[:, 0:1], carry[:, c:c + 1])
                        yield
                        P.act(zc[:, 1:129], pz, AF.Copy)
                        yield
                        P.copy(carry[:, c:c + 1], zc[:, 128:129])
                        yield
                        P.tt(dz_[:], zc[:, 0:128], zc[:, 1:129], ALU.subtract)
                        yield
                        P.stt(zt[:, c, :], dz_[:], muT[:, c:c + 1], zc[:, 1:129], ALU.mult, ALU.add)
                        yield
                    for base in range(0, 14, 4):
                        run_interleaved([r1_chain(c) for c in range(base, min(base + 4, 14))])

                do_R1(0)
                for n in range(RWKV_N):
                    tl = slice(n * 128, (n + 1) * 128)
                    P.act(th[0:64, :], zt[0:64, 12, :], AF.Tanh)
                    P.act(adb[64:128, :], zt[64:128, 12, :], AF.Copy)
                    P.act(sgd[:], zt[:, 13, :], AF.Sigmoid)
                    def r2_chain(c4):
                            T_ = tset[c4 % NSET]
                            s_, a_, L0, E1, E2, E3, E4 = T_["s_"], T_["a_"], T_["L0"], T_["E1"], T_["E2"], T_["E3"], T_["E4"]
                            kk, bv, tA, kp, rn, kk2, t2b = T_["kk"], T_["bv"], T_["tA"], T_["kp"], T_["rn"], T_["kk2"], T_["t2b"]
                            ld = s_
                            Lx = E3
                            kkn = kk
                            r_ = zt[:, c4, :]
                            k_ = zt[:, 4 + c4, :]
                            v_ = zt[:, 8 + c4, :]
                            cs = slice(c4 * 128, (c4 + 1) * 128)
                            pU = psA[0][:, cs]; pAa = psA[1][:, cs]; pG = psA[2][:, cs]; pN = psA[3][:, cs]
                            pBn = psA[4][:, cs]
                            P.mm(pU, w2b[0:64, cs], th[0:64, :])
                            yield
                            P.mm(pAa, a2b[64:128, cs], adb[64:128, :])
                            yield
                            P.mm(pG, g2b[:, cs], sgd[:])
                            yield
                            P.act(s_[:], pU, AF.Sigmoid, bias=cols["rwkv_w0"][:, c4:c4 + 1])
                            yield
                            P.act(a_[:], pAa, AF.Sigmoid, bias=cols["rwkv_a0"][:, c4:c4 + 1])
                            yield
                            P.act(gb[:, c4, :], pG, AF.Copy)
                            yield
                            if RW_SUB < 2:
                                return
                            P.ts(ld[:], s_[:], -0.6065306597126334, None, ALU.mult)
                            yield
                            P.op("dve", lambda E, o_=L0[:], d_=ld[:]: E.tensor_tensor_scan(o_, ones_b[:], d_, 0.0, ALU.mult, ALU.add),
                                 reads=[ones_b[:], ld[:]], writes=[L0[:]])
                            yield
                            Lc = L0
                            P.act(E1[:], Lc[:], AF.Exp)
                            yield
                            P.act(E2[:], Lc[:], AF.Exp, scale=-1.0)
                            yield
                            P.tt(Lx[:], Lc[:], ld[:], ALU.subtract)
                            yield
                            P.act(E3[:], Lx[:], AF.Exp)
                            yield
                            P.act(E4[:], Lc[:], AF.Exp, scale=-1.0, bias=Lc[:, 127:128])
                            yield
                            P.copy(gCt[:, c4:c4 + 1], E1[:, 127:128])
                            yield
                            if RW_SUB < 3:
                                return
                            P.ts(kk[:], k_, cols["rwkv_k_k"][:, c4:c4 + 1], None, ALU.mult)
                            yield
                            P.tt(kk2[:], kk[:], kk[:], ALU.mult)
                            yield
                            P.mm(pN, bones[:], kk2[:])
                            yield
                            P.ts(rn[:], pN, 1e-24, None, ALU.max)
                            yield
                            P.act(rn[:], rn[:], AF.Sqrt)
                            yield
                            P.recip(rn[:], rn[:])
                            yield
                            P.tt(kkn[:], kk[:], rn[:], ALU.mult)
                            yield
                            P.tt(bv[:], kkn[:], a_[:], ALU.mult)
                            yield
                            if RW_SUB < 4:
                                return
                            P.ts(tA[:], a_[:], cols["rwkv_k_a"][:, c4:c4 + 1], omk[:, c4:c4 + 1], ALU.mult, ALU.add)
                            yield
                            P.tt(kp[:], k_, tA[:], ALU.mult)
                            yield
                            P.tt(tA[:], r_, kp[:], ALU.mult)
                            yield
                            P.ts(t2b[:], tA[:], cols["rwkv_r_k"][:, c4:c4 + 1], None, ALU.mult)
                            yield
                            P.mm(pBn, bones[:], t2b[:])
                            yield
                            P.tt(bonus[:, c4, :], pBn, v_, ALU.mult)
                            yield
                            if RW_SUB < 5:
                                return
                            P.stt(AR[:, c4, 0:128], kkn[:], -1.0, E3[:], ALU.mult, ALU.mult)
                            yield
                            P.tt(AR[:, c4, 128:256], r_, E1[:], ALU.mult)
                            yield
                            P.tt(Bt[:, c4, :], bv[:], E2[:], ALU.mult)
                            yield
                            P.tt(Kt[:, c4, :], kp[:], E2[:], ALU.mult)
                            yield
                            P.tt(Bh[:, c4, :], bv[:], E4[:], ALU.mult)
                            yield
                            P.tt(Kh[:, c4, :], kp[:], E4[:], ALU.mult)
                            yield
                            P.copy(vb[:, c4, :], v_, eng="act")
                            yield
                    gens_ = [r2_chain(c4) for c4 in range(4)]
                    while gens_:
                        for g_ in list(gens_):
                            try:
                                next(g_)
                            except StopIteration:
                                gens_.remove(g_)
                    if RW_SUB < 6:
                        continue
                    for i3, (srcf, dstt) in enumerate(((vb, VT), (Kh, KhT), (Bh, BhT))):
                        tp = tpb[i3 % 2]
                        for c4 in range(4):
                            P.tr(tp[:, c4 * 128:(c4 + 1) * 128], srcf[:, c4, :], ident[:])
                        P.copy(dstt[:], tp[:, 0:512], eng=("act" if i3 % 2 else "dve"))
                    def head_info(w, hi):
                        h = 4 * w + hi
                        c4, e = divmod(h, 2)
                        return h, c4, e, slice(e * 64, (e + 1) * 64)

                    ubc = maskuu[:, :].unsqueeze(1).to_broadcast([128, 2, 256])
                    slbc = masksl[:, :].unsqueeze(1).to_broadcast([128, 4, 128])
                    idbc = ident[:, :].unsqueeze(1).to_broadcast([128, 4, 128])
                    for w in range(2):
                        Pb = [psA[3 * w + 0][:].rearrange("p (a c) -> p a c", a=2), psA[3 * w + 1][:].rearrange("p (a c) -> p a c", a=2)]
                        P2 = psA[3 * w + 2][:].rearrange("p (a c) -> p a c", a=4)
                        for hi in range(4):
                            h, c4, e, pse = head_info(w, hi)
                            P.mm(Pb[e][:, hi // 2, :], Bt[pse, c4, :], AR[pse, c4, :])
                        for e in range(2):
                            P.tt(MB[w][:, e::2, :], Pb[e], ubc, ALU.mult)
                        for hi in (0, 2):
                            h, c4, e, pse = head_info(w, hi)
                            P.mm(P2[:, hi, :], AR[pse, c4, 0:128], Bt[pse, c4, :])
                        for hi in (1, 3, 0, 2):
                            h, c4, e, pse = head_info(w, hi)
                            P.mm(Pb[e][:, hi // 2, :], Kt[pse, c4, :], AR[pse, c4, :])
                        for hi in (1, 3):
                            h, c4, e, pse = head_info(w, hi)
                            P.mm(P2[:, hi, :], AR[pse, c4, 0:128], Bt[pse, c4, :])
                        for e in range(2):
                            P.tt(MKt[w][:, e::2, :], Pb[e], ubc, ALU.mult)
                        P.tt(ATT[w][0][:], P2, slbc, ALU.mult)
                    for j in range(1, 7):
                        for w in range(2):
                            Pb = [psA[3 * w + 0][:].rearrange("p (a c) -> p a c", a=2), psA[3 * w + 1][:].rearrange("p (a c) -> p a c", a=2)]
                            P2 = psA[3 * w + 2][:].rearrange("p (a c) -> p a c", a=4)
                            src_ta = TA[w][(j - 1) % 2]
                            dst_ta = TA[w][j % 2]
                            At = ATT[w][(j - 1) % 2]
                            for hi in range(4):
                                e = hi % 2
                                if j == 1:
                                    P.mm(Pb[e][:, hi // 2, 128:256], At[:, hi, :], MB[w][:, hi, 0:128])
                                    P.mm(P2[:, hi, :], MB[w][:, hi, 0:128], At[:, hi, :])
                                else:
                                    P.mm(Pb[e][:, hi // 2, :], At[:, hi, :], src_ta[:, hi, :])
                                    P.mm(P2[:, hi, :], src_ta[:, hi, 128:256], At[:, hi, :])
                            if j == 1:
                                P.tt(dst_ta[:, :, 0:128], MB[w][:, :, 0:128], idbc, ALU.add)
                            else:
                                for e in range(2):
                                    P.tt(dst_ta[:, e::2, 0:128], Pb[e][:, :, 0:128], src_ta[:, e::2, 0:128], ALU.add)
                            if j < 6:
                                for e in range(2):
                                    P.act(dst_ta[:, e::2, 128:256], Pb[e][:, :, 128:256], AF.Copy)
                            P.act(ATT[w][j % 2][:], P2, AF.Copy)
                    for w in range(2):
                        P2 = psA[3 * w + 2][:].rearrange("p (a c) -> p a c", a=4)
                        for hi in range(4):
                            P.mm(P2[:, hi, :], ATT[w][0][:, hi, :], TA[w][0][:, hi, 0:128])
                        P.tt(Tfw[w][:], P2, TA[w][0][:, :, 0:128], ALU.add)
                    if n + 1 < RWKV_N:
                        do_R1(n + 1)
                    W1ps = psA[4][:].rearrange("p (h v) -> p h v", v=64)
                    Ups = psA[5][:].rearrange("p (h v) -> p h v", v=64)
                    Yps = psA[4][:].rearrange("p (h v) -> p h v", v=64)
                    Sps = psA[5][:].rearrange("p (c v) -> p c v", v=128)
                    hinfo = []
                    for h in range(8):
                        c4, e = divmod(h, 2)
                        hinfo.append((h, c4, slice(e * 64, (e + 1) * 64), slice(h * 64, (h + 1) * 64)))
                    for h, c4, pse, hsl in hinfo:
                        P.mm(W1ps[:, h, :], AR[pse, c4, 0:128], Sb[pse, c4, :], start=True, stop=False)
                        P.mm(W1ps[:, h, :], MKt[h // 4][:, h % 4, 0:128], VT[:, hsl], start=False, stop=True)
                    P.act(W1[:], psA[4][:], AF.Copy)
                    for h, c4, pse, hsl in hinfo:
                        P.mm(Ups[:, h, :], Tfw[h // 4][:, h % 4, :], W1[:, hsl])
                    P.copy(UT[:], psA[5][:])
                    for h, c4, pse, hsl in hinfo:
                        P.mm(Yps[:, h, :], AR[pse, c4, 128:256], Sb[pse, c4, :], start=True, stop=False)
                        P.mm(Yps[:, h, :], MB[h // 4][:, h % 4, 128:256], UT[:, hsl], start=False, stop=False)
                        P.mm(Yps[:, h, :], MKt[h // 4][:, h % 4, 128:256], VT[:, hsl], start=False, stop=True)
                    P.act(Ytok[:], psA[4][:], AF.Copy)
                    for c4 in range(4):
                        cs = slice(c4 * 128, (c4 + 1) * 128)
                        P.mm(Sps[:, c4, :], BhT[:, cs], UT[:, cs], start=True, stop=False)
                        P.mm(Sps[:, c4, :], KhT[:, cs], VT[:, cs], start=False, stop=True)
                    for e in range(2):
                        pse = slice(e * 64, (e + 1) * 64)
                        P.tt(Sf[pse, :, :], Sf[pse, :, :], gCt[pse, :].unsqueeze(2).to_broadcast([64, 4, 64]), ALU.mult)
                        P.tt(Sf[pse, :, :], Sf[pse, :, :], Sps[pse, :, e * 64:(e + 1) * 64], ALU.add)
                    P.act(Sb[:], Sf[:], AF.Copy)
                    y3 = Ytok[:].rearrange("p (h d) -> p h d", d=64)
                    P.red(sm[:], y3)
                    P.act(yc[:], Ytok[:], AF.Square)
                    P.red(sm2[:], yc[:].rearrange("p (h d) -> p h d", d=64))
                    P.ts(mean[:], sm[:], 1.0 / 64, None, ALU.mult)
                    P.tt(msq[:], mean[:], mean[:], ALU.mult)
                    P.stt(var[:], sm2[:], 1.0 / 64, msq[:], ALU.mult, ALU.subtract)
                    P.ts(var[:], var[:], GN_EPS, None, ALU.add)
                    P.act(var[:], var[:], AF.Sqrt)
                    P.recip(var[:], var[:])
                    P.tt(yc[:].rearrange("p (h d) -> p h d", d=64), y3, mean[:].unsqueeze(2).to_broadcast([128, 8, 64]), ALU.subtract)
                    P.tt(ynb[:].rearrange("p (h d) -> p h d", d=64), yc[:].rearrange("p (h d) -> p h d", d=64),
                         var[:].unsqueeze(2).to_broadcast([128, 8, 64]), ALU.mult)
                    tp = tpb[n % 2]
                    for c4 in range(4):
                        P.tr(tp[:, c4 * 128:(c4 + 1) * 128], ynb[:, c4 * 128:(c4 + 1) * 128], ident[:])
                    for c4 in range(4):
                        yo_ = yo[c4 % 2]
                        P.ts(yo_[:], tp[:, c4 * 128:(c4 + 1) * 128], cols["rwkv_gn_w"][:, c4:c4 + 1], cols["rwkv_gn_b"][:, c4:c4 + 1],
                             ALU.mult, ALU.add)
                        P.tt(yo_[:], yo_[:], bonus[:, c4, :], ALU.add)
                        P.tt(yrT[:, c4, tl], yo_[:], gb[:, c4, :], ALU.mult)
                if dbg is not None:
                    P.dma(dbg["yr"][:, :, :], yrT[:], q="pool")
                P.barrier()
            merge_branch(ctx, hT, yrT, 4, wn["w_br_rwkv"], 4096)

    def mix_phase():
        with _ES() as ctx:
            hT = P.sb([128, 8, S], BF16, "hT", ctx)
            with _ES() as c1:
                norm_to_hT(c1, hT, wn["mix_norm"], tpb)
                P.barrier()
            if MIX_ATTN:
                attn_branch(hT)
            if MIX_RWKV:
                rwkv_branch(hT)
            P.barrier()

    P.barrier()
    if not SKIPFFN:
        ffn_phase(wn["ffn1_norm"], wn["ffn1_w_gate"], wn["ffn1_w_up"], wn["ffn1_w_down"])
    mix_phase()
    if not SKIPFFN:
        ffn_phase(wn["ffn2_norm"], wn["ffn2_w_gate"], wn["ffn2_w_up"], wn["ffn2_w_down"])
    ple_phase()
    P.finish()
    stack.close()
    return nc


_CACHE = {}


import os as _os
DEBUG = bool(_os.environ.get("KDEBUG"))
MIX_ATTN = not _os.environ.get("KNOATTN")
MIX_RWKV = not _os.environ.get("KNORWKV")
SKIPFFN = bool(_os.environ.get("KSKIPFFN"))
ATT_M = int(_os.environ.get("KATT_M", "16"))
ATT_PR = int(_os.environ.get("KATT_PR", "2"))
RWKV_N = int(_os.environ.get("KRWKV_N", "16"))
SKIPMERGE = bool(_os.environ.get("KSKIPMERGE"))
ATT_G = [int(c) for c in _os.environ.get("KATT_G", "012")]
ATT_STAGE = int(_os.environ.get("KATT_STAGE", "9"))
ATT_SUB = int(_os.environ.get("KATT_SUB", "9"))
RW_STAGE = int(_os.environ.get("KRW_STAGE", "9"))
RW_SUB = int(_os.environ.get("KRW_SUB", "9"))


def _consts():
    p = np.arange(128)[:, None]
    c = np.arange(128)[None, :]
    NEG = -30000.0
    onespad = np.zeros((128, 2, 128), np.float32)
    onespad[:, 0, 0:64] = 1.0
    onespad[:, 1, 64:128] = 1.0
    invf = (1.0 / (np.float32(10000.0) ** (np.arange(0, 64, 2, dtype=np.float32) / np.float32(64)))).astype(np.float32)
    bo = np.zeros((128, 128), np.float32)
    bo[0:64, 0:64] = 1.0
    bo[64:128, 64:128] = 1.0
    su = (p < c).astype(np.float32)
    iu = (p <= c).astype(np.float32)
    return {"c_ident": np.eye(128, dtype=np.float32),
            "c_maskc": np.where(p <= c, 0.0, NEG).astype(np.float32),
            "c_maskp": np.where(p >= c, 0.0, NEG).astype(np.float32),
            "c_onespad": onespad,
            "c_invfreq": np.ascontiguousarray(np.broadcast_to(invf[None, :], (128, 32))),
            "c_maskuu": np.concatenate([su, iu], axis=1),
            "c_masksl": (c < p).astype(np.float32),
            "c_blockones": bo}


def kernel(**inputs):
    if "nc" not in _CACHE:
        _CACHE["nc"] = build_program(dbg=({} if DEBUG else None))
    nc = _CACHE["nc"]
    consts = _consts()
    in_maps = []
    for b in range(8):
        m = dict(consts)
        m["x"] = np.ascontiguousarray(inputs["x"][b])
        m["p"] = np.ascontiguousarray(inputs["p"][0, b])
        m["positions"] = np.ascontiguousarray(inputs["positions"][b])
        for k, v in inputs.items():
            if k in ("x", "p", "positions"):
                continue
            a = np.asarray(v)[0]
            m[k] = np.ascontiguousarray(a.reshape(-1) if k == "rwkv_r_k" else a)
        in_maps.append(m)
    res = run_bass_kernel_spmd(nc, in_maps, core_ids=list(range(8)))
    if DEBUG:
        _CACHE["dbg"] = [{k: np.asarray(v) for k, v in r.items() if k.startswith("dbg_")} for r in res.results]
    return np.stack([np.asarray(r["out"]) for r in res.results], axis=0).astype(np.float32)
```
